# Optimizing a Trainium2 kernel written in Bass

```python
import math
import jax, jax.numpy as jnp
from jax import lax
import numpy as np

D_MODEL = 2048
BATCH = 2
SEQ = 16384
DEPTH = 1

MLA_HEADS = 16
MLA_Q_RANK = 512
MLA_KV_RANK = 256
MLA_NOPE_DIM = 128
MLA_ROPE_DIM = 64
MLA_V_DIM = 128
ROPE_THETA = 10000.0
SWA_Q_HEADS = 32
SWA_KV_HEADS = 4
SWA_HEAD_DIM = 64
SWA_GROUP = SWA_Q_HEADS // SWA_KV_HEADS
WINDOW = 128
BLOCK = 128
REL_BUCKETS = 32
REL_MAX_DIST = 128
D_FF = 5632
CONV_WIDTH = 3
N_BRANCHES = 2
EPS = 1e-6

MLA_WIDTH = MLA_HEADS * MLA_V_DIM
SWA_WIDTH = SWA_Q_HEADS * SWA_HEAD_DIM
SWA_KV_WIDTH = SWA_KV_HEADS * SWA_HEAD_DIM
IN_SPLITS = (MLA_Q_RANK, MLA_KV_RANK, MLA_ROPE_DIM, SWA_WIDTH, SWA_KV_WIDTH, SWA_KV_WIDTH, N_BRANCHES * D_MODEL)
IN_COLS = MLA_Q_RANK + MLA_KV_RANK + MLA_ROPE_DIM + SWA_WIDTH + 2 * SWA_KV_WIDTH + N_BRANCHES * D_MODEL

kernel_name = "hybrid_mla_swa_gated_convffn"


def rms_norm(x, g):
    xf = x.astype(jnp.float32)
    y = xf * lax.rsqrt(jnp.mean(xf * xf, axis=-1, keepdims=True) + EPS)
    return (y * g.astype(jnp.float32)).astype(x.dtype)


def rope_angles(pos, dim):
    inv = ROPE_THETA ** (-jnp.arange(0, dim, 2, dtype=jnp.float32) / dim)
    ang = pos.astype(jnp.float32)[:, None] * inv[None, :]
    return jnp.cos(ang), jnp.sin(ang)


def apply_rope(x, cos, sin):
    half = x.shape[-1] // 2
    x1, x2 = x[..., :half], x[..., half:]
    cos = cos.astype(x.dtype)
    sin = sin.astype(x.dtype)
    return jnp.concatenate([x1 * cos - x2 * sin, x2 * cos + x1 * sin], axis=-1)


def t5_bucket(dist):
    n = jnp.maximum(dist, 0)
    max_exact = REL_BUCKETS // 2
    large = max_exact + (jnp.log(jnp.maximum(n, 1).astype(jnp.float32) / max_exact)
                         / math.log(REL_MAX_DIST / max_exact)
                         * (REL_BUCKETS - max_exact)).astype(jnp.int32)
    large = jnp.minimum(large, REL_BUCKETS - 1)
    return jnp.where(n < max_exact, n, large)


def mla_attention(q_nope, q_rope, k_nope, k_rope, v):
    B, S, H, _ = q_nope.shape
    nb = S // BLOCK
    scale = (MLA_NOPE_DIM + MLA_ROPE_DIM) ** -0.5
    qn = q_nope.reshape(B, nb, BLOCK, H, MLA_NOPE_DIM).transpose(1, 0, 2, 3, 4)
    qr = q_rope.reshape(B, nb, BLOCK, H, MLA_ROPE_DIM).transpose(1, 0, 2, 3, 4)
    k_pos = jnp.arange(S)

    def one_block(args):
        qn_b, qr_b, i = args
        s = (jnp.einsum('bqhd,bkhd->bhqk', qn_b, k_nope)
             + jnp.einsum('bqhr,bkr->bhqk', qr_b, k_rope)).astype(jnp.float32) * scale
        q_pos = i * BLOCK + jnp.arange(BLOCK)
        causal = k_pos[None, :] <= q_pos[:, None]
        s = jnp.where(causal, s, -jnp.inf)
        p = jax.nn.softmax(s, axis=-1).astype(v.dtype)
        return jnp.einsum('bhqk,bkhd->bqhd', p, v)

    out = lax.map(one_block, (qn, qr, jnp.arange(nb)))
    return out.transpose(1, 0, 2, 3, 4).reshape(B, S, H * MLA_V_DIM)


def swa_attention(q, k, v, sinks, rel_table):
    B, S, _ = q.shape
    nb = S // BLOCK
    q = q.reshape(B, nb, BLOCK, SWA_KV_HEADS, SWA_GROUP, SWA_HEAD_DIM)
    k = k.reshape(B, nb, BLOCK, SWA_KV_HEADS, SWA_HEAD_DIM)
    v = v.reshape(B, nb, BLOCK, SWA_KV_HEADS, SWA_HEAD_DIM)

    def band(t):
        prev = jnp.pad(t[:, :-1], ((0, 0), (1, 0), (0, 0), (0, 0), (0, 0)))
        return jnp.concatenate([prev, t], axis=2)

    kb, vb = band(k), band(v)
    s = jnp.einsum('bnqhgd,bnkhd->bhgnqk', q, kb).astype(jnp.float32) * SWA_HEAD_DIM ** -0.5
    qi = jnp.arange(BLOCK)[:, None]
    kj = jnp.arange(2 * BLOCK)[None, :]
    dist = qi + BLOCK - kj
    in_window = (dist >= 0) & (dist < WINDOW)
    has_prev = (jnp.arange(nb)[:, None, None] > 0) | (kj >= BLOCK)[None]
    mask = in_window[None] & has_prev
    bias = rel_table.astype(jnp.float32)[t5_bucket(dist)]
    bias = bias.transpose(2, 0, 1).reshape(SWA_KV_HEADS, SWA_GROUP, 1, BLOCK, 2 * BLOCK)
    s = jnp.where(mask[None, None, None], s + bias[None], -jnp.inf)
    sink = sinks.astype(jnp.float32).reshape(SWA_KV_HEADS, SWA_GROUP)[None, :, :, None, None]
    m = jnp.maximum(jnp.max(s, axis=-1), sink)
    p = jnp.exp(s - m[..., None])
    denom = jnp.sum(p, axis=-1) + jnp.exp(sink - m)
    p = (p / denom[..., None]).astype(vb.dtype)
    out = jnp.einsum('bhgnqk,bnkhd->bnqhgd', p, vb)
    return out.reshape(B, S, SWA_WIDTH)


def hybrid_mixer(h, w_in, mla_q_norm, mla_w_q_up, mla_kv_norm, mla_w_kv_up,
                 swa_sinks, rel_table, w_o_mla, w_o_swa, w_out):
    B, S, _ = h.shape
    offsets = [int(o) for o in np.cumsum(IN_SPLITS)[:-1]]
    c_q, c_kv, k_r, q_s, k_s, v_s, gates = jnp.split(h @ w_in, offsets, axis=-1)
    pos = jnp.arange(S)
    q = (rms_norm(c_q, mla_q_norm) @ mla_w_q_up).reshape(B, S, MLA_HEADS, MLA_NOPE_DIM + MLA_ROPE_DIM)
    kv = (rms_norm(c_kv, mla_kv_norm) @ mla_w_kv_up).reshape(B, S, MLA_HEADS, MLA_NOPE_DIM + MLA_V_DIM)
    cos, sin = rope_angles(pos, MLA_ROPE_DIM)
    q_rope = apply_rope(q[..., MLA_NOPE_DIM:], cos[:, None, :], sin[:, None, :])
    k_rope = apply_rope(k_r, cos, sin)
    o_a = mla_attention(q[..., :MLA_NOPE_DIM], q_rope, kv[..., :MLA_NOPE_DIM], k_rope, kv[..., MLA_NOPE_DIM:])
    o_b = swa_attention(q_s, k_s, v_s, swa_sinks, rel_table)
    g = jax.nn.sigmoid(gates.astype(jnp.float32)).reshape(B, S, N_BRANCHES, D_MODEL)
    merged = g[:, :, 0] * (o_a @ w_o_mla).astype(jnp.float32) + g[:, :, 1] * (o_b @ w_o_swa).astype(jnp.float32)
    return merged.astype(h.dtype) @ w_out


def conv_ffn(h, w_up, conv_w, conv_b, w_down):
    S = h.shape[1]
    a, b = jnp.split(h @ w_up, 2, axis=-1)
    ap = jnp.pad(a, ((0, 0), (CONV_WIDTH - 1, 0), (0, 0)))
    c = conv_b
    for j in range(CONV_WIDTH):
        c = c + conv_w[j] * ap[:, j:j + S]
    return (jax.nn.gelu(c, approximate=True) * b) @ w_down


def setup_inputs(seed: int = 0) -> dict:
    key = jax.random.key(seed)
    ks = jax.random.split(key, 24)
    L = DEPTH
    f32 = jnp.float32

    def w(k, shape, fan_in):
        return jax.random.normal(k, shape, f32) * fan_in ** -0.5

    def gain(k, shape):
        return 1.0 + 0.05 * jax.random.normal(k, shape, f32)

    return {
        "x": jax.random.normal(ks[0], (BATCH, SEQ, D_MODEL), f32),
        "norm_mix_pre": gain(ks[1], (L, D_MODEL)),
        "norm_mix_post": gain(ks[2], (L, D_MODEL)),
        "norm_ffn_pre": gain(ks[3], (L, D_MODEL)),
        "norm_ffn_post": gain(ks[4], (L, D_MODEL)),
        "w_in": w(ks[5], (L, D_MODEL, IN_COLS), D_MODEL),
        "mla_q_norm": gain(ks[6], (L, MLA_Q_RANK)),
        "mla_w_q_up": w(ks[7], (L, MLA_Q_RANK, MLA_HEADS * (MLA_NOPE_DIM + MLA_ROPE_DIM)), MLA_Q_RANK),
        "mla_kv_norm": gain(ks[8], (L, MLA_KV_RANK)),
        "mla_w_kv_up": w(ks[9], (L, MLA_KV_RANK, MLA_HEADS * (MLA_NOPE_DIM + MLA_V_DIM)), MLA_KV_RANK),
        "swa_sinks": jax.random.normal(ks[10], (L, SWA_Q_HEADS), f32),
        "rel_bias_table": 0.5 * jax.random.normal(ks[11], (REL_BUCKETS, SWA_Q_HEADS), f32),
        "w_o_mla": w(ks[12], (L, MLA_WIDTH, D_MODEL), MLA_WIDTH),
        "w_o_swa": w(ks[13], (L, SWA_WIDTH, D_MODEL), SWA_WIDTH),
        "w_out": w(ks[14], (L, D_MODEL, D_MODEL), D_MODEL),
        "ffn_w_up": w(ks[15], (L, D_MODEL, 2 * D_FF), D_MODEL),
        "ffn_conv_w": w(ks[16], (L, CONV_WIDTH, D_FF), CONV_WIDTH),
        "ffn_conv_b": 0.01 * jax.random.normal(ks[17], (L, D_FF), f32),
        "ffn_w_down": w(ks[18], (L, D_FF, D_MODEL), D_FF),
    }


def reference(x, norm_mix_pre, norm_mix_post, norm_ffn_pre, norm_ffn_post, w_in,
              mla_q_norm, mla_w_q_up, mla_kv_norm, mla_w_kv_up, swa_sinks, rel_bias_table,
              w_o_mla, w_o_swa, w_out, ffn_w_up, ffn_conv_w, ffn_conv_b, ffn_w_down):
    for l in range(DEPTH):
        h = rms_norm(x, norm_mix_pre[l])
        y = hybrid_mixer(h, w_in[l], mla_q_norm[l], mla_w_q_up[l], mla_kv_norm[l], mla_w_kv_up[l],
                         swa_sinks[l], rel_bias_table, w_o_mla[l], w_o_swa[l], w_out[l])
        x = x + rms_norm(y, norm_mix_post[l])
        h = rms_norm(x, norm_ffn_pre[l])
        y = conv_ffn(h, ffn_w_up[l], ffn_conv_w[l], ffn_conv_b[l], ffn_w_down[l])
        x = x + rms_norm(y, norm_ffn_post[l])
    return x
```

```python
import math
from contextlib import ExitStack

import numpy as np
import concourse.bass as bass
import concourse.mybir as mybir
from concourse.bass_utils import run_bass_kernel_spmd

F32 = mybir.dt.float32
BF16 = mybir.dt.bfloat16
AF = mybir.ActivationFunctionType
ALU = mybir.AluOpType
AX = mybir.AxisListType

D = 2048
JD = 16
NH = 16
NSH = 32
DFF = 5632
JF = 44
EPS = 1e-6
NEG = -30000.0
MLA_SCALE = 192.0 ** -0.5
SWA_SCALE = 64.0 ** -0.5

EPOCH = 16000
NDSEM = 16


class Tracer:
    ENG = ('pe', 'act', 'dve', 'pool', 'sp')

    def __init__(self, nc, es):
        self.nc = nc
        self.es = es
        self.eng = {'pe': nc.tensor, 'act': nc.scalar, 'dve': nc.vector, 'pool': nc.gpsimd, 'sp': nc.sync}
        self.count = {e: 0 for e in self.ENG}
        self.known = {e: {f: -1 for f in self.ENG} for e in self.ENG}
        self.known_dma = {e: set() for e in self.ENG}
        self.last_w = {}
        self.readers = {}
        self.ndma = {'sp': 0, 'pool': 0, 'act': 0}
        self.sems = {}
        self.nwaits = 0

    def sem(self, key):
        s = self.sems.get(key)
        if s is None:
            s = self.es.enter_context(self.nc.semaphore("s_" + "_".join(str(k) for k in key)))
            self.sems[key] = s
        return s

    def _need(self, c, sig):
        if sig[0] == 'c':
            e, n = sig[1], sig[2]
            if self.known[c][e] >= n:
                return
            self.known[c][e] = n
            self.eng[c].wait_ge(self.sem(('S', e, n // EPOCH)), n % EPOCH + 1)
        else:
            q, d = sig[1], sig[2]
            if (q, d) in self.known_dma[c]:
                return
            self.known_dma[c].add((q, d))
            self.eng[c].wait_ge(self.sem(('D' + q, d % NDSEM)), 16 * (d // NDSEM + 1))
        self.nwaits += 1

    def _dep(self, c, sig, kind):
        if sig[0] == 'c' and sig[1] == c:
            if c == 'pe' or (kind != 'raw' and c != 'pool'):
                return
        self._need(c, sig)

    def op(self, eng, fn, reads=(), writes=(), dma=False):
        for r in reads:
            w = self.last_w.get(r)
            if w is not None:
                self._dep(eng, w, 'raw')
            if r[0] == 'P' and r[1:].isdigit():
                for rd in self.readers.get(r, ()):
                    self._dep(eng, rd, 'war')
        for r in writes:
            w = self.last_w.get(r)
            if w is not None:
                self._dep(eng, w, 'waw')
            for rd in self.readers.get(r, ()):
                self._dep(eng, rd, 'war')
        if dma:
            d = self.ndma[eng]
            self.ndma[eng] += 1
            if d >= NDSEM:
                self._need(eng, ('d', eng, d - NDSEM))
            sig = ('d', eng, d)
            ins = fn(self.eng[eng])
            ins.then_inc(self.sem(('D' + eng, d % NDSEM)), 16)
        else:
            n = self.count[eng]
            self.count[eng] += 1
            sig = ('c', eng, n)
            ins = fn(self.eng[eng])
            ins.then_inc(self.sem(('S', eng, n // EPOCH)), 1)
        for r in reads:
            lst = self.readers.setdefault(r, [])
            if sig[0] == 'c':
                lst[:] = [x for x in lst if not (x[0] == 'c' and x[1] == sig[1])]
            lst.append(sig)
        for r in writes:
            self.last_w[r] = sig
            self.readers[r] = []
        return sig

    def barrier(self):
        for c in self.ENG:
            for e in self.ENG:
                if e == c or e == 'sp':
                    continue
                n = self.count[e] - 1
                if n >= 0:
                    self._need(c, ('c', e, n))
            for q in self.ndma:
                for k in range(max(0, self.ndma[q] - NDSEM), self.ndma[q]):
                    self._need(c, ('d', q, k))
        self.last_w.clear()
        self.readers.clear()

    def finish(self):
        for q in self.ndma:
            for k in range(max(0, self.ndma[q] - NDSEM), self.ndma[q]):
                self._need('sp', ('d', q, k))


def _regular(src0, nch, dst0):
    return [(src0 + 128 * i, 128, 1.0, dst0 + i, 0) for i in range(nch)]


def win_pieces():
    p = []
    p += _regular(0, 4, 0)
    p += _regular(512, 2, 4)
    p += [(768, 64, 1.0, 6, 0), (800, 32, -1.0, 6, 64), (768, 32, 1.0, 6, 96)]
    p += _regular(832, 16, 7)
    p += _regular(2880, 2, 23)
    p += _regular(3136, 2, 25)
    p += _regular(3392, 32, 27)
    return p, 59


def wq_pieces():
    p = []
    for h in range(NH):
        b = h * 192
        off = (h % 2) * 64
        p.append((b, 128, 1.0, h, 0))
        p.append((b + 128, 64, 1.0, 16 + h // 2, off))
        p.append((b + 160, 32, -1.0, 24 + h // 2, off))
        p.append((b + 128, 32, 1.0, 24 + h // 2, off + 32))
    return p, 32


def wkv_pieces():
    p = []
    for h in range(NH):
        p.append((h * 256, 128, 1.0, h, 0))
        p.append((h * 256 + 128, 128, 1.0, 16 + h, 0))
    return p, 32


class Builder:
    def __init__(self, S, KC=16, debug=False, stop_after=None):
        self.debug = debug
        self.stop_after = stop_after
        self.S = S
        self.NBLK = S // 128
        self.CB = self.NBLK // 4
        self.CH = S // 4
        self.NT = self.CB // 4
        self.NKT = self.NBLK // 4
        self.OWN0 = 3 * self.CB
        self.KC = min(KC, self.NBLK)
        self.nc = bass.Bass("TRN2", target_bir_lowering=False)
        self.es = ExitStack()

    def dram_in(self, name, shape):
        return self.nc.dram_tensor(name, list(shape), F32, kind="ExternalInput").ap()

    def dram_scr(self, name, shape, dt):
        return self.nc.dram_tensor(name, list(shape), dt, kind="Internal").ap()

    def sb(self, name, shape, dt, es=None):
        return (es or self.es).enter_context(self.nc.sbuf_tensor(name, list(shape), dt))

    def build(self):
        nc, S = self.nc, self.S
        with self.es:
            self.tr = Tracer(nc, self.es)
            I = {}
            I['xs'] = self.dram_in("xs", [S, D])
            for nm, shp in [("norm_mix_pre", [D]), ("norm_mix_post", [D]), ("norm_ffn_pre", [D]), ("norm_ffn_post", [D]),
                            ("w_in", [D, 7488]), ("mla_q_norm", [512]), ("mla_w_q_up", [512, 3072]),
                            ("mla_kv_norm", [256]), ("mla_w_kv_up", [256, 4096]), ("swa_sinks", [NSH]),
                            ("rel_bias_table", [32, NSH]), ("w_o_mla", [D, D]), ("w_o_swa", [D, D]), ("w_out", [D, D]),
                            ("ffn_w_up", [D, 2 * DFF]), ("ffn_conv_w", [3, DFF]), ("ffn_conv_b", [DFF]),
                            ("ffn_w_down", [DFF, D]),
                            ("cos128", [128, S]), ("sin128", [128, S]), ("kmask", [128, self.NBLK]),
                            ("onehot", [33, 2 * 128 * 128])]:
                I[nm] = self.dram_in(nm, shp)
            self.I = I
            self.out = nc.dram_tensor("out", [self.CH, D], F32, kind="ExternalOutput").ap()
            Wd = {}
            Wd['win'] = self.dram_scr("wb_in", [128, 59, JD, 128], BF16)
            Wd['wq'] = self.dram_scr("wb_q", [128, 32, 4, 128], BF16)
            Wd['wkv'] = self.dram_scr("wb_kv", [128, 32, 2, 128], BF16)
            Wd['womla'] = self.dram_scr("wb_omla", [128, 16, JD, 128], BF16)
            Wd['woswa'] = self.dram_scr("wb_oswa", [128, 16, JD, 128], BF16)
            Wd['wout'] = self.dram_scr("wb_out", [128, 16, JD, 128], BF16)
            Wd['wup'] = self.dram_scr("wb_up", [128, 88, JD, 128], BF16)
            Wd['wdown'] = self.dram_scr("wb_down", [128, 16, JF, 128], BF16)
            self.Wd = Wd
            self.KN = self.dram_scr("scr_kn", [NH, 128, S], BF16)
            self.KR = self.dram_scr("scr_kr", [64, S], BF16)
            self.VN = self.dram_scr("scr_vn", [NH, 128, self.NBLK, 129], BF16)
            self.KS = self.dram_scr("scr_ks", [256, S], BF16)
            self.VS = self.dram_scr("scr_vs", [128, self.NBLK, 4, 65], BF16)
            self.X1S = self.dram_scr("scr_x1", [self.CH, D], F32)
            self.BF = self.dram_scr("scr_bias", [NSH, 2 * 128 * 128], F32)
            self.ED = self.dram_scr("scr_E", [128, NSH, 256], BF16)
            self.RS = self.dram_scr("scr_rs", [1, 512], F32)
            self.PS = self.es.enter_context(nc.psum_tensor("psum_all", [128, 8 * 512], F32))
            self.PSB = self.PS[:].bitcast(BF16)
            self.consts()
            if self.stop_after != 'consts':
                self.phase_k()
                self.tr.barrier()
                if self.stop_after != 'k':
                    self.phase_q()
            self.tr.finish()
        return nc

    def bank(self, i, w=512):
        return self.PS[:, i * 512: i * 512 + w]

    def bankb(self, i, w=1024):
        return self.PSB[:, i * 1024: i * 1024 + w]

    def consts(self):
        nc, tr, I = self.nc, self.tr, self.I
        sb = self.sb
        self.identf = sb("identf", [128, 128], F32)
        self.ident = sb("ident", [128, 128], BF16)
        self.onesf = sb("onesf", [128, 128], F32)
        self.onesb = sb("onesb", [128, 128], BF16)
        self.epsc = sb("epsc", [128, 1], F32)
        self.onec = sb("onec", [128, 1], F32)
        self.negc = sb("negc", [128, 1], F32)
        self.kmask = sb("kmask_sb", [128, self.NBLK], F32)
        self.expsink = sb("expsink", [128, NSH], F32)
        self.cw = sb("convw", [128, 3, JF], F32)
        self.cb = sb("convb", [128, JF], F32)
        self.ahalo = sb("ahalo", [128, JF, 2], F32)
        self.gains = {}
        for nm, J in [("norm_mix_pre", JD), ("norm_mix_post", JD), ("norm_ffn_pre", JD), ("norm_ffn_post", JD),
                      ("mla_q_norm", 4), ("mla_kv_norm", 2)]:
            g = sb("g_" + nm, [128, J], F32)
            self.gains[nm] = g
            tr.op('sp', lambda e, g=g, nm=nm: e.dma_start(out=g[:], in_=I[nm].rearrange("(j p) -> p j", p=128), allow_slow_non_contiguous=True),
                  writes=['g_' + nm], dma=True)
        self.ngains = {}
        for nm, J in [("norm_mix_pre", JD), ("mla_q_norm", 4)]:
            g = sb("ng_" + nm, [128, J], F32)
            self.ngains[nm] = g
            tr.op('dve', lambda e, g=g, nm=nm: e.tensor_scalar(out=g[:], in0=self.gains[nm][:], scalar1=-1.0, scalar2=None,
                                                              op0=ALU.mult), reads=['g_' + nm], writes=['ng_' + nm])
        tr.op('pool', lambda e: e.memset(self.identf[:], 1.0), writes=['identf'])
        tr.op('pool', lambda e: e.affine_select(out=self.identf[:], in_=self.identf[:], pattern=[[1, 128]],
                                                compare_op=ALU.is_equal, fill=0.0, base=0, channel_multiplier=-1),
              reads=['identf'], writes=['identf'])
        tr.op('dve', lambda e: e.tensor_copy(out=self.ident[:], in_=self.identf[:]), reads=['identf'], writes=['ident'])
        tr.op('dve', lambda e: e.memset(self.onesf[:], 1.0), writes=['onesf'])
        tr.op('dve', lambda e: e.memset(self.onesb[:], 1.0), writes=['onesb'])
        tr.op('dve', lambda e: e.memset(self.epsc[:], EPS), writes=['epsc'])
        tr.op('dve', lambda e: e.memset(self.onec[:], 1.0), writes=['onec'])
        tr.op('dve', lambda e: e.memset(self.negc[:], -1.0), writes=['negc'])
        tr.op('dve', lambda e: e.memset(self.ahalo[:], 0.0), writes=['ahalo'])
        tr.op('sp', lambda e: e.dma_start(out=self.kmask[:], in_=I['kmask']), writes=['kmask'], dma=True)
        tr.op('sp', lambda e: e.dma_start(out=self.cw[:], in_=I['ffn_conv_w'].rearrange("w (j p) -> p w j", p=128), allow_slow_non_contiguous=True),
              writes=['cw'], dma=True)
        tr.op('sp', lambda e: e.dma_start(out=self.cb[:], in_=I['ffn_conv_b'].rearrange("(j p) -> p j", p=128), allow_slow_non_contiguous=True),
              writes=['cb'], dma=True)
        tr.op('sp', lambda e: e.dma_start(out=self.expsink[:], in_=I['swa_sinks'].partition_broadcast(128)),
              writes=['expsink'], dma=True)
        tr.op('act', lambda e: e.activation(out=self.expsink[:], in_=self.expsink[:], func=AF.Exp),
              reads=['expsink'], writes=['expsink'])
        with ExitStack() as les:
            tab = self.sb("tab_ext", [33, NSH], F32, les)
            oh = self.sb("oh_sb", [33, 4096], F32, les)
            bst = self.sb("bias_st", [NSH, 4096], F32, les)
            ebig = self.sb("ebig", [128, NSH, 256], F32, les)
            esb = self.sb("esb", [128, NSH, 256], BF16, les)
            tr.op('dve', lambda e: e.memset(tab[:], NEG), writes=['tab'])
            tr.op('sp', lambda e: e.dma_start(out=tab[0:32, :], in_=I['rel_bias_table']), writes=['tab'], dma=True)
            for cch in range(8):
                tr.op('sp', lambda e, cch=cch: e.dma_start(out=oh[:], in_=I['onehot'][:, cch * 4096:(cch + 1) * 4096]),
                      writes=['oh'], dma=True)
                for q in range(8):
                    bk = q % 4
                    tr.op('pe', lambda e, q=q, bk=bk: e.matmul(self.bank(bk)[0:32, :], lhsT=tab[:], rhs=oh[:, q * 512:(q + 1) * 512],
                                                               start=True, stop=True),
                          reads=['tab', 'oh'], writes=['P%d' % bk])
                    tr.op('act', lambda e, q=q, bk=bk: e.activation(out=bst[:, q * 512:(q + 1) * 512], in_=self.bank(bk)[0:32, :],
                                                                    func=AF.Copy),
                          reads=['P%d' % bk], writes=['bst'])
                tr.op('sp', lambda e, cch=cch: e.dma_start(out=self.BF[:, cch * 4096:(cch + 1) * 4096], in_=bst[:]),
                      reads=['bst'], writes=['BF'], dma=True)
            self.tr.barrier()
            tr.op('sp', lambda e: e.dma_start(out=ebig[:], in_=self.BF.rearrange("h (k x) -> k h x", k=128)),
                  writes=['ebig'], dma=True)
            tr.op('act', lambda e: e.activation(out=esb[:], in_=ebig[:], func=AF.Exp), reads=['ebig'], writes=['esb'])
            tr.op('sp', lambda e: e.dma_start(out=self.ED, in_=esb[:]), reads=['esb'], writes=['ED'], dma=True)
            self.tr.barrier()

    def phase_w_items(self, les):
        tr, I = self.tr, self.I
        stage = [self.sb("wst%d" % i, [128, JD, 256], F32, les) for i in range(2)]
        ost = [self.sb("wos%d" % i, [128, 2, JD, 128], BF16, les) for i in range(2)]
        self._wcnt = 0
        self._ccnt = 0

        def cast(out_ap, in_ap, gain_ap, rkeys, wkeys):
            k = self._ccnt % 3
            self._ccnt += 1
            if k == 0:
                tr.op('act', lambda e: e.activation(out=out_ap, in_=in_ap, func=AF.Copy, scale=gain_ap),
                      reads=rkeys, writes=wkeys)
            elif k == 1:
                tr.op('dve', lambda e: e.tensor_scalar(out=out_ap, in0=in_ap, scalar1=gain_ap, scalar2=None, op0=ALU.mult),
                      reads=rkeys, writes=wkeys)
            else:
                tr.op('pool', lambda e: e.tensor_scalar(out=out_ap, in0=in_ap, scalar1=gain_ap, scalar2=0.0,
                                                        op0=ALU.mult, op1=ALU.add),
                      reads=rkeys, writes=wkeys)

        def gcol(gname, j, sign):
            if gname is None:
                return (self.onec if sign > 0 else self.negc)[:, 0:1]
            return (self.gains[gname] if sign > 0 else self.ngains[gname])[:, j:j + 1]

        def piece_item(src, dst, J, gname, sc, ncol, sign, dn, doff, j0=0):
            def emit():
                b = self._wcnt % 2
                self._wcnt += 1
                st, os_ = stage[b], ost[b]
                tr.op('sp', lambda e: e.dma_start(
                    out=st[:, 0:J, 0:ncol], in_=src[j0 * 128:(j0 + J) * 128, sc:sc + ncol].rearrange("(j p) c -> p j c", p=128)),
                    writes=['wst%d' % b], dma=True)
                for j in range(J):
                    cast(os_[:, 0, j, 0:ncol], st[:, j, 0:ncol], gcol(gname, j0 + j, sign), ['wst%d' % b, 'gains'], ['wos%d' % b])
                tr.op('pool', lambda e: e.dma_start(out=dst[:, dn, j0:j0 + J, doff:doff + ncol], in_=os_[:, 0, 0:J, 0:ncol]),
                      reads=['wos%d' % b], writes=['wscr'], dma=True)
            return emit

        def pair_item(src, dst, J, gname, sc, dn, j0):
            def emit():
                b = self._wcnt % 2
                self._wcnt += 1
                st, os_ = stage[b], ost[b]
                tr.op('sp', lambda e: e.dma_start(
                    out=st[:, 0:J, :], in_=src[j0 * 128:(j0 + J) * 128, sc:sc + 256].rearrange("(j p) c -> p j c", p=128)),
                    writes=['wst%d' % b], dma=True)
                for j in range(J):
                    cast(os_[:, :, j, :], st[:, j, :].rearrange("p (g c) -> p g c", g=2), gcol(gname, j0 + j, 1.0),
                         ['wst%d' % b, 'gains'], ['wos%d' % b])
                tr.op('pool', lambda e: e.dma_start(out=dst[:, dn:dn + 2, j0:j0 + J, :], in_=os_[:, :, 0:J, :]),
                      reads=['wos%d' % b], writes=['wscr'], dma=True)
            return emit

        def regular(src, dst, J, gname, src0, nch, dst0):
            its = []
            assert nch % 2 == 0
            for c in range(0, nch, 2):
                for j0 in range(0, J, JD):
                    its.append(pair_item(src, dst, min(JD, J - j0), gname, src0 + 128 * c, dst0 + c, j0))
            return its

        def pieces(src, dst, J, gname, plist):
            return [piece_item(src, dst, J, gname, sc, ncol, sign, dn, doff) for (sc, ncol, sign, dn, doff) in plist]

        tr.op('dve', lambda e: e.memset(self.epsc[:], EPS),
              reads=['g_norm_mix_pre', 'g_norm_ffn_pre', 'g_mla_q_norm', 'g_mla_kv_norm', 'ng_norm_mix_pre', 'ng_mla_q_norm',
                     'onec', 'negc'], writes=['gains', 'epsc'])
        Wd = self.Wd
        pl, _ = win_pieces()
        spec = [p for p in pl if p[1] != 128 or p[4] != 0]
        pre = []
        pre += pieces(I['w_in'], Wd['win'], JD, "norm_mix_pre", spec)
        pre += regular(I['w_in'], Wd['win'], JD, "norm_mix_pre", 512, 2, 4)
        pre += regular(I['w_in'], Wd['win'], JD, "norm_mix_pre", 2880, 2, 23)
        pre += regular(I['w_in'], Wd['win'], JD, "norm_mix_pre", 3136, 2, 25)
        pl, _ = wkv_pieces()
        pre += pieces(I['mla_w_kv_up'], Wd['wkv'], 2, "mla_kv_norm", pl)
        rest = []
        rest += regular(I['w_in'], Wd['win'], JD, "norm_mix_pre", 0, 4, 0)
        rest += regular(I['w_in'], Wd['win'], JD, "norm_mix_pre", 832, 16, 7)
        rest += regular(I['w_in'], Wd['win'], JD, "norm_mix_pre", 3392, 32, 27)
        pl, _ = wq_pieces()
        rest += pieces(I['mla_w_q_up'], Wd['wq'], 4, "mla_q_norm", pl)
        rest += regular(I['w_o_mla'], Wd['womla'], JD, None, 0, 16, 0)
        rest += regular(I['w_o_swa'], Wd['woswa'], JD, None, 0, 16, 0)
        rest += regular(I['w_out'], Wd['wout'], JD, None, 0, 16, 0)
        rest += regular(I['ffn_w_up'], Wd['wup'], JD, "norm_ffn_pre", 0, 88, 0)
        rest += regular(I['ffn_w_down'], Wd['wdown'], JF, None, 0, 16, 0)
        return pre, rest

    def norm_T(self, bufs, blk, T0, dstT, dkey, src_dma=None, pre=None):
        tr = self.tr
        b = blk % 2
        xin = bufs['xin'][b]
        xk = 'xin%d' % b
        if src_dma is not None:
            tr.op('sp', lambda e: e.dma_start(out=xin[:], in_=src_dma), writes=[xk], dma=True)
        if pre is not None:
            pre(xin, xk)
        ss, rs, rstd, xsb = bufs['ss'], bufs['rs'], bufs['rstd'], bufs['xsb'][0]
        tr.op('act', lambda e: e.activation(out=xsb[:], in_=xin[:], func=AF.Square), reads=[xk], writes=['xsb0'])
        tr.op('dve', lambda e: e.tensor_reduce(out=ss[:], in_=xsb[:], axis=AX.X, op=ALU.add), reads=['xsb0'], writes=['ss'])
        tr.op('act', lambda e: e.activation(out=rs[:], in_=ss[:], func=AF.Sqrt, scale=1.0 / D, bias=self.epsc[:]),
              reads=['ss'], writes=['rs'])
        tr.op('dve', lambda e: e.reciprocal(out=rstd[:], in_=rs[:]), reads=['rs'], writes=['rstd'])
        if blk % 2 == 0:
            tr.op('dve', lambda e: e.tensor_scalar(out=xsb[:], in0=xin[:], scalar1=rstd[:], scalar2=None, op0=ALU.mult),
                  reads=[xk, 'rstd'], writes=['xsb0'])
        else:
            tr.op('act', lambda e: e.activation(out=xsb[:], in_=xin[:], func=AF.Copy, scale=rstd[:]),
                  reads=[xk, 'rstd'], writes=['xsb0'])
        for jg in range(4):
            bk = 6 + (jg % 2)
            for jj in range(4):
                j = jg * 4 + jj
                tr.op('pe', lambda e, j=j, jj=jj, bk=bk: e.transpose(out=self.bankb(bk)[:, jj * 128:(jj + 1) * 128],
                                                                    in_=xsb[:, j * 128:(j + 1) * 128], identity=self.ident[:]),
                      reads=['xsb0', 'ident'], writes=['P%d' % bk])
            src = self.bankb(bk, 512).rearrange("p (a c) -> p a c", a=4)
            dst = dstT[:, jg * 4:(jg + 1) * 4, T0:T0 + 128]
            wk = [dkey + str(jg * 4 + jj) for jj in range(4)]
            if jg % 2 == 0:
                tr.op('act', lambda e, src=src, dst=dst: e.activation(out=dst, in_=src, func=AF.Copy),
                      reads=['P%d' % bk], writes=wk)
            else:
                tr.op('dve', lambda e, src=src, dst=dst: e.tensor_copy(out=dst, in_=src), reads=['P%d' % bk], writes=wk)

    def tokbufs(self, les, pfx):
        return dict(xin=[self.sb(pfx + "xin%d" % i, [128, D], F32, les) for i in range(2)],
                    xsb=[self.sb(pfx + "xsb%d" % i, [128, D], BF16, les) for i in range(1)],
                    ss=self.sb(pfx + "ss", [128, 1], F32, les),
                    rs=self.sb(pfx + "rs", [128, 1], F32, les), rstd=self.sb(pfx + "rstd", [128, 1], F32, les))

    def evac(self, idx, out_ap, in_ap, reads, writes):
        if idx % 2 == 0:
            self.tr.op('act', lambda e: e.activation(out=out_ap, in_=in_ap, func=AF.Copy), reads=reads, writes=writes)
        else:
            self.tr.op('dve', lambda e: e.tensor_copy(out=out_ap, in_=in_ap), reads=reads, writes=writes)

    def featnorm(self, pbanks, nj, dim, T, bufs, dst, dkey):
        tr = self.tr
        sqc, rsb, rstdb = bufs['sqc'], bufs['rsb'], bufs['rstdb']
        for j in range(nj):
            tr.op('act', lambda e, j=j: e.activation(out=sqc[j % 2][:, 0:T], in_=self.bank(pbanks[j], T), func=AF.Square),
                  reads=['P%d' % pbanks[j]], writes=['sqc%d' % (j % 2)])
            tr.op('pe', lambda e, j=j: e.matmul(self.bank(5, T), lhsT=self.onesb[:], rhs=sqc[j % 2][:, 0:T],
                                                start=(j == 0), stop=(j == nj - 1)),
                  reads=['sqc%d' % (j % 2), 'onesb'], writes=['P5'])
        tr.op('act', lambda e: e.activation(out=rsb[:, 0:T], in_=self.bank(5, T), func=AF.Sqrt, scale=1.0 / dim, bias=self.epsc[:]),
              reads=['P5'], writes=['rsb'])
        tr.op('dve', lambda e: e.reciprocal(out=rstdb[:, 0:T], in_=rsb[:, 0:T]), reads=['rsb'], writes=['rstdb'])
        for j in range(nj):
            tr.op('dve', lambda e, j=j: e.tensor_tensor(out=dst[:, j, 0:T], in0=self.bank(pbanks[j], T), in1=rstdb[:, 0:T],
                                                        op=ALU.mult),
                  reads=['P%d' % pbanks[j], 'rstdb'], writes=[dkey + str(j)])

    def phase_k(self):
        tr, I, Wd = self.tr, self.I, self.Wd
        with ExitStack() as les:
            sb = lambda n, s, d: self.sb(n, s, d, les)
            bufs = self.tokbufs(les, "k_")
            hT2 = [sb("k_hT%d" % i, [128, JD, 512], BF16) for i in range(2)]
            wk = sb("k_win", [128, 7, JD, 128], BF16)
            wkv = sb("k_wkv", [128, 32, 2, 128], BF16)
            fb = dict(sqc=[sb("k_sqc%d" % i, [128, 512], BF16) for i in range(2)], rsb=sb("k_rsb", [128, 512], F32),
                      rstdb=sb("k_rstdb", [128, 512], F32))
            ckvn = sb("k_ckvn", [128, 2, 512], BF16)
            cs = sb("k_cos", [64, 512], F32)
            sn = sb("k_sin", [64, 512], F32)
            t1 = sb("k_t1", [64, 512], F32)
            t2 = sb("k_t2", [64, 512], F32)
            kro = sb("k_kro", [64, 512], BF16)
            kst = sb("k_kst", [128, 2, 512], BF16)
            vss = sb("k_vss", [128, 4, 4, 65], BF16)
            knst = sb("k_knst", [128, NH, 512], BF16)
            vnst = sb("k_vnst", [128, NH, 4, 129], BF16)
            tr.op('dve', lambda e: e.memset(vss[:], 1.0), writes=['vss'])
            tr.op('pool', lambda e: e.memset(vnst[:], 1.0), writes=['vnst'])
            wpre, wrest = self.phase_w_items(les)
            for it in wpre:
                it()
            self.tr.barrier()
            wpos = 0
            wper = (len(wrest) + self.NKT - 1) // self.NKT
            tr.op('sp', lambda e: e.dma_start(out=wk[:, 0:3], in_=Wd['win'][:, 4:7]), writes=['wk'], dma=True)
            tr.op('sp', lambda e: e.dma_start(out=wk[:, 3:7], in_=Wd['win'][:, 23:27]), writes=['wk'], dma=True)
            tr.op('sp', lambda e: e.dma_start(out=wkv[:], in_=Wd['wkv']), writes=['wkv'], dma=True)
            self._kev = 0

            def proj_gen(kt):
                ev = self._kev
                s0 = kt * 512
                hT = hT2[kt % 2]
                hpf = 'hT%s' % ('a' if kt % 2 == 0 else 'b')
                hk = [hpf + str(j) for j in range(JD)]
                tr.op('sp', lambda e, s0=s0: e.dma_start(out=cs[:], in_=I['cos128'][0:64, s0:s0 + 512]), writes=['cs'], dma=True)
                tr.op('sp', lambda e, s0=s0: e.dma_start(out=sn[:], in_=I['sin128'][0:64, s0:s0 + 512]), writes=['sn'], dma=True)
                for c in range(2):
                    for j in range(JD):
                        tr.op('pe', lambda e, c=c, j=j: e.matmul(self.bank(c), lhsT=wk[:, c, j, :], rhs=hT[:, j, :],
                                                                 start=(j == 0), stop=(j == JD - 1)),
                              reads=['wk', hk[j]], writes=['P%d' % c])
                self.featnorm([0, 1], 2, 256.0, 512, fb, ckvn, 'ckvn')
                yield
                for c in range(2):
                    for j in range(JD):
                        tr.op('pe', lambda e, c=c, j=j: e.matmul(self.bank(2 + c)[0:64, :], lhsT=wk[:, 2, j, c * 64:(c + 1) * 64],
                                                                 rhs=hT[:, j, :], start=(j == 0), stop=(j == JD - 1)),
                              reads=['wk', hk[j]], writes=['P%d' % (2 + c)])
                tr.op('dve', lambda e: e.tensor_tensor(out=t1[:], in0=self.bank(2)[0:64, :], in1=cs[:], op=ALU.mult),
                      reads=['P2', 'cs'], writes=['t1'])
                tr.op('dve', lambda e: e.tensor_tensor(out=t2[:], in0=self.bank(3)[0:64, :], in1=sn[:], op=ALU.mult),
                      reads=['P3', 'sn'], writes=['t2'])
                tr.op('pool', lambda e: e.tensor_tensor(out=kro[:], in0=t1[:], in1=t2[:], op=ALU.add), reads=['t1', 't2'], writes=['kro'])
                tr.op('pool', lambda e, s0=s0: e.dma_start(out=self.KR[:, s0:s0 + 512], in_=kro[:]), reads=['kro'],
                      writes=['KR%d' % kt], dma=True)
                for c in range(2):
                    for j in range(JD):
                        tr.op('pe', lambda e, c=c, j=j: e.matmul(self.bank(2 + c), lhsT=wk[:, 3 + c, j, :], rhs=hT[:, j, :],
                                                                 start=(j == 0), stop=(j == JD - 1)),
                              reads=['wk', hk[j]], writes=['P%d' % (2 + c)])
                    self.evac(ev, kst[:, c, :], self.bank(2 + c), ['P%d' % (2 + c)], ['kst'])
                    ev += 1
                tr.op('pool', lambda e, s0=s0: e.dma_start(out=self.KS[:, s0:s0 + 512].rearrange("(c p) s -> p c s", p=128), in_=kst[:]),
                      reads=['kst'], writes=['KS%d' % kt], dma=True)
                for blk in range(4):
                    bk = 4 if blk % 2 == 0 else 2
                    for j in range(JD):
                        tr.op('pe', lambda e, blk=blk, j=j, bk=bk: e.matmul(
                            self.bank(bk, 256).rearrange("p (a c) -> p a c", a=2), lhsT=hT[:, j, blk * 128:(blk + 1) * 128],
                            rhs=wk[:, 5:7, j, :], start=(j == 0), stop=(j == JD - 1)),
                            reads=['wk', hk[j]], writes=['P%d' % bk])
                    self.evac(ev, vss[:, blk, :, 0:64], self.bank(bk, 256).rearrange("p (g c) -> p g c", g=4), ['P%d' % bk], ['vss'])
                    ev += 1
                tr.op('pool', lambda e, kt=kt: e.dma_start(out=self.VS[:, 4 * kt:4 * kt + 4], in_=vss[:]), reads=['vss'],
                      writes=['VS%d' % kt], dma=True)
                yield
                cj = ['ckvn0', 'ckvn1']
                for h in range(NH):
                    bk = 2 + (h % 4)
                    for j in range(2):
                        tr.op('pe', lambda e, h=h, j=j, bk=bk: e.matmul(self.bank(bk), lhsT=wkv[:, h, j, :], rhs=ckvn[:, j, :],
                                                                        start=(j == 0), stop=(j == 1)),
                              reads=['wkv', cj[j]], writes=['P%d' % bk])
                    self.evac(ev, knst[:, h, :], self.bank(bk), ['P%d' % bk], ['knst'])
                    ev += 1
                tr.op('pool', lambda e, s0=s0: e.dma_start(out=self.KN[:, :, s0:s0 + 512].rearrange("h d s -> d h s"), in_=knst[:]),
                      reads=['knst'], writes=['KN%d' % kt], dma=True)
                yield
                i = 0
                for blk in range(4):
                    for hg in range(4):
                        bk = 2 + (i % 4)
                        i += 1
                        for j in range(2):
                            tr.op('pe', lambda e, blk=blk, hg=hg, j=j, bk=bk: e.matmul(
                                self.bank(bk).rearrange("p (a c) -> p a c", a=4), lhsT=ckvn[:, j, blk * 128:(blk + 1) * 128],
                                rhs=wkv[:, 16 + 4 * hg:20 + 4 * hg, j, :], start=(j == 0), stop=(j == 1)),
                                reads=['wkv', cj[j]], writes=['P%d' % bk])
                        self.evac(ev, vnst[:, 4 * hg:4 * hg + 4, blk, 0:128], self.bank(bk).rearrange("p (a c) -> p a c", a=4),
                                  ['P%d' % bk], ['vnst'])
                        ev += 1
                tr.op('pool', lambda e, kt=kt: e.dma_start(out=self.VN[:, :, 4 * kt:4 * kt + 4, :].rearrange("h k b c -> k h b c"),
                                                          in_=vnst[:]),
                      reads=['vnst'], writes=['VN%d' % kt], dma=True)
                self._kev = ev

            for kt in range(self.NKT + 1):
                g = proj_gen(kt - 1) if kt >= 1 else None
                for blk in range(4):
                    if kt < self.NKT:
                        self.norm_T(bufs, blk, blk * 128, hT2[kt % 2], 'hT%s' % ('a' if kt % 2 == 0 else 'b'),
                                    src_dma=I['xs'][kt * 512 + blk * 128: kt * 512 + (blk + 1) * 128, :])
                    if g is not None:
                        next(g, None)
                for it in wrest[wpos:wpos + wper]:
                    it()
                wpos += wper
            for it in wrest[wpos:]:
                it()
            self.tr.barrier()

    def _ffn_chunk(self, tr, cur, sub, wn, T, halo, hT, hk, aext, sg, gT):
        isb = wn >= JF
        jc = wn % JF
        bk = (0 + 2 * (jc % 2)) + (1 if isb else 0)
        wv, wkey = cur[0][:, sub], cur[1]
        for j in range(JD):
            tr.op('pe', lambda e, j=j, wv=wv, bk=bk: e.matmul(self.bank(bk, T), lhsT=wv[:, j, :], rhs=hT[:, j, 0:T],
                                                              start=(j == 0), stop=(j == JD - 1)),
                  reads=[wkey, hk[j]], writes=['P%d' % bk])
        if not isb:
            ax = aext[jc % 2]
            ak = 'aext%d' % (jc % 2)
            if not halo:
                tr.op('pool', lambda e, ax=ax, jc=jc: e.tensor_copy(out=ax[:, 0:2], in_=self.ahalo[:, jc, :]),
                      reads=['ahalo%d' % jc], writes=[ak])
                tr.op('act', lambda e, ax=ax, bk=bk: e.activation(out=ax[:, 2:2 + T], in_=self.bank(bk, T), func=AF.Copy),
                      reads=['P%d' % bk], writes=[ak])
            tr.op('act', lambda e, jc=jc, bk=bk: e.activation(out=self.ahalo[:, jc, :], in_=self.bank(bk)[:, T - 2:T], func=AF.Copy),
                  reads=['P%d' % bk], writes=['ahalo%d' % jc])
        else:
            ax = aext[jc % 2]
            ak = 'aext%d' % (jc % 2)
            tt = sg[jc % 2]
            tk = 'sg%d' % (jc % 2)
            tr.op('dve', lambda e, ax=ax, tt=tt, jc=jc: e.tensor_scalar(out=tt[:, 0:T], in0=ax[:, 2:2 + T], scalar1=self.cw[:, 2, jc:jc + 1],
                                                                        scalar2=self.cb[:, jc:jc + 1], op0=ALU.mult, op1=ALU.add),
                  reads=[ak, 'cw', 'cb'], writes=[tk])
            tr.op('dve', lambda e, ax=ax, tt=tt, jc=jc: e.scalar_tensor_tensor(out=tt[:, 0:T], in0=ax[:, 1:1 + T], scalar=self.cw[:, 1, jc:jc + 1],
                                                                               in1=tt[:, 0:T], op0=ALU.mult, op1=ALU.add),
                  reads=[ak, tk, 'cw'], writes=[tk])
            tr.op('dve', lambda e, ax=ax, tt=tt, jc=jc: e.scalar_tensor_tensor(out=tt[:, 0:T], in0=ax[:, 0:T], scalar=self.cw[:, 0, jc:jc + 1],
                                                                               in1=tt[:, 0:T], op0=ALU.mult, op1=ALU.add),
                  reads=[ak, tk, 'cw'], writes=[tk])
            tr.op('act', lambda e, tt=tt: e.activation(out=tt[:, 0:T], in_=tt[:, 0:T], func=AF.Gelu_apprx_tanh),
                  reads=[tk], writes=[tk])
            tr.op('dve', lambda e, tt=tt, jc=jc, bk=bk: e.tensor_tensor(out=gT[:, jc, 0:T], in0=self.bank(bk, T), in1=tt[:, 0:T], op=ALU.mult),
                  reads=['P%d' % bk, tk], writes=['gT%d' % jc])


    def phase_q(self):
        tr, I, Wd = self.tr, self.I, self.Wd
        KC = self.KC
        with ExitStack() as les:
            sb = lambda n, s, d: self.sb(n, s, d, les)
            bufs = self.tokbufs(les, "q_")
            hT = sb("q_hT", [128, JD, 512], BF16)
            R = sb("q_R", [128, 32768], BF16)
            qnT = R[:, 0:8192].rearrange("p (a t) -> p a t", a=16)
            qsT = R[:, 8192:16384].rearrange("p (a t) -> p a t", a=16)
            oaT = qsT
            obT = R[:, 16384:24576].rearrange("p (a t) -> p a t", a=16)
            qrT = R[:, 24576:32768].rearrange("p (a t) -> p a t", a=16)
            tr.op('pool', lambda e: e.memset(qrT, 0.0), writes=['qr%d' % i for i in range(16)])
            gT = R[:, 0:JF * 512].rearrange("p (a t) -> p a t", a=JF)
            mT = qnT
            A = sb("q_A", [128, 3 * (2 * KC * 128 + KC * 129)], BF16)
            asz = 2 * KC * 128 + KC * 129
            ygT = A[:, 0:8192].rearrange("p (a t) -> p a t", a=16) if 3 * asz >= 8192 else sb("q_ygT", [128, JD, 512], BF16)
            knc = [A[:, i * asz: i * asz + KC * 128] for i in range(3)]
            krc = [A[:, i * asz + KC * 128: i * asz + 2 * KC * 128] for i in range(3)]
            vac = [A[:, i * asz + 2 * KC * 128: (i + 1) * asz].rearrange("p (b c) -> p b c", c=129) for i in range(3)]
            wr = [sb("q_wr%d" % i, [128, 5632], BF16) for i in range(2)]
            fb = dict(sqc=[sb("q_sqc%d" % i, [128, 512], BF16) for i in range(2)], rsb=sb("q_rsb", [128, 512], F32),
                      rstdb=sb("q_rstdb", [128, 512], F32))
            cqn = sb("q_cqn", [128, 4, 512], BF16)
            cs = sb("q_cos", [128, 512], F32)
            sn = sb("q_sin", [128, 512], F32)
            pT = [sb("q_pT%d" % i, [128, 512], BF16) for i in range(3)]
            otok = sb("q_otok", [128, 4, 128], BF16)
            dd = sb("q_dd", [128, 4], F32)
            rden = sb("q_rden", [128, 4], F32)
            ksT2 = sb("q_ksT2", [128, 2, 4, 640], BF16)
            tr.op('pool', lambda e: e.memset(ksT2[:], 0.0), writes=['ksT2'])
            vsA = sb("q_vsA", [128, 5, 4, 65], BF16)
            pTs = [sb("q_pTs%d" % i, [128, 1024], BF16) for i in range(2)]
            obtok = [sb("q_obtok%d" % i, [128, 4, 128], BF16) for i in range(2)]
            eh = [sb("q_eh%d" % i, [128, 2, 128], BF16) for i in range(2)]
            sg = [sb("q_sg%d" % i, [128, 512], F32) for i in range(2)]
            t1, t2 = sg
            ysq = fb['sqc']
            rtok2 = sb("q_rtok2", [128, 8], F32)
            aext = [sb("q_aext%d" % i, [128, 516], F32) for i in range(2)]
            self._wl = 0
            self._ev = 0

            def wload(wname, n0, g, J):
                b = self._wl % 2
                self._wl += 1
                v = wr[b][:, 0:g * J * 128].rearrange("p (g j c) -> p g j c", g=g, j=J)
                tr.op('sp', lambda e: e.dma_start(out=v, in_=Wd[wname][:, n0:n0 + g]), writes=['wr%d' % b], dma=True)
                return v, 'wr%d' % b

            def proj_groups(wname, chunks, J, gmax):
                groups = []
                i = 0
                while i < len(chunks):
                    g = 1
                    while g < gmax and i + g < len(chunks) and chunks[i + g] == chunks[i] + g:
                        g += 1
                    groups.append((chunks[i], g))
                    i += g
                pend = wload(wname, groups[0][0], groups[0][1], J)
                for gi, (n0, g) in enumerate(groups):
                    cur = pend
                    pend = wload(wname, groups[gi + 1][0], groups[gi + 1][1], J) if gi + 1 < len(groups) else None
                    for k in range(g):
                        yield n0 + k, cur[0][:, k], cur[1]

            def post_norm(T, nb, gname, res_src, res_dst_fn, last):
                rsb, rstdb = fb['rsb'], fb['rstdb']
                tr.op('act', lambda e: e.activation(out=rsb[:, 0:T], in_=self.bank(5, T), func=AF.Sqrt, scale=1.0 / D, bias=self.epsc[:]),
                      reads=['P5'], writes=['rsb'])
                tr.op('dve', lambda e: e.reciprocal(out=rstdb[:, 0:T], in_=rsb[:, 0:T]), reads=['rsb'], writes=['rstdb'])
                if self.stop_after == 'q8b':
                    return
                tr.op('sp', lambda e: e.dma_start(out=self.RS[0:1, 0:T], in_=rstdb[0:1, 0:T]), reads=['rstdb'], writes=['RS'], dma=True)
                tr.op('sp', lambda e: e.dma_start(out=rtok2[:, 0:nb], in_=self.RS[0, 0:T].rearrange("(b p) -> p b", p=128),
                                                  allow_slow_non_contiguous=True), reads=['RS'], writes=['rtok2'], dma=True)
                if self.stop_after == 'q8c':
                    return
                for blk in range(nb):
                    b = blk % 2
                    xin = bufs['xin'][b]
                    xk = 'xin%d' % b
                    tr.op('sp', lambda e, blk=blk, xin=xin: e.dma_start(out=xin[:], in_=res_src(blk)), writes=[xk], dma=True)
                    for mg in range(4):
                        bk = 6 + (mg % 2)
                        for mm in range(4):
                            m = mg * 4 + mm
                            tr.op('pe', lambda e, m=m, mm=mm, bk=bk, blk=blk: e.transpose(
                                out=self.bankb(bk)[:, mm * 128:(mm + 1) * 128], in_=ygT[:, m, blk * 128:(blk + 1) * 128],
                                identity=self.ident[:]), reads=['yg%d' % m, 'ident'], writes=['P%d' % bk])
                        tr.op('dve', lambda e, mg=mg, bk=bk, blk=blk, xin=xin: e.scalar_tensor_tensor(
                            out=xin[:, mg * 512:(mg + 1) * 512], in0=self.bankb(bk, 512), scalar=rtok2[:, blk:blk + 1],
                            in1=xin[:, mg * 512:(mg + 1) * 512], op0=ALU.mult, op1=ALU.add),
                            reads=['P%d' % bk, 'rtok2', xk], writes=[xk])
                    if self.stop_after == 'q8d':
                        continue
                    res_dst_fn(blk, xin, xk)

            def y_chunk(m, ps_bank, T, nb, gname):
                pk = 'P%d' % ps_bank
                tr.op('act', lambda e: e.activation(out=ysq[m % 2][:, 0:T], in_=self.bank(ps_bank, T), func=AF.Square),
                      reads=[pk], writes=['sqc%d' % (m % 2)])

                tr.op('dve', lambda e: e.tensor_scalar(out=ygT[:, m, 0:T], in0=self.bank(ps_bank, T),
                                                       scalar1=self.gains[gname][:, m:m + 1], scalar2=None, op0=ALU.mult),
                      reads=[pk, 'g_' + gname], writes=['yg%d' % m])
                tr.op('pe', lambda e: e.matmul(self.bank(5, T), lhsT=self.onesb[:], rhs=ysq[m % 2][:, 0:T],
                                               start=(m == 0), stop=(m == JD - 1)),
                      reads=['sqc%d' % (m % 2), 'onesb'], writes=['P5'])

            tiles = [(self.OWN0 - 1, 1, True)] + [(self.OWN0 + 4 * i, 4, False) for i in range(self.NT)]
            for ti, (B0, nb, halo) in enumerate(tiles):
                T = nb * 128
                s0 = B0 * 128
                r0 = s0 - self.OWN0 * 128
                hk = ['hT%d' % j for j in range(JD)]
                if not getattr(self, '_q1_hoisted', False):
                    for blk in range(nb):
                        self.norm_T(bufs, blk, blk * 128, hT, 'hT', src_dma=I['xs'][s0 + blk * 128: s0 + (blk + 1) * 128, :])
                self._q1_hoisted = False
                tr.op('sp', lambda e, s0=s0, T=T: e.dma_start(out=cs[:, 0:T], in_=I['cos128'][:, s0:s0 + T]), writes=['cs'], dma=True)
                tr.op('sp', lambda e, s0=s0, T=T: e.dma_start(out=sn[:, 0:T], in_=I['sin128'][:, s0:s0 + T]), writes=['sn'], dma=True)
                for n, wv, wkey in proj_groups('win', [0, 1, 2, 3], JD, 2):
                    for j in range(JD):
                        tr.op('pe', lambda e, n=n, j=j, wv=wv: e.matmul(self.bank(n, T), lhsT=wv[:, j, :], rhs=hT[:, j, 0:T],
                                                                        start=(j == 0), stop=(j == JD - 1)),
                              reads=[wkey, hk[j]], writes=['P%d' % n])
                self.featnorm([0, 1, 2, 3], 4, 512.0, T, fb, cqn, 'cqn')
                cqk = ['cqn%d' % j for j in range(4)]
                i = 0
                for n, wv, wkey in proj_groups('wq', list(range(16)), 4, 8):
                    bk = i % 4
                    i += 1
                    for j in range(4):
                        tr.op('pe', lambda e, j=j, wv=wv, bk=bk: e.matmul(self.bank(bk, T), lhsT=wv[:, j, :], rhs=cqn[:, j, 0:T],
                                                                          start=(j == 0), stop=(j == 3)),
                              reads=[wkey, cqk[j]], writes=['P%d' % bk])
                    self.evac(self._ev, qnT[:, n, 0:T], self.bank(bk, T), ['P%d' % bk], ['qn%d' % n])
                    self._ev += 1
                order = []
                for hp in range(8):
                    order += [16 + hp, 24 + hp]
                for n, wv, wkey in proj_groups('wq', order, 4, 1):
                    hp = (n - 16) % 8
                    rot = n >= 24
                    bk = 1 if rot else 0
                    for j in range(4):
                        tr.op('pe', lambda e, j=j, wv=wv, bk=bk: e.matmul(self.bank(bk, T), lhsT=wv[:, j, :], rhs=cqn[:, j, 0:T],
                                                                          start=(j == 0), stop=(j == 3)),
                              reads=[wkey, cqk[j]], writes=['P%d' % bk])
                    if not rot:
                        tr.op('dve', lambda e: e.tensor_tensor(out=t1[:, 0:T], in0=self.bank(0, T), in1=cs[:, 0:T], op=ALU.mult),
                              reads=['P0', 'cs'], writes=['sg0'])
                    else:
                        tr.op('dve', lambda e: e.tensor_tensor(out=t2[:, 0:T], in0=self.bank(1, T), in1=sn[:, 0:T], op=ALU.mult),
                              reads=['P1', 'sn'], writes=['sg1'])
                        for hh in range(2):
                            tr.op('pool', lambda e, hp=hp, hh=hh: e.tensor_tensor(
                                out=qrT[hh * 64:(hh + 1) * 64, 2 * hp + hh, 0:T], in0=t1[hh * 64:(hh + 1) * 64, 0:T],
                                in1=t2[hh * 64:(hh + 1) * 64, 0:T], op=ALU.add),
                                reads=['sg0', 'sg1'], writes=['qr%d' % (2 * hp + hh)])
                i = 0
                for n, wv, wkey in proj_groups('win', list(range(7, 23)), JD, 2):
                    bk = i % 4
                    i += 1
                    for j in range(JD):
                        tr.op('pe', lambda e, j=j, wv=wv, bk=bk: e.matmul(self.bank(bk, T), lhsT=wv[:, j, :], rhs=hT[:, j, 0:T],
                                                                          start=(j == 0), stop=(j == JD - 1)),
                              reads=[wkey, hk[j]], writes=['P%d' % bk])
                    self.evac(self._ev, qsT[:, n - 7, 0:T], self.bank(bk, T), ['P%d' % bk], ['qs%d' % (n - 7)])
                    self._ev += 1
                if self.stop_after == 'q4':
                    self.tr.barrier()
                    return
                nkb = nb + 1
                for half in range(2):
                    tr.op('sp', lambda e, half=half: e.dma_start(
                        out=ksT2[half * 64:(half + 1) * 64, half, :, 0:nkb * 128],
                        in_=self.KS[:, (B0 - 1) * 128:(B0 + nb) * 128].rearrange("(g d) s -> d g s", d=64)),
                        reads=['KS%d' % k for k in range((B0 - 1) // 4, (B0 + nb - 1) // 4 + 1)], writes=['ksT2'], dma=True)
                tr.op('sp', lambda e: e.dma_start(out=vsA[:, 0:nkb], in_=self.VS[:, B0 - 1:B0 + nb]),
                      reads=['VS%d' % k for k in range((B0 - 1) // 4, (B0 + nb - 1) // 4 + 1)], writes=['vsA'], dma=True)
                for h in range(NSH):
                    g = h // 8
                    hp, ho = h // 2, (h % 2) * 64
                    sbk = 0 if h % 2 == 0 else 2
                    SP_ = self.PS[:, sbk * 512: sbk * 512 + 2 * nb * 128]
                    skeys = ['P%d' % sbk, 'P%d' % (sbk + 1)]
                    obk = 4 + (h % 2)
                    mmlist = []
                    for kb in range(nkb):
                        if kb == 0:
                            mmlist.append((kb, 0, 0, 1))
                        elif kb == nb:
                            mmlist.append((kb, 2 * nb - 1, nb - 1, 1))
                        else:
                            seg0 = 2 * kb - 1
                            if (seg0 * 128) // 512 != ((seg0 + 2) * 128 - 1) // 512:
                                mmlist.append((kb, seg0, kb - 1, 1))
                                mmlist.append((kb, seg0 + 1, kb, 1))
                            else:
                                mmlist.append((kb, seg0, kb - 1, 2))
                    for (kb, seg, qb, nq) in mmlist:
                        tr.op('pe', lambda e, kb=kb, seg=seg, qb=qb, nq=nq: e.matmul(
                            SP_[:, seg * 128:(seg + nq) * 128], lhsT=ksT2[:, h % 2, g, kb * 128:(kb + 1) * 128],
                            rhs=qsT[:, hp, qb * 128:(qb + nq) * 128], start=True, stop=True),
                            reads=['ksT2', 'qs%d' % hp], writes=skeys)
                    pt = pTs[h % 2]
                    pk = 'pTs%d' % (h % 2)
                    tr.op('act', lambda e, pt=pt: e.activation(out=pt[:, 0:128], in_=SP_[:, 0:128], func=AF.Exp, scale=SWA_SCALE,
                                                               bias=self.kmask[:, B0 - 1:B0]),
                          reads=skeys + ['kmask'], writes=[pk])
                    tr.op('act', lambda e, pt=pt: e.activation(out=pt[:, 128:2 * nb * 128], in_=SP_[:, 128:2 * nb * 128], func=AF.Exp,
                                                               scale=SWA_SCALE), reads=skeys, writes=[pk])
                    pv = pt[:, 0:2 * nb * 128].rearrange("p (b r q) -> p b r q", b=nb, r=2)
                    ehh = eh[h % 2]
                    tr.op('sp', lambda e, ehh=ehh, h=h: e.dma_start(out=ehh[:].rearrange("p r q -> p (r q)"), in_=self.ED[:, h, :]),
                          writes=['eh%d' % (h % 2)], dma=True)
                    ev_ = ehh[:].unsqueeze(1).to_broadcast([128, nb, 2, 128])
                    tr.op('pool' if h % 2 == 0 else 'dve',
                          lambda e, pv=pv, ev_=ev_: e.tensor_tensor(out=pv, in0=pv, in1=ev_, op=ALU.mult),
                          reads=[pk, 'eh%d' % (h % 2)], writes=[pk])
                    OB = self.bank(obk, nb * 65).rearrange("p (b c) -> p b c", c=65)
                    first = True
                    for qb in range(nb):
                        for r in range(2):
                            tr.op('pe', lambda e, qb=qb, r=r, first=first, pv=pv: e.matmul(
                                OB[:, qb, :], lhsT=pv[:, qb, r, :], rhs=vsA[:, qb + r, g, :], start=first,
                                stop=(qb == nb - 1 and r == 1), skip_group_check=True),
                                reads=[pk, 'vsA'], writes=['P%d' % obk])
                            first = False
                    tr.op('dve', lambda e, OB=OB, h=h: e.tensor_scalar(out=dd[:, 0:nb], in0=OB[:, :, 64], scalar1=self.expsink[:, h:h + 1],
                                                                       scalar2=None, op0=ALU.add),
                          reads=['P%d' % obk, 'expsink'], writes=['dd'])
                    tr.op('dve', lambda e: e.reciprocal(out=rden[:, 0:nb], in_=dd[:, 0:nb]), reads=['dd'], writes=['rden'])
                    obt = obtok[hp % 2]
                    tr.op('dve', lambda e, OB=OB, obt=obt: e.tensor_tensor(
                        out=obt[:, 0:nb, ho:ho + 64], in0=OB[:, :, 0:64], in1=rden[:, 0:nb].unsqueeze(2).to_broadcast([128, nb, 64]),
                        op=ALU.mult), reads=['P%d' % obk, 'rden'], writes=['obtok%d' % (hp % 2)])
                    if h % 2 == 1:
                        tbk = 6 + (hp % 2)
                        for qb in range(nb):
                            tr.op('pe', lambda e, qb=qb, obt=obt, tbk=tbk: e.transpose(out=self.bankb(tbk)[:, qb * 128:(qb + 1) * 128],
                                                                                       in_=obt[:, qb, :], identity=self.ident[:]),
                                  reads=['obtok%d' % (hp % 2), 'ident'], writes=['P%d' % tbk])
                        tr.op('act', lambda e, hp=hp, tbk=tbk: e.activation(out=obT[:, hp, 0:T], in_=self.bankb(tbk, T), func=AF.Copy),
                              reads=['P%d' % tbk], writes=['ob%d' % hp])
                if self.stop_after == 'q5':
                    self.tr.barrier()
                    return
                nkbm = B0 + nb
                nch = (nkbm + KC - 1) // KC
                self._kv = getattr(self, '_kv', 0)

                def kvload(h, ci):
                    sl = self._kv % 3
                    self._kv += 1
                    k0 = ci * KC
                    kn_ = min(KC, nkbm - k0)
                    kts = list(range(k0 // 4, (k0 + kn_ - 1) // 4 + 1))
                    tr.op('sp', lambda e: e.dma_start(out=knc[sl][:, 0:kn_ * 128], in_=self.KN[h, :, k0 * 128:(k0 + kn_) * 128]),
                          reads=['KN%d' % k for k in kts], writes=['knc%d' % sl], dma=True)
                    for half in range(2):
                        tr.op('sp', lambda e, half=half: e.dma_start(out=krc[sl][half * 64:(half + 1) * 64, 0:kn_ * 128],
                                                                     in_=self.KR[:, k0 * 128:(k0 + kn_) * 128]),
                              reads=['KR%d' % k for k in kts], writes=['krc%d' % sl], dma=True)
                    tr.op('sp', lambda e: e.dma_start(out=vac[sl][:, 0:kn_, :], in_=self.VN[h, :, k0:k0 + kn_, :]),
                          reads=['VN%d' % k for k in kts], writes=['vac%d' % sl], dma=True)
                    return sl, k0, kn_

                loads = [(h, ci) for h in range(NH) for ci in range(nch)]
                loaded = {}

                def ensure(gi):
                    if gi < len(loads) and gi not in loaded:
                        loaded[gi] = kvload(*loads[gi])

                ensure(0)
                self._pt = getattr(self, '_pt', 0)
                for h in range(NH):
                    hp, ho = h // 2, (h % 2) * 64
                    oa, obb = (2, 3) if h % 2 == 0 else (4, 5)
                    obanks = [oa, obb]
                    started = [False, False]
                    steps = []
                    for ci in range(nch):
                        k0 = ci * KC
                        for kbl in range(min(KC, nkbm - k0)):
                            steps.append((h * nch + ci, kbl, k0 + kbl))

                    def s_step(si):
                        gi, kbl, kb = steps[si]
                        ensure(gi)
                        if kbl == 0:
                            ensure(gi + 1)
                        sl = loaded[gi][0]
                        r = max(0, kb - B0)
                        c0 = r * 128
                        sbk = si % 2
                        tr.op('pe', lambda e: e.matmul(self.bank(sbk)[:, c0:T], lhsT=knc[sl][:, kbl * 128:(kbl + 1) * 128],
                                                       rhs=qnT[:, h, c0:T], start=True, stop=False),
                              reads=['knc%d' % sl, 'qn%d' % h], writes=['P%d' % sbk])
                        tr.op('pe', lambda e: e.matmul(self.bank(sbk)[:, c0:T], lhsT=krc[sl][:, kbl * 128:(kbl + 1) * 128],
                                                       rhs=qrT[:, h, c0:T], start=False, stop=True),
                              reads=['krc%d' % sl, 'qr%d' % h], writes=['P%d' % sbk])

                    def e_step(si):
                        gi, kbl, kb = steps[si]
                        sl = loaded[gi][0]
                        r = max(0, kb - B0)
                        c0 = r * 128
                        sbk = si % 2
                        pi = self._pt % 3
                        self._pt += 1
                        p = pT[pi]
                        tr.op('act', lambda e: e.activation(out=p[:, c0:T], in_=self.bank(sbk)[:, c0:T], func=AF.Exp, scale=MLA_SCALE,
                                                            bias=self.kmask[:, kb:kb + 1]),
                              reads=['P%d' % sbk, 'kmask'], writes=['pT%d' % pi])
                        if kb >= B0:
                            tr.op('pool', lambda e: e.affine_select(out=p[:, c0:c0 + 128], in_=p[:, c0:c0 + 128], pattern=[[1, 128]],
                                                                    compare_op=ALU.is_ge, fill=0.0, base=0, channel_multiplier=-1),
                                  reads=['pT%d' % pi], writes=['pT%d' % pi])
                        return pi

                    def v_step(si, pi):
                        gi, kbl, kb = steps[si]
                        sl = loaded[gi][0]
                        r = max(0, kb - B0)
                        p = pT[pi]
                        for qs in range(r, nb):
                            ob_ = obanks[qs // 2]
                            st = not started[qs // 2]
                            started[qs // 2] = True
                            last = (kb == B0 + qs) and (qs % 2 == 1 or qs == nb - 1)
                            tr.op('pe', lambda e, qs=qs, ob_=ob_, st=st, last=last: e.matmul(
                                self.bank(ob_)[:, (qs % 2) * 129:(qs % 2) * 129 + 129], lhsT=p[:, qs * 128:(qs + 1) * 128],
                                rhs=vac[sl][:, kbl, :], start=st, stop=last, skip_group_check=True),
                                reads=['pT%d' % pi, 'vac%d' % sl], writes=['P%d' % ob_])

                    ns = len(steps)
                    s_step(0)
                    for si in range(ns):
                        if si + 1 < ns:
                            s_step(si + 1)
                        pi = e_step(si)
                        v_step(si, pi)
                    for bi in range((nb + 1) // 2):
                        nq = min(2, nb - 2 * bi)
                        OV = self.bank(obanks[bi], 2 * 129).rearrange("p (q c) -> p q c", c=129)
                        tr.op('dve', lambda e, OV=OV, bi=bi, nq=nq: e.tensor_scalar(out=dd[:, 2 * bi:2 * bi + nq], in0=OV[:, 0:nq, 128],
                                                                                    scalar1=1e-30, scalar2=None, op0=ALU.add),
                              reads=['P%d' % obanks[bi]], writes=['dd'])
                        tr.op('dve', lambda e, bi=bi, nq=nq: e.reciprocal(out=rden[:, 2 * bi:2 * bi + nq], in_=dd[:, 2 * bi:2 * bi + nq]),
                              reads=['dd'], writes=['rden'])
                        tr.op('dve', lambda e, OV=OV, bi=bi, nq=nq: e.tensor_tensor(
                            out=otok[:, 2 * bi:2 * bi + nq, :], in0=OV[:, 0:nq, 0:128],
                            in1=rden[:, 2 * bi:2 * bi + nq].unsqueeze(2).to_broadcast([128, nq, 128]), op=ALU.mult),
                            reads=['P%d' % obanks[bi], 'rden'], writes=['otok'])
                    tbk = 6 + (h % 2)
                    for qb in range(nb):
                        tr.op('pe', lambda e, qb=qb, tbk=tbk: e.transpose(out=self.bankb(tbk)[:, qb * 128:(qb + 1) * 128], in_=otok[:, qb, :],
                                                                         identity=self.ident[:]),
                              reads=['otok', 'ident'], writes=['P%d' % tbk])
                    tr.op('act', lambda e, h=h, tbk=tbk: e.activation(out=oaT[:, h, 0:T], in_=self.bankb(tbk, T), func=AF.Copy),
                          reads=['P%d' % tbk] + ['qs%d' % h], writes=['oa%d' % h, 'qs%d' % h])
                if self.debug and not halo and B0 == self.OWN0:
                    for nm, src in [("dbg_oa", oaT), ("dbg_ob", obT), ("dbg_qn", qnT), ("dbg_hT", hT[:])]:
                        dt_ = self.dram_scr(nm, [128, 16, 512], BF16)
                        tr.op('sp', lambda e, dt_=dt_, src=src: e.dma_start(out=dt_, in_=src), reads=['oa%d' % i for i in range(16)] + ['ob%d' % i for i in range(16)], dma=True)
                    dt_ = self.dram_scr("dbg_qr", [128, 16, 512], BF16)
                    tr.op('sp', lambda e, dt_=dt_: e.dma_start(out=dt_, in_=qrT), dma=True)
                self.tr.barrier()
                if self.stop_after == 'q6':
                    self.tr.barrier()
                    return
                wl = []
                for n2 in range(8):
                    wl.append(('womla', 2 * n2))
                    wl.append(('woswa', 2 * n2))
                    wl.append(('win', 27 + 2 * n2))
                    wl.append(('win', 43 + 2 * n2))
                pend = wload(wl[0][0], wl[0][1], 2, JD)
                for wi, (wname, wn) in enumerate(wl):
                    cur = pend
                    pend = wload(wl[wi + 1][0], wl[wi + 1][1], 2, JD) if wi + 1 < len(wl) else None
                    kind = wi % 4
                    for sub in range(2):
                        n = 2 * (wi // 4) + sub
                        bk = (n % 2) * 4 + kind
                        wv, wkey = cur[0][:, sub], cur[1]
                        for j in range(JD):
                            if kind == 0:
                                rhs, rk = oaT[:, j, 0:T], 'oa%d' % j
                            elif kind == 1:
                                rhs, rk = obT[:, j, 0:T], 'ob%d' % j
                            else:
                                rhs, rk = hT[:, j, 0:T], hk[j]
                            tr.op('pe', lambda e, j=j, wv=wv, bk=bk, rhs=rhs: e.matmul(self.bank(bk, T), lhsT=wv[:, j, :], rhs=rhs,
                                                                                       start=(j == 0), stop=(j == JD - 1)),
                                  reads=[wkey, rk], writes=['P%d' % bk])
                    if kind != 3:
                        continue
                    for sub in range(2):
                        n = 2 * (wi // 4) + sub
                        b0 = 0 if n % 2 == 0 else 4
                        tr.op('act', lambda e, b0=b0: e.activation(out=sg[0][:, 0:T], in_=self.bank(b0 + 2, T), func=AF.Sigmoid),
                              reads=['P%d' % (b0 + 2)], writes=['sg0'])
                        tr.op('act', lambda e, b0=b0: e.activation(out=sg[1][:, 0:T], in_=self.bank(b0 + 3, T), func=AF.Sigmoid),
                              reads=['P%d' % (b0 + 3)], writes=['sg1'])
                        tr.op('dve', lambda e, b0=b0: e.tensor_tensor(out=sg[0][:, 0:T], in0=self.bank(b0, T), in1=sg[0][:, 0:T], op=ALU.mult),
                              reads=['P%d' % b0, 'sg0'], writes=['sg0'])
                        tr.op('dve', lambda e, b0=b0: e.tensor_tensor(out=sg[1][:, 0:T], in0=self.bank(b0 + 1, T), in1=sg[1][:, 0:T], op=ALU.mult),
                              reads=['P%d' % (b0 + 1), 'sg1'], writes=['sg1'])
                        tr.op('pool', lambda e, n=n: e.tensor_tensor(out=mT[:, n, 0:T], in0=sg[0][:, 0:T], in1=sg[1][:, 0:T], op=ALU.add),
                              reads=['sg0', 'sg1'], writes=['mT%d' % n])
                if self.debug and not halo and B0 == self.OWN0:
                    dt_ = self.dram_scr("dbg_m", [128, 16, 512], BF16)
                    tr.op('sp', lambda e, dt_=dt_: e.dma_start(out=dt_, in_=mT), reads=['mT%d' % i for i in range(16)], dma=True)
                self.tr.barrier()
                if self.stop_after == 'q7':
                    self.tr.barrier()
                    return
                i = 0
                for m, wv, wkey in proj_groups('wout', list(range(16)), JD, 2):
                    bk = i % 4
                    i += 1
                    for n in range(JD):
                        tr.op('pe', lambda e, n=n, wv=wv, bk=bk: e.matmul(self.bank(bk, T), lhsT=wv[:, n, :], rhs=mT[:, n, 0:T],
                                                                          start=(n == 0), stop=(n == JD - 1)),
                              reads=[wkey, 'mT%d' % n], writes=['P%d' % bk])
                    y_chunk(m, bk, T, nb, 'norm_mix_post')

                if self.stop_after == 'q8a':
                    self.tr.barrier()
                    return

                def x1_done(blk, xin, xk):
                    if not halo:
                        tr.op('sp', lambda e: e.dma_start(out=self.X1S[r0 + blk * 128:r0 + (blk + 1) * 128, :], in_=xin[:]),
                              reads=[xk], writes=['X1S'], dma=True)
                    self.norm_T(bufs, blk, blk * 128, hT, 'hT')

                post_norm(T, nb, 'norm_mix_post', lambda blk: I['xs'][s0 + blk * 128:s0 + (blk + 1) * 128, :], x1_done, False)
                self.tr.barrier()
                if self.stop_after in ('q8b', 'q8c', 'q8d'):
                    return
                if self.stop_after == 'q8':
                    self.tr.barrier()
                    return
                wl = []
                for j2 in range(JF // 2):
                    wl.append(2 * j2)
                    if not halo:
                        wl.append(JF + 2 * j2)
                pend = wload('wup', wl[0], 2, JD)
                for wi, wn0 in enumerate(wl):
                    cur = pend
                    pend = wload('wup', wl[wi + 1], 2, JD) if wi + 1 < len(wl) else None
                    for sub in range(2):
                        self._ffn_chunk(tr, cur, sub, wn0 + sub, T, halo, hT, hk, aext, sg, gT)
                if False:
                    isb = jc = bk = wv = wkey = None
                    pass
                if halo:
                    self.tr.barrier()
                    if self.stop_after == 'halo':
                        return
                    continue
                i = 0
                for m, wv, wkey in proj_groups('wdown', list(range(16)), JF, 1):
                    bk = i % 4
                    i += 1
                    for k in range(JF):
                        tr.op('pe', lambda e, k=k, wv=wv, bk=bk: e.matmul(self.bank(bk, T), lhsT=wv[:, k, :], rhs=gT[:, k, 0:T],
                                                                          start=(k == 0), stop=(k == JF - 1)),
                              reads=[wkey, 'gT%d' % k], writes=['P%d' % bk])
                    y_chunk(m, bk, T, nb, 'norm_ffn_post')
                    if m % 4 == 3 and ti + 1 < len(tiles):
                        nB0 = tiles[ti + 1][0]
                        nblk = m // 4
                        self.norm_T(bufs, nblk, nblk * 128, hT, 'hT',
                                    src_dma=I['xs'][nB0 * 128 + nblk * 128: nB0 * 128 + (nblk + 1) * 128, :])
                        self._q1_hoisted = True

                def out_done(blk, xin, xk):
                    tr.op('sp', lambda e: e.dma_start(out=self.out[r0 + blk * 128:r0 + (blk + 1) * 128, :], in_=xin[:]),
                          reads=[xk], writes=['OUT'], dma=True)

                post_norm(T, nb, 'norm_ffn_post', lambda blk: self.X1S[r0 + blk * 128:r0 + (blk + 1) * 128, :], out_done, True)
                self.tr.barrier()


def _t5_bucket(n):
    n = np.maximum(n, 0)
    nf = np.maximum(n, 1).astype(np.float32)
    large = 16 + (np.log(nf / np.float32(16)) / np.float32(math.log(128 / 16)) * np.float32(16)).astype(np.int32)
    large = np.minimum(large, 31)
    return np.where(n < 16, n, large)


def _onehot():
    k = np.arange(128)[:, None, None]
    r = np.arange(2)[None, :, None]
    q = np.arange(128)[None, None, :]
    dist = q + 128 * (1 - r) - k
    valid = (dist >= 0) & (dist < 128)
    b = np.where(valid, _t5_bucket(dist), 32)
    oh = np.zeros((33, 128, 2, 128), np.float32)
    for i in range(33):
        oh[i] = (b == i)
    return oh.reshape(33, -1)


_CACHE = {}
NCORES_DEBUG = None


def run(inputs, S):
    x = np.asarray(inputs['x'], np.float32)
    B = x.shape[0]
    CH = S // 4
    NBLK = S // 128
    if S not in _CACHE:
        _CACHE[S] = Builder(S).build()
    nc = _CACHE[S]
    inv = (10000.0 ** (-np.arange(0, 64, 2, dtype=np.float32) / np.float32(64))).astype(np.float32)
    oh = _onehot()
    shared = {}
    for nm in ["norm_mix_pre", "norm_mix_post", "norm_ffn_pre", "norm_ffn_post", "w_in", "mla_q_norm", "mla_w_q_up",
               "mla_kv_norm", "mla_w_kv_up", "swa_sinks", "w_o_mla", "w_o_swa", "w_out", "ffn_w_up", "ffn_conv_w",
               "ffn_conv_b", "ffn_w_down"]:
        a = np.asarray(inputs[nm], np.float32)
        shared[nm] = np.ascontiguousarray(a.reshape(a.shape[1:]))
    shared["rel_bias_table"] = np.ascontiguousarray(np.asarray(inputs["rel_bias_table"], np.float32))
    shared["onehot"] = oh
    in_maps = []
    for core in range(8):
        b, c = core // 4, core % 4
        if b >= B:
            b = B - 1
        pad = (3 - c) * CH
        xs = np.zeros((S, D), np.float32)
        xs[pad:] = x[b, :S - pad]
        pos = (np.arange(S) - pad).astype(np.float32)
        ang = (pos[None, :] * inv[:, None]).astype(np.float32)
        cos = np.cos(ang.astype(np.float64)).astype(np.float32)
        sin = np.sin(ang.astype(np.float64)).astype(np.float32)
        cos128 = np.ascontiguousarray(np.tile(cos, (4, 1)))
        sin128 = np.ascontiguousarray(np.tile(sin, (4, 1)))
        km = np.zeros((128, NBLK), np.float32)
        km[:, :pad // 128] = NEG
        m = dict(shared)
        m.update(xs=xs, cos128=cos128, sin128=sin128, kmask=km)
        in_maps.append(m)
    ncores = NCORES_DEBUG or 8
    res = run_bass_kernel_spmd(nc, in_maps[:ncores], core_ids=list(range(ncores)))
    out = np.zeros((B, S, D), np.float32)
    for core in range(ncores):
        b, c = core // 4, core % 4
        if b < B:
            out[b, c * CH:(c + 1) * CH] = res.results[core]["out"]
    return out


def kernel(**inputs):
    return run(inputs, 16384)
```

```python
import math
from contextlib import ExitStack

import numpy as np
import concourse.bass as bass
import concourse.mybir as mybir
from concourse.bass_utils import run_bass_kernel_spmd

F32 = mybir.dt.float32
BF16 = mybir.dt.bfloat16
AF = mybir.ActivationFunctionType
ALU = mybir.AluOpType
AX = mybir.AxisListType

D = 2048
JD = 16
NH = 16
NSH = 32
DFF = 5632
JF = 44
EPS = 1e-6
NEG = -30000.0
MLA_SCALE = 192.0 ** -0.5
SWA_SCALE = 64.0 ** -0.5

EPOCH = 16000
NDSEM = 16


class Tracer:
    ENG = ('pe', 'act', 'dve', 'pool', 'sp')

    def __init__(self, nc, es):
        self.nc = nc
        self.es = es
        self.eng = {'pe': nc.tensor, 'act': nc.scalar, 'dve': nc.vector, 'pool': nc.gpsimd, 'sp': nc.sync}
        self.count = {e: 0 for e in self.ENG}
        self.known = {e: {f: -1 for f in self.ENG} for e in self.ENG}
        self.known_dma = {e: set() for e in self.ENG}
        self.last_w = {}
        self.readers = {}
        self.ndma = {'sp': 0, 'pool': 0, 'act': 0}
        self.sems = {}
        self.nwaits = 0

    def sem(self, key):
        s = self.sems.get(key)
        if s is None:
            s = self.es.enter_context(self.nc.semaphore("s_" + "_".join(str(k) for k in key)))
            self.sems[key] = s
        return s

    def _need(self, c, sig):
        if sig[0] == 'c':
            e, n = sig[1], sig[2]
            if self.known[c][e] >= n:
                return
            self.known[c][e] = n
            self.eng[c].wait_ge(self.sem(('S', e, n // EPOCH)), n % EPOCH + 1)
        else:
            q, d = sig[1], sig[2]
            if (q, d) in self.known_dma[c]:
                return
            self.known_dma[c].add((q, d))
            self.eng[c].wait_ge(self.sem(('D' + q, d % NDSEM)), 16 * (d // NDSEM + 1))
        self.nwaits += 1

    def _dep(self, c, sig, kind):
        if sig[0] == 'c' and sig[1] == c:
            if c == 'pe' or (kind != 'raw' and c != 'pool'):
                return
        self._need(c, sig)

    def op(self, eng, fn, reads=(), writes=(), dma=False):
        for r in reads:
            w = self.last_w.get(r)
            if w is not None:
                self._dep(eng, w, 'raw')
            if r[0] == 'P' and r[1:].isdigit():
                for rd in self.readers.get(r, ()):
                    self._dep(eng, rd, 'war')
        for r in writes:
            w = self.last_w.get(r)
            if w is not None:
                self._dep(eng, w, 'waw')
            for rd in self.readers.get(r, ()):
                self._dep(eng, rd, 'war')
        if dma:
            d = self.ndma[eng]
            self.ndma[eng] += 1
            if d >= NDSEM:
                self._need(eng, ('d', eng, d - NDSEM))
            sig = ('d', eng, d)
            ins = fn(self.eng[eng])
            ins.then_inc(self.sem(('D' + eng, d % NDSEM)), 16)
        else:
            n = self.count[eng]
            self.count[eng] += 1
            sig = ('c', eng, n)
            ins = fn(self.eng[eng])
            ins.then_inc(self.sem(('S', eng, n // EPOCH)), 1)
        for r in reads:
            lst = self.readers.setdefault(r, [])
            if sig[0] == 'c':
                lst[:] = [x for x in lst if not (x[0] == 'c' and x[1] == sig[1])]
            lst.append(sig)
        for r in writes:
            self.last_w[r] = sig
            self.readers[r] = []
        return sig

    def barrier(self):
        for c in self.ENG:
            for e in self.ENG:
                if e == c or e == 'sp':
                    continue
                n = self.count[e] - 1
                if n >= 0:
                    self._need(c, ('c', e, n))
            for q in self.ndma:
                for k in range(max(0, self.ndma[q] - NDSEM), self.ndma[q]):
                    self._need(c, ('d', q, k))
        self.last_w.clear()
        self.readers.clear()

    def finish(self):
        for q in self.ndma:
            for k in range(max(0, self.ndma[q] - NDSEM), self.ndma[q]):
                self._need('sp', ('d', q, k))


def _regular(src0, nch, dst0):
    return [(src0 + 128 * i, 128, 1.0, dst0 + i, 0) for i in range(nch)]


def win_pieces():
    p = []
    p += _regular(0, 4, 0)
    p += _regular(512, 2, 4)
    p += [(768, 64, 1.0, 6, 0), (800, 32, -1.0, 6, 64), (768, 32, 1.0, 6, 96)]
    p += _regular(832, 16, 7)
    p += _regular(2880, 2, 23)
    p += _regular(3136, 2, 25)
    p += _regular(3392, 32, 27)
    return p, 59


def wq_pieces():
    p = []
    for h in range(NH):
        b = h * 192
        off = (h % 2) * 64
        p.append((b, 128, 1.0, h, 0))
        p.append((b + 128, 64, 1.0, 16 + h // 2, off))
        p.append((b + 160, 32, -1.0, 24 + h // 2, off))
        p.append((b + 128, 32, 1.0, 24 + h // 2, off + 32))
    return p, 32


def wkv_pieces():
    p = []
    for h in range(NH):
        p.append((h * 256, 128, 1.0, h, 0))
        p.append((h * 256 + 128, 128, 1.0, 16 + h, 0))
    return p, 32


class Builder:
    def __init__(self, S, KC=16, debug=False, stop_after=None):
        self.debug = debug
        self.stop_after = stop_after
        self.S = S
        self.NBLK = S // 128
        self.CB = self.NBLK // 4
        self.CH = S // 4
        self.NT = self.CB // 4
        self.NKT = self.NBLK // 4
        self.OWN0 = 3 * self.CB
        self.KC = min(KC, self.NBLK)
        self.nc = bass.Bass("TRN2", target_bir_lowering=False)
        self.es = ExitStack()

    def dram_in(self, name, shape):
        return self.nc.dram_tensor(name, list(shape), F32, kind="ExternalInput").ap()

    def dram_scr(self, name, shape, dt):
        return self.nc.dram_tensor(name, list(shape), dt, kind="Internal").ap()

    def sb(self, name, shape, dt, es=None):
        return (es or self.es).enter_context(self.nc.sbuf_tensor(name, list(shape), dt))

    def build(self):
        nc, S = self.nc, self.S
        with self.es:
            self.tr = Tracer(nc, self.es)
            I = {}
            I['xs'] = self.dram_in("xs", [S, D])
            for nm, shp in [("norm_mix_pre", [D]), ("norm_mix_post", [D]), ("norm_ffn_pre", [D]), ("norm_ffn_post", [D]),
                            ("w_in", [D, 7488]), ("mla_q_norm", [512]), ("mla_w_q_up", [512, 3072]),
                            ("mla_kv_norm", [256]), ("mla_w_kv_up", [256, 4096]), ("swa_sinks", [NSH]),
                            ("rel_bias_table", [32, NSH]), ("w_o_mla", [D, D]), ("w_o_swa", [D, D]), ("w_out", [D, D]),
                            ("ffn_w_up", [D, 2 * DFF]), ("ffn_conv_w", [3, DFF]), ("ffn_conv_b", [DFF]),
                            ("ffn_w_down", [DFF, D]),
                            ("cos128", [128, S]), ("sin128", [128, S]), ("kmask", [128, self.NBLK]),
                            ("onehot", [33, 2 * 128 * 128])]:
                I[nm] = self.dram_in(nm, shp)
            self.I = I
            self.out = nc.dram_tensor("out", [self.CH, D], F32, kind="ExternalOutput").ap()
            Wd = {}
            Wd['win'] = self.dram_scr("wb_in", [128, 59, JD, 128], BF16)
            Wd['wq'] = self.dram_scr("wb_q", [128, 32, 4, 128], BF16)
            Wd['wkv'] = self.dram_scr("wb_kv", [128, 32, 2, 128], BF16)
            Wd['womla'] = self.dram_scr("wb_omla", [128, 16, JD, 128], BF16)
            Wd['woswa'] = self.dram_scr("wb_oswa", [128, 16, JD, 128], BF16)
            Wd['wout'] = self.dram_scr("wb_out", [128, 16, JD, 128], BF16)
            Wd['wup'] = self.dram_scr("wb_up", [128, 88, JD, 128], BF16)
            Wd['wdown'] = self.dram_scr("wb_down", [128, 16, JF, 128], BF16)
            self.Wd = Wd
            self.KN = self.dram_scr("scr_kn", [NH, 128, S], BF16)
            self.KR = self.dram_scr("scr_kr", [64, S], BF16)
            self.VN = self.dram_scr("scr_vn", [NH, 128, self.NBLK, 129], BF16)
            self.KS = self.dram_scr("scr_ks", [256, S], BF16)
            self.VS = self.dram_scr("scr_vs", [128, self.NBLK, 4, 65], BF16)
            self.X1S = self.dram_scr("scr_x1", [self.CH, D], F32)
            self.BF = self.dram_scr("scr_bias", [NSH, 2 * 128 * 128], F32)
            self.ED = self.dram_scr("scr_E", [128, NSH, 256], BF16)
            self.RS = self.dram_scr("scr_rs", [1, 512], F32)
            self.PS = self.es.enter_context(nc.psum_tensor("psum_all", [128, 8 * 512], F32))
            self.PSB = self.PS[:].bitcast(BF16)
            self.consts()
            if self.stop_after != 'consts':
                self.phase_k()
                self.tr.barrier()
                if self.stop_after != 'k':
                    self.phase_q()
            self.tr.finish()
        return nc

    def bank(self, i, w=512):
        return self.PS[:, i * 512: i * 512 + w]

    def bankb(self, i, w=1024):
        return self.PSB[:, i * 1024: i * 1024 + w]

    def consts(self):
        nc, tr, I = self.nc, self.tr, self.I
        sb = self.sb
        self.identf = sb("identf", [128, 128], F32)
        self.ident = sb("ident", [128, 128], BF16)
        self.onesf = sb("onesf", [128, 128], F32)
        self.onesb = sb("onesb", [128, 128], BF16)
        self.epsc = sb("epsc", [128, 1], F32)
        self.onec = sb("onec", [128, 1], F32)
        self.negc = sb("negc", [128, 1], F32)
        self.kmask = sb("kmask_sb", [128, self.NBLK], F32)
        self.expsink = sb("expsink", [128, NSH], F32)
        self.cw = sb("convw", [128, 3, JF], F32)
        self.cb = sb("convb", [128, JF], F32)
        self.ahalo = sb("ahalo", [128, JF, 2], F32)
        self.gains = {}
        for nm, J in [("norm_mix_pre", JD), ("norm_mix_post", JD), ("norm_ffn_pre", JD), ("norm_ffn_post", JD),
                      ("mla_q_norm", 4), ("mla_kv_norm", 2)]:
            g = sb("g_" + nm, [128, J], F32)
            self.gains[nm] = g
            tr.op('sp', lambda e, g=g, nm=nm: e.dma_start(out=g[:], in_=I[nm].rearrange("(j p) -> p j", p=128), allow_slow_non_contiguous=True),
                  writes=['g_' + nm], dma=True)
        self.ngains = {}
        for nm, J in [("norm_mix_pre", JD), ("mla_q_norm", 4)]:
            g = sb("ng_" + nm, [128, J], F32)
            self.ngains[nm] = g
            tr.op('dve', lambda e, g=g, nm=nm: e.tensor_scalar(out=g[:], in0=self.gains[nm][:], scalar1=-1.0, scalar2=None,
                                                              op0=ALU.mult), reads=['g_' + nm], writes=['ng_' + nm])
        tr.op('pool', lambda e: e.memset(self.identf[:], 1.0), writes=['identf'])
        tr.op('pool', lambda e: e.affine_select(out=self.identf[:], in_=self.identf[:], pattern=[[1, 128]],
                                                compare_op=ALU.is_equal, fill=0.0, base=0, channel_multiplier=-1),
              reads=['identf'], writes=['identf'])
        tr.op('dve', lambda e: e.tensor_copy(out=self.ident[:], in_=self.identf[:]), reads=['identf'], writes=['ident'])
        tr.op('dve', lambda e: e.memset(self.onesf[:], 1.0), writes=['onesf'])
        tr.op('dve', lambda e: e.memset(self.onesb[:], 1.0), writes=['onesb'])
        tr.op('dve', lambda e: e.memset(self.epsc[:], EPS), writes=['epsc'])
        tr.op('dve', lambda e: e.memset(self.onec[:], 1.0), writes=['onec'])
        tr.op('dve', lambda e: e.memset(self.negc[:], -1.0), writes=['negc'])
        tr.op('dve', lambda e: e.memset(self.ahalo[:], 0.0), writes=['ahalo'])
        tr.op('sp', lambda e: e.dma_start(out=self.kmask[:], in_=I['kmask']), writes=['kmask'], dma=True)
        tr.op('sp', lambda e: e.dma_start(out=self.cw[:], in_=I['ffn_conv_w'].rearrange("w (j p) -> p w j", p=128), allow_slow_non_contiguous=True),
              writes=['cw'], dma=True)
        tr.op('sp', lambda e: e.dma_start(out=self.cb[:], in_=I['ffn_conv_b'].rearrange("(j p) -> p j", p=128), allow_slow_non_contiguous=True),
              writes=['cb'], dma=True)
        tr.op('sp', lambda e: e.dma_start(out=self.expsink[:], in_=I['swa_sinks'].partition_broadcast(128)),
              writes=['expsink'], dma=True)
        tr.op('act', lambda e: e.activation(out=self.expsink[:], in_=self.expsink[:], func=AF.Exp),
              reads=['expsink'], writes=['expsink'])
        with ExitStack() as les:
            tab = self.sb("tab_ext", [33, NSH], F32, les)
            oh = self.sb("oh_sb", [33, 4096], F32, les)
            bst = self.sb("bias_st", [NSH, 4096], F32, les)
            ebig = self.sb("ebig", [128, NSH, 256], F32, les)
            esb = self.sb("esb", [128, NSH, 256], BF16, les)
            tr.op('dve', lambda e: e.memset(tab[:], NEG), writes=['tab'])
            tr.op('sp', lambda e: e.dma_start(out=tab[0:32, :], in_=I['rel_bias_table']), writes=['tab'], dma=True)
            for cch in range(8):
                tr.op('sp', lambda e, cch=cch: e.dma_start(out=oh[:], in_=I['onehot'][:, cch * 4096:(cch + 1) * 4096]),
                      writes=['oh'], dma=True)
                for q in range(8):
                    bk = q % 4
                    tr.op('pe', lambda e, q=q, bk=bk: e.matmul(self.bank(bk)[0:32, :], lhsT=tab[:], rhs=oh[:, q * 512:(q + 1) * 512],
                                                               start=True, stop=True),
                          reads=['tab', 'oh'], writes=['P%d' % bk])
                    tr.op('act', lambda e, q=q, bk=bk: e.activation(out=bst[:, q * 512:(q + 1) * 512], in_=self.bank(bk)[0:32, :],
                                                                    func=AF.Copy),
                          reads=['P%d' % bk], writes=['bst'])
                tr.op('sp', lambda e, cch=cch: e.dma_start(out=self.BF[:, cch * 4096:(cch + 1) * 4096], in_=bst[:]),
                      reads=['bst'], writes=['BF'], dma=True)
            self.tr.barrier()
            tr.op('sp', lambda e: e.dma_start(out=ebig[:], in_=self.BF.rearrange("h (k x) -> k h x", k=128)),
                  writes=['ebig'], dma=True)
            tr.op('act', lambda e: e.activation(out=esb[:], in_=ebig[:], func=AF.Exp), reads=['ebig'], writes=['esb'])
            tr.op('sp', lambda e: e.dma_start(out=self.ED, in_=esb[:]), reads=['esb'], writes=['ED'], dma=True)
            self.tr.barrier()

    def phase_w_items(self, les):
        tr, I = self.tr, self.I
        stage = [self.sb("wst%d" % i, [128, JD, 256], F32, les) for i in range(2)]
        ost = [self.sb("wos%d" % i, [128, 2, JD, 128], BF16, les) for i in range(2)]
        self._wcnt = 0
        self._ccnt = 0

        def cast(out_ap, in_ap, gain_ap, rkeys, wkeys):
            k = self._ccnt % 3
            self._ccnt += 1
            if k == 0:
                tr.op('act', lambda e: e.activation(out=out_ap, in_=in_ap, func=AF.Copy, scale=gain_ap),
                      reads=rkeys, writes=wkeys)
            elif k == 1:
                tr.op('dve', lambda e: e.tensor_scalar(out=out_ap, in0=in_ap, scalar1=gain_ap, scalar2=None, op0=ALU.mult),
                      reads=rkeys, writes=wkeys)
            else:
                tr.op('pool', lambda e: e.tensor_scalar(out=out_ap, in0=in_ap, scalar1=gain_ap, scalar2=0.0,
                                                        op0=ALU.mult, op1=ALU.add),
                      reads=rkeys, writes=wkeys)

        def gcol(gname, j, sign):
            if gname is None:
                return (self.onec if sign > 0 else self.negc)[:, 0:1]
            return (self.gains[gname] if sign > 0 else self.ngains[gname])[:, j:j + 1]

        def piece_item(src, dst, J, gname, sc, ncol, sign, dn, doff, j0=0):
            def emit():
                b = self._wcnt % 2
                self._wcnt += 1
                st, os_ = stage[b], ost[b]
                tr.op('sp', lambda e: e.dma_start(
                    out=st[:, 0:J, 0:ncol], in_=src[j0 * 128:(j0 + J) * 128, sc:sc + ncol].rearrange("(j p) c -> p j c", p=128)),
                    writes=['wst%d' % b], dma=True)
                for j in range(J):
                    cast(os_[:, 0, j, 0:ncol], st[:, j, 0:ncol], gcol(gname, j0 + j, sign), ['wst%d' % b, 'gains'], ['wos%d' % b])
                tr.op('pool', lambda e: e.dma_start(out=dst[:, dn, j0:j0 + J, doff:doff + ncol], in_=os_[:, 0, 0:J, 0:ncol]),
                      reads=['wos%d' % b], writes=['wscr'], dma=True)
            return emit

        def pair_item(src, dst, J, gname, sc, dn, j0):
            def emit():
                b = self._wcnt % 2
                self._wcnt += 1
                st, os_ = stage[b], ost[b]
                tr.op('sp', lambda e: e.dma_start(
                    out=st[:, 0:J, :], in_=src[j0 * 128:(j0 + J) * 128, sc:sc + 256].rearrange("(j p) c -> p j c", p=128)),
                    writes=['wst%d' % b], dma=True)
                for j in range(J):
                    cast(os_[:, :, j, :], st[:, j, :].rearrange("p (g c) -> p g c", g=2), gcol(gname, j0 + j, 1.0),
                         ['wst%d' % b, 'gains'], ['wos%d' % b])
                tr.op('pool', lambda e: e.dma_start(out=dst[:, dn:dn + 2, j0:j0 + J, :], in_=os_[:, :, 0:J, :]),
                      reads=['wos%d' % b], writes=['wscr'], dma=True)
            return emit

        def regular(src, dst, J, gname, src0, nch, dst0):
            its = []
            assert nch % 2 == 0
            for c in range(0, nch, 2):
                for j0 in range(0, J, JD):
                    its.append(pair_item(src, dst, min(JD, J - j0), gname, src0 + 128 * c, dst0 + c, j0))
            return its

        def pieces(src, dst, J, gname, plist):
            return [piece_item(src, dst, J, gname, sc, ncol, sign, dn, doff) for (sc, ncol, sign, dn, doff) in plist]

        tr.op('dve', lambda e: e.memset(self.epsc[:], EPS),
              reads=['g_norm_mix_pre', 'g_norm_ffn_pre', 'g_mla_q_norm', 'g_mla_kv_norm', 'ng_norm_mix_pre', 'ng_mla_q_norm',
                     'onec', 'negc'], writes=['gains', 'epsc'])
        Wd = self.Wd
        pl, _ = win_pieces()
        spec = [p for p in pl if p[1] != 128 or p[4] != 0]
        pre = []
        pre += pieces(I['w_in'], Wd['win'], JD, "norm_mix_pre", spec)
        pre += regular(I['w_in'], Wd['win'], JD, "norm_mix_pre", 512, 2, 4)
        pre += regular(I['w_in'], Wd['win'], JD, "norm_mix_pre", 2880, 2, 23)
        pre += regular(I['w_in'], Wd['win'], JD, "norm_mix_pre", 3136, 2, 25)
        pl, _ = wkv_pieces()
        pre += pieces(I['mla_w_kv_up'], Wd['wkv'], 2, "mla_kv_norm", pl)
        rest = []
        rest += regular(I['w_in'], Wd['win'], JD, "norm_mix_pre", 0, 4, 0)
        rest += regular(I['w_in'], Wd['win'], JD, "norm_mix_pre", 832, 16, 7)
        rest += regular(I['w_in'], Wd['win'], JD, "norm_mix_pre", 3392, 32, 27)
        pl, _ = wq_pieces()
        rest += pieces(I['mla_w_q_up'], Wd['wq'], 4, "mla_q_norm", pl)
        rest += regular(I['w_o_mla'], Wd['womla'], JD, None, 0, 16, 0)
        rest += regular(I['w_o_swa'], Wd['woswa'], JD, None, 0, 16, 0)
        rest += regular(I['w_out'], Wd['wout'], JD, None, 0, 16, 0)
        rest += regular(I['ffn_w_up'], Wd['wup'], JD, "norm_ffn_pre", 0, 88, 0)
        rest += regular(I['ffn_w_down'], Wd['wdown'], JF, None, 0, 16, 0)
        return pre, rest

    def norm_T(self, bufs, blk, T0, dstT, dkey, src_dma=None, pre=None):
        tr = self.tr
        b = blk % 2
        xin = bufs['xin'][b]
        xk = 'xin%d' % b
        if src_dma is not None:
            tr.op('sp', lambda e: e.dma_start(out=xin[:], in_=src_dma), writes=[xk], dma=True)
        if pre is not None:
            pre(xin, xk)
        ss, rs, rstd, xsb = bufs['ss'], bufs['rs'], bufs['rstd'], bufs['xsb'][0]
        tr.op('act', lambda e: e.activation(out=xsb[:], in_=xin[:], func=AF.Square), reads=[xk], writes=['xsb0'])
        tr.op('dve', lambda e: e.tensor_reduce(out=ss[:], in_=xsb[:], axis=AX.X, op=ALU.add), reads=['xsb0'], writes=['ss'])
        tr.op('act', lambda e: e.activation(out=rs[:], in_=ss[:], func=AF.Sqrt, scale=1.0 / D, bias=self.epsc[:]),
              reads=['ss'], writes=['rs'])
        tr.op('dve', lambda e: e.reciprocal(out=rstd[:], in_=rs[:]), reads=['rs'], writes=['rstd'])
        if blk % 2 == 0:
            tr.op('dve', lambda e: e.tensor_scalar(out=xsb[:], in0=xin[:], scalar1=rstd[:], scalar2=None, op0=ALU.mult),
                  reads=[xk, 'rstd'], writes=['xsb0'])
        else:
            tr.op('act', lambda e: e.activation(out=xsb[:], in_=xin[:], func=AF.Copy, scale=rstd[:]),
                  reads=[xk, 'rstd'], writes=['xsb0'])
        for jg in range(4):
            bk = 6 + (jg % 2)
            for jj in range(4):
                j = jg * 4 + jj
                tr.op('pe', lambda e, j=j, jj=jj, bk=bk: e.transpose(out=self.bankb(bk)[:, jj * 128:(jj + 1) * 128],
                                                                    in_=xsb[:, j * 128:(j + 1) * 128], identity=self.ident[:]),
                      reads=['xsb0', 'ident'], writes=['P%d' % bk])
            src = self.bankb(bk, 512).rearrange("p (a c) -> p a c", a=4)
            dst = dstT[:, jg * 4:(jg + 1) * 4, T0:T0 + 128]
            wk = [dkey + str(jg * 4 + jj) for jj in range(4)]
            if jg % 2 == 0:
                tr.op('act', lambda e, src=src, dst=dst: e.activation(out=dst, in_=src, func=AF.Copy),
                      reads=['P%d' % bk], writes=wk)
            else:
                tr.op('dve', lambda e, src=src, dst=dst: e.tensor_copy(out=dst, in_=src), reads=['P%d' % bk], writes=wk)

    def tokbufs(self, les, pfx):
        return dict(xin=[self.sb(pfx + "xin%d" % i, [128, D], F32, les) for i in range(2)],
                    xsb=[self.sb(pfx + "xsb%d" % i, [128, D], BF16, les) for i in range(1)],
                    ss=self.sb(pfx + "ss", [128, 1], F32, les),
                    rs=self.sb(pfx + "rs", [128, 1], F32, les), rstd=self.sb(pfx + "rstd", [128, 1], F32, les))

    def evac(self, idx, out_ap, in_ap, reads, writes):
        if idx % 2 == 0:
            self.tr.op('act', lambda e: e.activation(out=out_ap, in_=in_ap, func=AF.Copy), reads=reads, writes=writes)
        else:
            self.tr.op('dve', lambda e: e.tensor_copy(out=out_ap, in_=in_ap), reads=reads, writes=writes)

    def featnorm(self, pbanks, nj, dim, T, bufs, dst, dkey):
        tr = self.tr
        sqc, rsb, rstdb = bufs['sqc'], bufs['rsb'], bufs['rstdb']
        for j in range(nj):
            tr.op('act', lambda e, j=j: e.activation(out=sqc[j % 2][:, 0:T], in_=self.bank(pbanks[j], T), func=AF.Square),
                  reads=['P%d' % pbanks[j]], writes=['sqc%d' % (j % 2)])
            tr.op('pe', lambda e, j=j: e.matmul(self.bank(5, T), lhsT=self.onesb[:], rhs=sqc[j % 2][:, 0:T],
                                                start=(j == 0), stop=(j == nj - 1)),
                  reads=['sqc%d' % (j % 2), 'onesb'], writes=['P5'])
        tr.op('act', lambda e: e.activation(out=rsb[:, 0:T], in_=self.bank(5, T), func=AF.Sqrt, scale=1.0 / dim, bias=self.epsc[:]),
              reads=['P5'], writes=['rsb'])
        tr.op('dve', lambda e: e.reciprocal(out=rstdb[:, 0:T], in_=rsb[:, 0:T]), reads=['rsb'], writes=['rstdb'])
        for j in range(nj):
            tr.op('dve', lambda e, j=j: e.tensor_tensor(out=dst[:, j, 0:T], in0=self.bank(pbanks[j], T), in1=rstdb[:, 0:T],
                                                        op=ALU.mult),
                  reads=['P%d' % pbanks[j], 'rstdb'], writes=[dkey + str(j)])

    def phase_k(self):
        tr, I, Wd = self.tr, self.I, self.Wd
        with ExitStack() as les:
            sb = lambda n, s, d: self.sb(n, s, d, les)
            bufs = self.tokbufs(les, "k_")
            hT2 = [sb("k_hT%d" % i, [128, JD, 512], BF16) for i in range(2)]
            wk = sb("k_win", [128, 7, JD, 128], BF16)
            wkv = sb("k_wkv", [128, 32, 2, 128], BF16)
            fb = dict(sqc=[sb("k_sqc%d" % i, [128, 512], BF16) for i in range(2)], rsb=sb("k_rsb", [128, 512], F32),
                      rstdb=sb("k_rstdb", [128, 512], F32))
            ckvn = sb("k_ckvn", [128, 2, 512], BF16)
            cs = sb("k_cos", [64, 512], F32)
            sn = sb("k_sin", [64, 512], F32)
            t1 = sb("k_t1", [64, 512], F32)
            t2 = sb("k_t2", [64, 512], F32)
            kro = sb("k_kro", [64, 512], BF16)
            kst = sb("k_kst", [128, 2, 512], BF16)
            vss = sb("k_vss", [128, 4, 4, 65], BF16)
            knst = sb("k_knst", [128, NH, 512], BF16)
            vnst = sb("k_vnst", [128, NH, 4, 129], BF16)
            tr.op('dve', lambda e: e.memset(vss[:], 1.0), writes=['vss'])
            tr.op('pool', lambda e: e.memset(vnst[:], 1.0), writes=['vnst'])
            wpre, wrest = self.phase_w_items(les)
            for it in wpre:
                it()
            self.tr.barrier()
            wpos = 0
            wper = (len(wrest) + self.NKT - 1) // self.NKT
            tr.op('sp', lambda e: e.dma_start(out=wk[:, 0:3], in_=Wd['win'][:, 4:7]), writes=['wk'], dma=True)
            tr.op('sp', lambda e: e.dma_start(out=wk[:, 3:7], in_=Wd['win'][:, 23:27]), writes=['wk'], dma=True)
            tr.op('sp', lambda e: e.dma_start(out=wkv[:], in_=Wd['wkv']), writes=['wkv'], dma=True)
            self._kev = 0

            def proj_gen(kt):
                ev = self._kev
                s0 = kt * 512
                hT = hT2[kt % 2]
                hpf = 'hT%s' % ('a' if kt % 2 == 0 else 'b')
                hk = [hpf + str(j) for j in range(JD)]
                tr.op('sp', lambda e, s0=s0: e.dma_start(out=cs[:], in_=I['cos128'][0:64, s0:s0 + 512]), writes=['cs'], dma=True)
                tr.op('sp', lambda e, s0=s0: e.dma_start(out=sn[:], in_=I['sin128'][0:64, s0:s0 + 512]), writes=['sn'], dma=True)
                for c in range(2):
                    for j in range(JD):
                        tr.op('pe', lambda e, c=c, j=j: e.matmul(self.bank(c), lhsT=wk[:, c, j, :], rhs=hT[:, j, :],
                                                                 start=(j == 0), stop=(j == JD - 1)),
                              reads=['wk', hk[j]], writes=['P%d' % c])
                self.featnorm([0, 1], 2, 256.0, 512, fb, ckvn, 'ckvn')
                yield
                for c in range(2):
                    for j in range(JD):
                        tr.op('pe', lambda e, c=c, j=j: e.matmul(self.bank(2 + c)[0:64, :], lhsT=wk[:, 2, j, c * 64:(c + 1) * 64],
                                                                 rhs=hT[:, j, :], start=(j == 0), stop=(j == JD - 1)),
                              reads=['wk', hk[j]], writes=['P%d' % (2 + c)])
                tr.op('dve', lambda e: e.tensor_tensor(out=t1[:], in0=self.bank(2)[0:64, :], in1=cs[:], op=ALU.mult),
                      reads=['P2', 'cs'], writes=['t1'])
                tr.op('dve', lambda e: e.tensor_tensor(out=t2[:], in0=self.bank(3)[0:64, :], in1=sn[:], op=ALU.mult),
                      reads=['P3', 'sn'], writes=['t2'])
                tr.op('pool', lambda e: e.tensor_tensor(out=kro[:], in0=t1[:], in1=t2[:], op=ALU.add), reads=['t1', 't2'], writes=['kro'])
                tr.op('pool', lambda e, s0=s0: e.dma_start(out=self.KR[:, s0:s0 + 512], in_=kro[:]), reads=['kro'],
                      writes=['KR%d' % kt], dma=True)
                for c in range(2):
                    for j in range(JD):
                        tr.op('pe', lambda e, c=c, j=j: e.matmul(self.bank(2 + c), lhsT=wk[:, 3 + c, j, :], rhs=hT[:, j, :],
                                                                 start=(j == 0), stop=(j == JD - 1)),
                              reads=['wk', hk[j]], writes=['P%d' % (2 + c)])
                    self.evac(ev, kst[:, c, :], self.bank(2 + c), ['P%d' % (2 + c)], ['kst'])
                    ev += 1
                tr.op('pool', lambda e, s0=s0: e.dma_start(out=self.KS[:, s0:s0 + 512].rearrange("(c p) s -> p c s", p=128), in_=kst[:]),
                      reads=['kst'], writes=['KS%d' % kt], dma=True)
                for blk in range(4):
                    bk = 4 if blk % 2 == 0 else 2
                    for j in range(JD):
                        tr.op('pe', lambda e, blk=blk, j=j, bk=bk: e.matmul(
                            self.bank(bk, 256).rearrange("p (a c) -> p a c", a=2), lhsT=hT[:, j, blk * 128:(blk + 1) * 128],
                            rhs=wk[:, 5:7, j, :], start=(j == 0), stop=(j == JD - 1)),
                            reads=['wk', hk[j]], writes=['P%d' % bk])
                    self.evac(ev, vss[:, blk, :, 0:64], self.bank(bk, 256).rearrange("p (g c) -> p g c", g=4), ['P%d' % bk], ['vss'])
                    ev += 1
                tr.op('pool', lambda e, kt=kt: e.dma_start(out=self.VS[:, 4 * kt:4 * kt + 4], in_=vss[:]), reads=['vss'],
                      writes=['VS%d' % kt], dma=True)
                yield
                cj = ['ckvn0', 'ckvn1']
                for h in range(NH):
                    bk = 2 + (h % 4)
                    for j in range(2):
                        tr.op('pe', lambda e, h=h, j=j, bk=bk: e.matmul(self.bank(bk), lhsT=wkv[:, h, j, :], rhs=ckvn[:, j, :],
                                                                        start=(j == 0), stop=(j == 1)),
                              reads=['wkv', cj[j]], writes=['P%d' % bk])
                    self.evac(ev, knst[:, h, :], self.bank(bk), ['P%d' % bk], ['knst'])
                    ev += 1
                tr.op('pool', lambda e, s0=s0: e.dma_start(out=self.KN[:, :, s0:s0 + 512].rearrange("h d s -> d h s"), in_=knst[:]),
                      reads=['knst'], writes=['KN%d' % kt], dma=True)
                yield
                i = 0
                for blk in range(4):
                    for hg in range(4):
                        bk = 2 + (i % 4)
                        i += 1
                        for j in range(2):
                            tr.op('pe', lambda e, blk=blk, hg=hg, j=j, bk=bk: e.matmul(
                                self.bank(bk).rearrange("p (a c) -> p a c", a=4), lhsT=ckvn[:, j, blk * 128:(blk + 1) * 128],
                                rhs=wkv[:, 16 + 4 * hg:20 + 4 * hg, j, :], start=(j == 0), stop=(j == 1)),
                                reads=['wkv', cj[j]], writes=['P%d' % bk])
                        self.evac(ev, vnst[:, 4 * hg:4 * hg + 4, blk, 0:128], self.bank(bk).rearrange("p (a c) -> p a c", a=4),
                                  ['P%d' % bk], ['vnst'])
                        ev += 1
                tr.op('pool', lambda e, kt=kt: e.dma_start(out=self.VN[:, :, 4 * kt:4 * kt + 4, :].rearrange("h k b c -> k h b c"),
                                                          in_=vnst[:]),
                      reads=['vnst'], writes=['VN%d' % kt], dma=True)
                self._kev = ev

            for kt in range(self.NKT + 1):
                g = proj_gen(kt - 1) if kt >= 1 else None
                for blk in range(4):
                    if kt < self.NKT:
                        self.norm_T(bufs, blk, blk * 128, hT2[kt % 2], 'hT%s' % ('a' if kt % 2 == 0 else 'b'),
                                    src_dma=I['xs'][kt * 512 + blk * 128: kt * 512 + (blk + 1) * 128, :])
                    if g is not None:
                        next(g, None)
                for it in wrest[wpos:wpos + wper]:
                    it()
                wpos += wper
            for it in wrest[wpos:]:
                it()
            self.tr.barrier()

    def _ffn_chunk(self, tr, cur, sub, wn, T, halo, hT, hk, aext, sg, gT):
        isb = wn >= JF
        jc = wn % JF
        bk = (0 + 2 * (jc % 2)) + (1 if isb else 0)
        wv, wkey = cur[0][:, sub], cur[1]
        for j in range(JD):
            tr.op('pe', lambda e, j=j, wv=wv, bk=bk: e.matmul(self.bank(bk, T), lhsT=wv[:, j, :], rhs=hT[:, j, 0:T],
                                                              start=(j == 0), stop=(j == JD - 1)),
                  reads=[wkey, hk[j]], writes=['P%d' % bk])
        if not isb:
            ax = aext[jc % 2]
            ak = 'aext%d' % (jc % 2)
            if not halo:
                tr.op('pool', lambda e, ax=ax, jc=jc: e.tensor_copy(out=ax[:, 0:2], in_=self.ahalo[:, jc, :]),
                      reads=['ahalo%d' % jc], writes=[ak])
                tr.op('act', lambda e, ax=ax, bk=bk: e.activation(out=ax[:, 2:2 + T], in_=self.bank(bk, T), func=AF.Copy),
                      reads=['P%d' % bk], writes=[ak])
            tr.op('act', lambda e, jc=jc, bk=bk: e.activation(out=self.ahalo[:, jc, :], in_=self.bank(bk)[:, T - 2:T], func=AF.Copy),
                  reads=['P%d' % bk], writes=['ahalo%d' % jc])
        else:
            ax = aext[jc % 2]
            ak = 'aext%d' % (jc % 2)
            tt = sg[jc % 2]
            tk = 'sg%d' % (jc % 2)
            tr.op('dve', lambda e, ax=ax, tt=tt, jc=jc: e.tensor_scalar(out=tt[:, 0:T], in0=ax[:, 2:2 + T], scalar1=self.cw[:, 2, jc:jc + 1],
                                                                        scalar2=self.cb[:, jc:jc + 1], op0=ALU.mult, op1=ALU.add),
                  reads=[ak, 'cw', 'cb'], writes=[tk])
            tr.op('dve', lambda e, ax=ax, tt=tt, jc=jc: e.scalar_tensor_tensor(out=tt[:, 0:T], in0=ax[:, 1:1 + T], scalar=self.cw[:, 1, jc:jc + 1],
                                                                               in1=tt[:, 0:T], op0=ALU.mult, op1=ALU.add),
                  reads=[ak, tk, 'cw'], writes=[tk])
            tr.op('dve', lambda e, ax=ax, tt=tt, jc=jc: e.scalar_tensor_tensor(out=tt[:, 0:T], in0=ax[:, 0:T], scalar=self.cw[:, 0, jc:jc + 1],
                                                                               in1=tt[:, 0:T], op0=ALU.mult, op1=ALU.add),
                  reads=[ak, tk, 'cw'], writes=[tk])
            tr.op('act', lambda e, tt=tt: e.activation(out=tt[:, 0:T], in_=tt[:, 0:T], func=AF.Gelu_apprx_tanh),
                  reads=[tk], writes=[tk])
            tr.op('dve', lambda e, tt=tt, jc=jc, bk=bk: e.tensor_tensor(out=gT[:, jc, 0:T], in0=self.bank(bk, T), in1=tt[:, 0:T], op=ALU.mult),
                  reads=['P%d' % bk, tk], writes=['gT%d' % jc])


    def phase_q(self):
        tr, I, Wd = self.tr, self.I, self.Wd
        KC = self.KC
        with ExitStack() as les:
            sb = lambda n, s, d: self.sb(n, s, d, les)
            bufs = self.tokbufs(les, "q_")
            hT = sb("q_hT", [128, JD, 512], BF16)
            R = sb("q_R", [128, 32768], BF16)
            qnT = R[:, 0:8192].rearrange("p (a t) -> p a t", a=16)
            qsT = R[:, 8192:16384].rearrange("p (a t) -> p a t", a=16)
            oaT = qsT
            obT = R[:, 16384:24576].rearrange("p (a t) -> p a t", a=16)
            qrT = R[:, 24576:32768].rearrange("p (a t) -> p a t", a=16)
            tr.op('pool', lambda e: e.memset(qrT, 0.0), writes=['qr%d' % i for i in range(16)])
            gT = R[:, 0:JF * 512].rearrange("p (a t) -> p a t", a=JF)
            mT = qnT
            A = sb("q_A", [128, 3 * (2 * KC * 128 + KC * 129)], BF16)
            asz = 2 * KC * 128 + KC * 129
            ygT = A[:, 0:8192].rearrange("p (a t) -> p a t", a=16) if 3 * asz >= 8192 else sb("q_ygT", [128, JD, 512], BF16)
            knc = [A[:, i * asz: i * asz + KC * 128] for i in range(3)]
            krc = [A[:, i * asz + KC * 128: i * asz + 2 * KC * 128] for i in range(3)]
            vac = [A[:, i * asz + 2 * KC * 128: (i + 1) * asz].rearrange("p (b c) -> p b c", c=129) for i in range(3)]
            wr = [sb("q_wr%d" % i, [128, 5632], BF16) for i in range(2)]
            fb = dict(sqc=[sb("q_sqc%d" % i, [128, 512], BF16) for i in range(2)], rsb=sb("q_rsb", [128, 512], F32),
                      rstdb=sb("q_rstdb", [128, 512], F32))
            cqn = sb("q_cqn", [128, 4, 512], BF16)
            cs = sb("q_cos", [128, 512], F32)
            sn = sb("q_sin", [128, 512], F32)
            pT = [sb("q_pT%d" % i, [128, 512], BF16) for i in range(3)]
            otok = sb("q_otok", [128, 4, 128], BF16)
            dd = sb("q_dd", [128, 4], F32)
            rden = sb("q_rden", [128, 4], F32)
            ksT2 = sb("q_ksT2", [128, 2, 4, 640], BF16)
            tr.op('pool', lambda e: e.memset(ksT2[:], 0.0), writes=['ksT2'])
            vsA = sb("q_vsA", [128, 5, 4, 65], BF16)
            pTs = [sb("q_pTs%d" % i, [128, 1024], BF16) for i in range(2)]
            obtok = [sb("q_obtok%d" % i, [128, 4, 128], BF16) for i in range(2)]
            eh = [sb("q_eh%d" % i, [128, 2, 128], BF16) for i in range(2)]
            sg = [sb("q_sg%d" % i, [128, 512], F32) for i in range(2)]
            t1, t2 = sg
            ysq = fb['sqc']
            rtok2 = sb("q_rtok2", [128, 8], F32)
            aext = [sb("q_aext%d" % i, [128, 516], F32) for i in range(2)]
            self._wl = 0
            self._ev = 0

            def wload(wname, n0, g, J):
                b = self._wl % 2
                self._wl += 1
                v = wr[b][:, 0:g * J * 128].rearrange("p (g j c) -> p g j c", g=g, j=J)
                tr.op('sp', lambda e: e.dma_start(out=v, in_=Wd[wname][:, n0:n0 + g]), writes=['wr%d' % b], dma=True)
                return v, 'wr%d' % b

            def proj_groups(wname, chunks, J, gmax):
                groups = []
                i = 0
                while i < len(chunks):
                    g = 1
                    while g < gmax and i + g < len(chunks) and chunks[i + g] == chunks[i] + g:
                        g += 1
                    groups.append((chunks[i], g))
                    i += g
                pend = wload(wname, groups[0][0], groups[0][1], J)
                for gi, (n0, g) in enumerate(groups):
                    cur = pend
                    pend = wload(wname, groups[gi + 1][0], groups[gi + 1][1], J) if gi + 1 < len(groups) else None
                    for k in range(g):
                        yield n0 + k, cur[0][:, k], cur[1]

            def post_norm(T, nb, gname, res_src, res_dst_fn, last):
                rsb, rstdb = fb['rsb'], fb['rstdb']
                tr.op('act', lambda e: e.activation(out=rsb[:, 0:T], in_=self.bank(5, T), func=AF.Sqrt, scale=1.0 / D, bias=self.epsc[:]),
                      reads=['P5'], writes=['rsb'])
                tr.op('dve', lambda e: e.reciprocal(out=rstdb[:, 0:T], in_=rsb[:, 0:T]), reads=['rsb'], writes=['rstdb'])
                if self.stop_after == 'q8b':
                    return
                tr.op('sp', lambda e: e.dma_start(out=self.RS[0:1, 0:T], in_=rstdb[0:1, 0:T]), reads=['rstdb'], writes=['RS'], dma=True)
                tr.op('sp', lambda e: e.dma_start(out=rtok2[:, 0:nb], in_=self.RS[0, 0:T].rearrange("(b p) -> p b", p=128),
                                                  allow_slow_non_contiguous=True), reads=['RS'], writes=['rtok2'], dma=True)
                if self.stop_after == 'q8c':
                    return
                for blk in range(nb):
                    b = blk % 2
                    xin = bufs['xin'][b]
                    xk = 'xin%d' % b
                    tr.op('sp', lambda e, blk=blk, xin=xin: e.dma_start(out=xin[:], in_=res_src(blk)), writes=[xk], dma=True)
                    for mg in range(4):
                        bk = 6 + (mg % 2)
                        for mm in range(4):
                            m = mg * 4 + mm
                            tr.op('pe', lambda e, m=m, mm=mm, bk=bk, blk=blk: e.transpose(
                                out=self.bankb(bk)[:, mm * 128:(mm + 1) * 128], in_=ygT[:, m, blk * 128:(blk + 1) * 128],
                                identity=self.ident[:]), reads=['yg%d' % m, 'ident'], writes=['P%d' % bk])
                        tr.op('dve', lambda e, mg=mg, bk=bk, blk=blk, xin=xin: e.scalar_tensor_tensor(
                            out=xin[:, mg * 512:(mg + 1) * 512], in0=self.bankb(bk, 512), scalar=rtok2[:, blk:blk + 1],
                            in1=xin[:, mg * 512:(mg + 1) * 512], op0=ALU.mult, op1=ALU.add),
                            reads=['P%d' % bk, 'rtok2', xk], writes=[xk])
                    if self.stop_after == 'q8d':
                        continue
                    res_dst_fn(blk, xin, xk)

            def y_chunk(m, ps_bank, T, nb, gname):
                pk = 'P%d' % ps_bank
                tr.op('act', lambda e: e.activation(out=ysq[m % 2][:, 0:T], in_=self.bank(ps_bank, T), func=AF.Square),
                      reads=[pk], writes=['sqc%d' % (m % 2)])

                tr.op('dve', lambda e: e.tensor_scalar(out=ygT[:, m, 0:T], in0=self.bank(ps_bank, T),
                                                       scalar1=self.gains[gname][:, m:m + 1], scalar2=None, op0=ALU.mult),
                      reads=[pk, 'g_' + gname], writes=['yg%d' % m])
                tr.op('pe', lambda e: e.matmul(self.bank(5, T), lhsT=self.onesb[:], rhs=ysq[m % 2][:, 0:T],
                                               start=(m == 0), stop=(m == JD - 1)),
                      reads=['sqc%d' % (m % 2), 'onesb'], writes=['P5'])

            tiles = [(self.OWN0 - 1, 1, True)] + [(self.OWN0 + 4 * i, 4, False) for i in range(self.NT)]
            for (B0, nb, halo) in tiles:
                T = nb * 128
                s0 = B0 * 128
                r0 = s0 - self.OWN0 * 128
                hk = ['hT%d' % j for j in range(JD)]
                for blk in range(nb):
                    self.norm_T(bufs, blk, blk * 128, hT, 'hT', src_dma=I['xs'][s0 + blk * 128: s0 + (blk + 1) * 128, :])
                tr.op('sp', lambda e, s0=s0, T=T: e.dma_start(out=cs[:, 0:T], in_=I['cos128'][:, s0:s0 + T]), writes=['cs'], dma=True)
                tr.op('sp', lambda e, s0=s0, T=T: e.dma_start(out=sn[:, 0:T], in_=I['sin128'][:, s0:s0 + T]), writes=['sn'], dma=True)
                for n, wv, wkey in proj_groups('win', [0, 1, 2, 3], JD, 2):
                    for j in range(JD):
                        tr.op('pe', lambda e, n=n, j=j, wv=wv: e.matmul(self.bank(n, T), lhsT=wv[:, j, :], rhs=hT[:, j, 0:T],
                                                                        start=(j == 0), stop=(j == JD - 1)),
                              reads=[wkey, hk[j]], writes=['P%d' % n])
                self.featnorm([0, 1, 2, 3], 4, 512.0, T, fb, cqn, 'cqn')
                cqk = ['cqn%d' % j for j in range(4)]
                i = 0
                for n, wv, wkey in proj_groups('wq', list(range(16)), 4, 8):
                    bk = i % 4
                    i += 1
                    for j in range(4):
                        tr.op('pe', lambda e, j=j, wv=wv, bk=bk: e.matmul(self.bank(bk, T), lhsT=wv[:, j, :], rhs=cqn[:, j, 0:T],
                                                                          start=(j == 0), stop=(j == 3)),
                              reads=[wkey, cqk[j]], writes=['P%d' % bk])
                    self.evac(self._ev, qnT[:, n, 0:T], self.bank(bk, T), ['P%d' % bk], ['qn%d' % n])
                    self._ev += 1
                order = []
                for hp in range(8):
                    order += [16 + hp, 24 + hp]
                for n, wv, wkey in proj_groups('wq', order, 4, 1):
                    hp = (n - 16) % 8
                    rot = n >= 24
                    bk = 1 if rot else 0
                    for j in range(4):
                        tr.op('pe', lambda e, j=j, wv=wv, bk=bk: e.matmul(self.bank(bk, T), lhsT=wv[:, j, :], rhs=cqn[:, j, 0:T],
                                                                          start=(j == 0), stop=(j == 3)),
                              reads=[wkey, cqk[j]], writes=['P%d' % bk])
                    if not rot:
                        tr.op('dve', lambda e: e.tensor_tensor(out=t1[:, 0:T], in0=self.bank(0, T), in1=cs[:, 0:T], op=ALU.mult),
                              reads=['P0', 'cs'], writes=['sg0'])
                    else:
                        tr.op('dve', lambda e: e.tensor_tensor(out=t2[:, 0:T], in0=self.bank(1, T), in1=sn[:, 0:T], op=ALU.mult),
                              reads=['P1', 'sn'], writes=['sg1'])
                        for hh in range(2):
                            tr.op('pool', lambda e, hp=hp, hh=hh: e.tensor_tensor(
                                out=qrT[hh * 64:(hh + 1) * 64, 2 * hp + hh, 0:T], in0=t1[hh * 64:(hh + 1) * 64, 0:T],
                                in1=t2[hh * 64:(hh + 1) * 64, 0:T], op=ALU.add),
                                reads=['sg0', 'sg1'], writes=['qr%d' % (2 * hp + hh)])
                i = 0
                for n, wv, wkey in proj_groups('win', list(range(7, 23)), JD, 2):
                    bk = i % 4
                    i += 1
                    for j in range(JD):
                        tr.op('pe', lambda e, j=j, wv=wv, bk=bk: e.matmul(self.bank(bk, T), lhsT=wv[:, j, :], rhs=hT[:, j, 0:T],
                                                                          start=(j == 0), stop=(j == JD - 1)),
                              reads=[wkey, hk[j]], writes=['P%d' % bk])
                    self.evac(self._ev, qsT[:, n - 7, 0:T], self.bank(bk, T), ['P%d' % bk], ['qs%d' % (n - 7)])
                    self._ev += 1
                if self.stop_after == 'q4':
                    self.tr.barrier()
                    return
                nkb = nb + 1
                for half in range(2):
                    tr.op('sp', lambda e, half=half: e.dma_start(
                        out=ksT2[half * 64:(half + 1) * 64, half, :, 0:nkb * 128],
                        in_=self.KS[:, (B0 - 1) * 128:(B0 + nb) * 128].rearrange("(g d) s -> d g s", d=64)),
                        reads=['KS%d' % k for k in range((B0 - 1) // 4, (B0 + nb - 1) // 4 + 1)], writes=['ksT2'], dma=True)
                tr.op('sp', lambda e: e.dma_start(out=vsA[:, 0:nkb], in_=self.VS[:, B0 - 1:B0 + nb]),
                      reads=['VS%d' % k for k in range((B0 - 1) // 4, (B0 + nb - 1) // 4 + 1)], writes=['vsA'], dma=True)
                for h in range(NSH):
                    g = h // 8
                    hp, ho = h // 2, (h % 2) * 64
                    sbk = 0 if h % 2 == 0 else 2
                    SP_ = self.PS[:, sbk * 512: sbk * 512 + 2 * nb * 128]
                    skeys = ['P%d' % sbk, 'P%d' % (sbk + 1)]
                    obk = 4 + (h % 2)
                    mmlist = []
                    for kb in range(nkb):
                        if kb == 0:
                            mmlist.append((kb, 0, 0, 1))
                        elif kb == nb:
                            mmlist.append((kb, 2 * nb - 1, nb - 1, 1))
                        else:
                            seg0 = 2 * kb - 1
                            if (seg0 * 128) // 512 != ((seg0 + 2) * 128 - 1) // 512:
                                mmlist.append((kb, seg0, kb - 1, 1))
                                mmlist.append((kb, seg0 + 1, kb, 1))
                            else:
                                mmlist.append((kb, seg0, kb - 1, 2))
                    for (kb, seg, qb, nq) in mmlist:
                        tr.op('pe', lambda e, kb=kb, seg=seg, qb=qb, nq=nq: e.matmul(
                            SP_[:, seg * 128:(seg + nq) * 128], lhsT=ksT2[:, h % 2, g, kb * 128:(kb + 1) * 128],
                            rhs=qsT[:, hp, qb * 128:(qb + nq) * 128], start=True, stop=True),
                            reads=['ksT2', 'qs%d' % hp], writes=skeys)
                    pt = pTs[h % 2]
                    pk = 'pTs%d' % (h % 2)
                    tr.op('act', lambda e, pt=pt: e.activation(out=pt[:, 0:128], in_=SP_[:, 0:128], func=AF.Exp, scale=SWA_SCALE,
                                                               bias=self.kmask[:, B0 - 1:B0]),
                          reads=skeys + ['kmask'], writes=[pk])
                    tr.op('act', lambda e, pt=pt: e.activation(out=pt[:, 128:2 * nb * 128], in_=SP_[:, 128:2 * nb * 128], func=AF.Exp,
                                                               scale=SWA_SCALE), reads=skeys, writes=[pk])
                    pv = pt[:, 0:2 * nb * 128].rearrange("p (b r q) -> p b r q", b=nb, r=2)
                    ehh = eh[h % 2]
                    tr.op('sp', lambda e, ehh=ehh, h=h: e.dma_start(out=ehh[:].rearrange("p r q -> p (r q)"), in_=self.ED[:, h, :]),
                          writes=['eh%d' % (h % 2)], dma=True)
                    ev_ = ehh[:].unsqueeze(1).to_broadcast([128, nb, 2, 128])
                    tr.op('pool' if h % 2 == 0 else 'dve',
                          lambda e, pv=pv, ev_=ev_: e.tensor_tensor(out=pv, in0=pv, in1=ev_, op=ALU.mult),
                          reads=[pk, 'eh%d' % (h % 2)], writes=[pk])
                    OB = self.bank(obk, nb * 65).rearrange("p (b c) -> p b c", c=65)
                    first = True
                    for qb in range(nb):
                        for r in range(2):
                            tr.op('pe', lambda e, qb=qb, r=r, first=first, pv=pv: e.matmul(
                                OB[:, qb, :], lhsT=pv[:, qb, r, :], rhs=vsA[:, qb + r, g, :], start=first,
                                stop=(qb == nb - 1 and r == 1), skip_group_check=True),
                                reads=[pk, 'vsA'], writes=['P%d' % obk])
                            first = False
                    tr.op('dve', lambda e, OB=OB, h=h: e.tensor_scalar(out=dd[:, 0:nb], in0=OB[:, :, 64], scalar1=self.expsink[:, h:h + 1],
                                                                       scalar2=None, op0=ALU.add),
                          reads=['P%d' % obk, 'expsink'], writes=['dd'])
                    tr.op('dve', lambda e: e.reciprocal(out=rden[:, 0:nb], in_=dd[:, 0:nb]), reads=['dd'], writes=['rden'])
                    obt = obtok[hp % 2]
                    tr.op('dve', lambda e, OB=OB, obt=obt: e.tensor_tensor(
                        out=obt[:, 0:nb, ho:ho + 64], in0=OB[:, :, 0:64], in1=rden[:, 0:nb].unsqueeze(2).to_broadcast([128, nb, 64]),
                        op=ALU.mult), reads=['P%d' % obk, 'rden'], writes=['obtok%d' % (hp % 2)])
                    if h % 2 == 1:
                        tbk = 6 + (hp % 2)
                        for qb in range(nb):
                            tr.op('pe', lambda e, qb=qb, obt=obt, tbk=tbk: e.transpose(out=self.bankb(tbk)[:, qb * 128:(qb + 1) * 128],
                                                                                       in_=obt[:, qb, :], identity=self.ident[:]),
                                  reads=['obtok%d' % (hp % 2), 'ident'], writes=['P%d' % tbk])
                        tr.op('act', lambda e, hp=hp, tbk=tbk: e.activation(out=obT[:, hp, 0:T], in_=self.bankb(tbk, T), func=AF.Copy),
                              reads=['P%d' % tbk], writes=['ob%d' % hp])
                if self.stop_after == 'q5':
                    self.tr.barrier()
                    return
                nkbm = B0 + nb
                nch = (nkbm + KC - 1) // KC
                self._kv = getattr(self, '_kv', 0)

                def kvload(h, ci):
                    sl = self._kv % 3
                    self._kv += 1
                    k0 = ci * KC
                    kn_ = min(KC, nkbm - k0)
                    kts = list(range(k0 // 4, (k0 + kn_ - 1) // 4 + 1))
                    tr.op('sp', lambda e: e.dma_start(out=knc[sl][:, 0:kn_ * 128], in_=self.KN[h, :, k0 * 128:(k0 + kn_) * 128]),
                          reads=['KN%d' % k for k in kts], writes=['knc%d' % sl], dma=True)
                    for half in range(2):
                        tr.op('sp', lambda e, half=half: e.dma_start(out=krc[sl][half * 64:(half + 1) * 64, 0:kn_ * 128],
                                                                     in_=self.KR[:, k0 * 128:(k0 + kn_) * 128]),
                              reads=['KR%d' % k for k in kts], writes=['krc%d' % sl], dma=True)
                    tr.op('sp', lambda e: e.dma_start(out=vac[sl][:, 0:kn_, :], in_=self.VN[h, :, k0:k0 + kn_, :]),
                          reads=['VN%d' % k for k in kts], writes=['vac%d' % sl], dma=True)
                    return sl, k0, kn_

                loads = [(h, ci) for h in range(NH) for ci in range(nch)]
                loaded = {}

                def ensure(gi):
                    if gi < len(loads) and gi not in loaded:
                        loaded[gi] = kvload(*loads[gi])

                ensure(0)
                self._pt = getattr(self, '_pt', 0)
                for h in range(NH):
                    hp, ho = h // 2, (h % 2) * 64
                    oa, obb = (2, 3) if h % 2 == 0 else (4, 5)
                    obanks = [oa, obb]
                    started = [False, False]
                    steps = []
                    for ci in range(nch):
                        k0 = ci * KC
                        for kbl in range(min(KC, nkbm - k0)):
                            steps.append((h * nch + ci, kbl, k0 + kbl))

                    def s_step(si):
                        gi, kbl, kb = steps[si]
                        ensure(gi)
                        if kbl == 0:
                            ensure(gi + 1)
                        sl = loaded[gi][0]
                        r = max(0, kb - B0)
                        c0 = r * 128
                        sbk = (0, 1, 7)[si % 3]
                        tr.op('pe', lambda e: e.matmul(self.bank(sbk)[:, c0:T], lhsT=knc[sl][:, kbl * 128:(kbl + 1) * 128],
                                                       rhs=qnT[:, h, c0:T], start=True, stop=False),
                              reads=['knc%d' % sl, 'qn%d' % h], writes=['P%d' % sbk])
                        tr.op('pe', lambda e: e.matmul(self.bank(sbk)[:, c0:T], lhsT=krc[sl][:, kbl * 128:(kbl + 1) * 128],
                                                       rhs=qrT[:, h, c0:T], start=False, stop=True),
                              reads=['krc%d' % sl, 'qr%d' % h], writes=['P%d' % sbk])

                    def e_step(si):
                        gi, kbl, kb = steps[si]
                        sl = loaded[gi][0]
                        r = max(0, kb - B0)
                        c0 = r * 128
                        sbk = (0, 1, 7)[si % 3]
                        pi = self._pt % 3
                        self._pt += 1
                        p = pT[pi]
                        tr.op('act', lambda e: e.activation(out=p[:, c0:T], in_=self.bank(sbk)[:, c0:T], func=AF.Exp, scale=MLA_SCALE,
                                                            bias=self.kmask[:, kb:kb + 1]),
                              reads=['P%d' % sbk, 'kmask'], writes=['pT%d' % pi])
                        if kb >= B0:
                            tr.op('pool', lambda e: e.affine_select(out=p[:, c0:c0 + 128], in_=p[:, c0:c0 + 128], pattern=[[1, 128]],
                                                                    compare_op=ALU.is_ge, fill=0.0, base=0, channel_multiplier=-1),
                                  reads=['pT%d' % pi], writes=['pT%d' % pi])
                        return pi

                    def v_step(si, pi):
                        gi, kbl, kb = steps[si]
                        sl = loaded[gi][0]
                        r = max(0, kb - B0)
                        p = pT[pi]
                        for qs in range(r, nb):
                            ob_ = obanks[qs // 2]
                            st = not started[qs // 2]
                            started[qs // 2] = True
                            last = (kb == B0 + qs) and (qs % 2 == 1 or qs == nb - 1)
                            tr.op('pe', lambda e, qs=qs, ob_=ob_, st=st, last=last: e.matmul(
                                self.bank(ob_)[:, (qs % 2) * 129:(qs % 2) * 129 + 129], lhsT=p[:, qs * 128:(qs + 1) * 128],
                                rhs=vac[sl][:, kbl, :], start=st, stop=last, skip_group_check=True),
                                reads=['pT%d' % pi, 'vac%d' % sl], writes=['P%d' % ob_])

                    ns = len(steps)
                    s_step(0)
                    if ns > 1:
                        s_step(1)
                    for si in range(ns):
                        if si + 2 < ns:
                            s_step(si + 2)
                        pi = e_step(si)
                        v_step(si, pi)
                    for bi in range((nb + 1) // 2):
                        nq = min(2, nb - 2 * bi)
                        OV = self.bank(obanks[bi], 2 * 129).rearrange("p (q c) -> p q c", c=129)
                        tr.op('dve', lambda e, OV=OV, bi=bi, nq=nq: e.tensor_scalar(out=dd[:, 2 * bi:2 * bi + nq], in0=OV[:, 0:nq, 128],
                                                                                    scalar1=1e-30, scalar2=None, op0=ALU.add),
                              reads=['P%d' % obanks[bi]], writes=['dd'])
                        tr.op('dve', lambda e, bi=bi, nq=nq: e.reciprocal(out=rden[:, 2 * bi:2 * bi + nq], in_=dd[:, 2 * bi:2 * bi + nq]),
                              reads=['dd'], writes=['rden'])
                        tr.op('dve', lambda e, OV=OV, bi=bi, nq=nq: e.tensor_tensor(
                            out=otok[:, 2 * bi:2 * bi + nq, :], in0=OV[:, 0:nq, 0:128],
                            in1=rden[:, 2 * bi:2 * bi + nq].unsqueeze(2).to_broadcast([128, nq, 128]), op=ALU.mult),
                            reads=['P%d' % obanks[bi], 'rden'], writes=['otok'])
                    tbk = 6
                    for qb in range(nb):
                        tr.op('pe', lambda e, qb=qb, tbk=tbk: e.transpose(out=self.bankb(tbk)[:, qb * 128:(qb + 1) * 128], in_=otok[:, qb, :],
                                                                         identity=self.ident[:]),
                              reads=['otok', 'ident'], writes=['P%d' % tbk])
                    tr.op('act', lambda e, h=h, tbk=tbk: e.activation(out=oaT[:, h, 0:T], in_=self.bankb(tbk, T), func=AF.Copy),
                          reads=['P%d' % tbk] + ['qs%d' % h], writes=['oa%d' % h, 'qs%d' % h])
                if self.debug and not halo and B0 == self.OWN0:
                    for nm, src in [("dbg_oa", oaT), ("dbg_ob", obT), ("dbg_qn", qnT), ("dbg_hT", hT[:])]:
                        dt_ = self.dram_scr(nm, [128, 16, 512], BF16)
                        tr.op('sp', lambda e, dt_=dt_, src=src: e.dma_start(out=dt_, in_=src), reads=['oa%d' % i for i in range(16)] + ['ob%d' % i for i in range(16)], dma=True)
                    dt_ = self.dram_scr("dbg_qr", [128, 16, 512], BF16)
                    tr.op('sp', lambda e, dt_=dt_: e.dma_start(out=dt_, in_=qrT), dma=True)
                self.tr.barrier()
                if self.stop_after == 'q6':
                    self.tr.barrier()
                    return
                wl = []
                for n2 in range(8):
                    wl.append(('womla', 2 * n2))
                    wl.append(('woswa', 2 * n2))
                    wl.append(('win', 27 + 2 * n2))
                    wl.append(('win', 43 + 2 * n2))
                pend = wload(wl[0][0], wl[0][1], 2, JD)
                for wi, (wname, wn) in enumerate(wl):
                    cur = pend
                    pend = wload(wl[wi + 1][0], wl[wi + 1][1], 2, JD) if wi + 1 < len(wl) else None
                    kind = wi % 4
                    for sub in range(2):
                        n = 2 * (wi // 4) + sub
                        bk = (n % 2) * 4 + kind
                        wv, wkey = cur[0][:, sub], cur[1]
                        for j in range(JD):
                            if kind == 0:
                                rhs, rk = oaT[:, j, 0:T], 'oa%d' % j
                            elif kind == 1:
                                rhs, rk = obT[:, j, 0:T], 'ob%d' % j
                            else:
                                rhs, rk = hT[:, j, 0:T], hk[j]
                            tr.op('pe', lambda e, j=j, wv=wv, bk=bk, rhs=rhs: e.matmul(self.bank(bk, T), lhsT=wv[:, j, :], rhs=rhs,
                                                                                       start=(j == 0), stop=(j == JD - 1)),
                                  reads=[wkey, rk], writes=['P%d' % bk])
                    if kind != 3:
                        continue
                    for sub in range(2):
                        n = 2 * (wi // 4) + sub
                        b0 = 0 if n % 2 == 0 else 4
                        tr.op('act', lambda e, b0=b0: e.activation(out=sg[0][:, 0:T], in_=self.bank(b0 + 2, T), func=AF.Sigmoid),
                              reads=['P%d' % (b0 + 2)], writes=['sg0'])
                        tr.op('act', lambda e, b0=b0: e.activation(out=sg[1][:, 0:T], in_=self.bank(b0 + 3, T), func=AF.Sigmoid),
                              reads=['P%d' % (b0 + 3)], writes=['sg1'])
                        tr.op('dve', lambda e, b0=b0: e.tensor_tensor(out=sg[0][:, 0:T], in0=self.bank(b0, T), in1=sg[0][:, 0:T], op=ALU.mult),
                              reads=['P%d' % b0, 'sg0'], writes=['sg0'])
                        tr.op('dve', lambda e, b0=b0: e.tensor_tensor(out=sg[1][:, 0:T], in0=self.bank(b0 + 1, T), in1=sg[1][:, 0:T], op=ALU.mult),
                              reads=['P%d' % (b0 + 1), 'sg1'], writes=['sg1'])
                        tr.op('pool', lambda e, n=n: e.tensor_tensor(out=mT[:, n, 0:T], in0=sg[0][:, 0:T], in1=sg[1][:, 0:T], op=ALU.add),
                              reads=['sg0', 'sg1'], writes=['mT%d' % n])
                if self.debug and not halo and B0 == self.OWN0:
                    dt_ = self.dram_scr("dbg_m", [128, 16, 512], BF16)
                    tr.op('sp', lambda e, dt_=dt_: e.dma_start(out=dt_, in_=mT), reads=['mT%d' % i for i in range(16)], dma=True)
                self.tr.barrier()
                if self.stop_after == 'q7':
                    self.tr.barrier()
                    return
                i = 0
                for m, wv, wkey in proj_groups('wout', list(range(16)), JD, 2):
                    bk = i % 4
                    i += 1
                    for n in range(JD):
                        tr.op('pe', lambda e, n=n, wv=wv, bk=bk: e.matmul(self.bank(bk, T), lhsT=wv[:, n, :], rhs=mT[:, n, 0:T],
                                                                          start=(n == 0), stop=(n == JD - 1)),
                              reads=[wkey, 'mT%d' % n], writes=['P%d' % bk])
                    y_chunk(m, bk, T, nb, 'norm_mix_post')

                if self.stop_after == 'q8a':
                    self.tr.barrier()
                    return

                def x1_done(blk, xin, xk):
                    if not halo:
                        tr.op('sp', lambda e: e.dma_start(out=self.X1S[r0 + blk * 128:r0 + (blk + 1) * 128, :], in_=xin[:]),
                              reads=[xk], writes=['X1S'], dma=True)
                    self.norm_T(bufs, blk, blk * 128, hT, 'hT')

                post_norm(T, nb, 'norm_mix_post', lambda blk: I['xs'][s0 + blk * 128:s0 + (blk + 1) * 128, :], x1_done, False)
                self.tr.barrier()
                if self.stop_after in ('q8b', 'q8c', 'q8d'):
                    return
                if self.stop_after == 'q8':
                    self.tr.barrier()
                    return
                wl = []
                for j2 in range(JF // 2):
                    wl.append(2 * j2)
                    if not halo:
                        wl.append(JF + 2 * j2)
                pend = wload('wup', wl[0], 2, JD)
                for wi, wn0 in enumerate(wl):
                    cur = pend
                    pend = wload('wup', wl[wi + 1], 2, JD) if wi + 1 < len(wl) else None
                    for sub in range(2):
                        self._ffn_chunk(tr, cur, sub, wn0 + sub, T, halo, hT, hk, aext, sg, gT)
                if False:
                    isb = jc = bk = wv = wkey = None
                    pass
                if halo:
                    self.tr.barrier()
                    if self.stop_after == 'halo':
                        return
                    continue
                i = 0
                for m, wv, wkey in proj_groups('wdown', list(range(16)), JF, 1):
                    bk = i % 4
                    i += 1
                    for k in range(JF):
                        tr.op('pe', lambda e, k=k, wv=wv, bk=bk: e.matmul(self.bank(bk, T), lhsT=wv[:, k, :], rhs=gT[:, k, 0:T],
                                                                          start=(k == 0), stop=(k == JF - 1)),
                              reads=[wkey, 'gT%d' % k], writes=['P%d' % bk])
                    y_chunk(m, bk, T, nb, 'norm_ffn_post')

                def out_done(blk, xin, xk):
                    tr.op('sp', lambda e: e.dma_start(out=self.out[r0 + blk * 128:r0 + (blk + 1) * 128, :], in_=xin[:]),
                          reads=[xk], writes=['OUT'], dma=True)

                post_norm(T, nb, 'norm_ffn_post', lambda blk: self.X1S[r0 + blk * 128:r0 + (blk + 1) * 128, :], out_done, True)
                self.tr.barrier()


def _t5_bucket(n):
    n = np.maximum(n, 0)
    nf = np.maximum(n, 1).astype(np.float32)
    large = 16 + (np.log(nf / np.float32(16)) / np.float32(math.log(128 / 16)) * np.float32(16)).astype(np.int32)
    large = np.minimum(large, 31)
    return np.where(n < 16, n, large)


def _onehot():
    k = np.arange(128)[:, None, None]
    r = np.arange(2)[None, :, None]
    q = np.arange(128)[None, None, :]
    dist = q + 128 * (1 - r) - k
    valid = (dist >= 0) & (dist < 128)
    b = np.where(valid, _t5_bucket(dist), 32)
    oh = np.zeros((33, 128, 2, 128), np.float32)
    for i in range(33):
        oh[i] = (b == i)
    return oh.reshape(33, -1)


_CACHE = {}
NCORES_DEBUG = None


def run(inputs, S):
    x = np.asarray(inputs['x'], np.float32)
    B = x.shape[0]
    CH = S // 4
    NBLK = S // 128
    if S not in _CACHE:
        _CACHE[S] = Builder(S).build()
    nc = _CACHE[S]
    inv = (10000.0 ** (-np.arange(0, 64, 2, dtype=np.float32) / np.float32(64))).astype(np.float32)
    oh = _onehot()
    shared = {}
    for nm in ["norm_mix_pre", "norm_mix_post", "norm_ffn_pre", "norm_ffn_post", "w_in", "mla_q_norm", "mla_w_q_up",
               "mla_kv_norm", "mla_w_kv_up", "swa_sinks", "w_o_mla", "w_o_swa", "w_out", "ffn_w_up", "ffn_conv_w",
               "ffn_conv_b", "ffn_w_down"]:
        a = np.asarray(inputs[nm], np.float32)
        shared[nm] = np.ascontiguousarray(a.reshape(a.shape[1:]))
    shared["rel_bias_table"] = np.ascontiguousarray(np.asarray(inputs["rel_bias_table"], np.float32))
    shared["onehot"] = oh
    in_maps = []
    for core in range(8):
        b, c = core // 4, core % 4
        if b >= B:
            b = B - 1
        pad = (3 - c) * CH
        xs = np.zeros((S, D), np.float32)
        xs[pad:] = x[b, :S - pad]
        pos = (np.arange(S) - pad).astype(np.float32)
        ang = (pos[None, :] * inv[:, None]).astype(np.float32)
        cos = np.cos(ang.astype(np.float64)).astype(np.float32)
        sin = np.sin(ang.astype(np.float64)).astype(np.float32)
        cos128 = np.ascontiguousarray(np.tile(cos, (4, 1)))
        sin128 = np.ascontiguousarray(np.tile(sin, (4, 1)))
        km = np.zeros((128, NBLK), np.float32)
        km[:, :pad // 128] = NEG
        m = dict(shared)
        m.update(xs=xs, cos128=cos128, sin128=sin128, kmask=km)
        in_maps.append(m)
    ncores = NCORES_DEBUG or 8
    res = run_bass_kernel_spmd(nc, in_maps[:ncores], core_ids=list(range(ncores)))
    out = np.zeros((B, S, D), np.float32)
    for core in range(ncores):
        b, c = core // 4, core % 4
        if b < B:
            out[b, c * CH:(c + 1) * CH] = res.results[core]["out"]
    return out


def kernel(**inputs):
    return run(inputs, 16384)
```

```python
import math
from contextlib import ExitStack

import numpy as np
import concourse.bass as bass
import concourse.mybir as mybir
from concourse.bass_utils import run_bass_kernel_spmd

F32 = mybir.dt.float32
BF16 = mybir.dt.bfloat16
AF = mybir.ActivationFunctionType
ALU = mybir.AluOpType
AX = mybir.AxisListType

D = 2048
JD = 16
NH = 16
NSH = 32
DFF = 5632
JF = 44
EPS = 1e-6
NEG = -30000.0
MLA_SCALE = 192.0 ** -0.5
SWA_SCALE = 64.0 ** -0.5

EPOCH = 16000
NDSEM = 16


class Tracer:
    ENG = ('pe', 'act', 'dve', 'pool', 'sp')

    def __init__(self, nc, es):
        self.nc = nc
        self.es = es
        self.eng = {'pe': nc.tensor, 'act': nc.scalar, 'dve': nc.vector, 'pool': nc.gpsimd, 'sp': nc.sync}
        self.count = {e: 0 for e in self.ENG}
        self.known = {e: {f: -1 for f in self.ENG} for e in self.ENG}
        self.known_dma = {e: set() for e in self.ENG}
        self.last_w = {}
        self.readers = {}
        self.ndma = {'sp': 0, 'pool': 0, 'act': 0}
        self.sems = {}
        self.nwaits = 0

    def sem(self, key):
        s = self.sems.get(key)
        if s is None:
            s = self.es.enter_context(self.nc.semaphore("s_" + "_".join(str(k) for k in key)))
            self.sems[key] = s
        return s

    def _need(self, c, sig):
        if sig[0] == 'c':
            e, n = sig[1], sig[2]
            if self.known[c][e] >= n:
                return
            self.known[c][e] = n
            self.eng[c].wait_ge(self.sem(('S', e, n // EPOCH)), n % EPOCH + 1)
        else:
            q, d = sig[1], sig[2]
            if (q, d) in self.known_dma[c]:
                return
            self.known_dma[c].add((q, d))
            self.eng[c].wait_ge(self.sem(('D' + q, d % NDSEM)), 16 * (d // NDSEM + 1))
        self.nwaits += 1

    def _dep(self, c, sig, kind):
        if sig[0] == 'c' and sig[1] == c:
            if c == 'pe' or (kind != 'raw' and c != 'pool'):
                return
        self._need(c, sig)

    def op(self, eng, fn, reads=(), writes=(), dma=False):
        for r in reads:
            w = self.last_w.get(r)
            if w is not None:
                self._dep(eng, w, 'raw')
            if r[0] == 'P' and r[1:].isdigit():
                for rd in self.readers.get(r, ()):
                    self._dep(eng, rd, 'war')
        for r in writes:
            w = self.last_w.get(r)
            if w is not None:
                self._dep(eng, w, 'waw')
            for rd in self.readers.get(r, ()):
                self._dep(eng, rd, 'war')
        if dma:
            d = self.ndma[eng]
            self.ndma[eng] += 1
            if d >= NDSEM:
                self._need(eng, ('d', eng, d - NDSEM))
            sig = ('d', eng, d)
            ins = fn(self.eng[eng])
            ins.then_inc(self.sem(('D' + eng, d % NDSEM)), 16)
        else:
            n = self.count[eng]
            self.count[eng] += 1
            sig = ('c', eng, n)
            ins = fn(self.eng[eng])
            ins.then_inc(self.sem(('S', eng, n // EPOCH)), 1)
        for r in reads:
            lst = self.readers.setdefault(r, [])
            if sig[0] == 'c':
                lst[:] = [x for x in lst if not (x[0] == 'c' and x[1] == sig[1])]
            lst.append(sig)
        for r in writes:
            self.last_w[r] = sig
            self.readers[r] = []
        return sig

    def barrier(self):
        for c in self.ENG:
            for e in self.ENG:
                if e == c or e == 'sp':
                    continue
                n = self.count[e] - 1
                if n >= 0:
                    self._need(c, ('c', e, n))
            for q in self.ndma:
                for k in range(max(0, self.ndma[q] - NDSEM), self.ndma[q]):
                    self._need(c, ('d', q, k))
        self.last_w.clear()
        self.readers.clear()

    def finish(self):
        for q in self.ndma:
            for k in range(max(0, self.ndma[q] - NDSEM), self.ndma[q]):
                self._need('sp', ('d', q, k))


def _regular(src0, nch, dst0):
    return [(src0 + 128 * i, 128, 1.0, dst0 + i, 0) for i in range(nch)]


def win_pieces():
    p = []
    p += _regular(0, 4, 0)
    p += _regular(512, 2, 4)
    p += [(768, 64, 1.0, 6, 0), (800, 32, -1.0, 6, 64), (768, 32, 1.0, 6, 96)]
    p += _regular(832, 16, 7)
    p += _regular(2880, 2, 23)
    p += _regular(3136, 2, 25)
    p += _regular(3392, 32, 27)
    return p, 59


def wq_pieces():
    p = []
    for h in range(NH):
        b = h * 192
        off = (h % 2) * 64
        p.append((b, 128, 1.0, h, 0))
        p.append((b + 128, 64, 1.0, 16 + h // 2, off))
        p.append((b + 160, 32, -1.0, 24 + h // 2, off))
        p.append((b + 128, 32, 1.0, 24 + h // 2, off + 32))
    return p, 32


def wkv_pieces():
    p = []
    for h in range(NH):
        p.append((h * 256, 128, 1.0, h, 0))
        p.append((h * 256 + 128, 128, 1.0, 16 + h, 0))
    return p, 32


class Builder:
    def __init__(self, S, KC=16, debug=False, stop_after=None):
        self.debug = debug
        self.stop_after = stop_after
        self.S = S
        self.NBLK = S // 128
        self.CB = self.NBLK // 4
        self.CH = S // 4
        self.NT = self.CB // 4
        self.NKT = self.NBLK // 4
        self.OWN0 = 3 * self.CB
        self.KC = min(KC, self.NBLK)
        self.nc = bass.Bass("TRN2", target_bir_lowering=False)
        self.es = ExitStack()

    def dram_in(self, name, shape):
        return self.nc.dram_tensor(name, list(shape), F32, kind="ExternalInput").ap()

    def dram_scr(self, name, shape, dt):
        return self.nc.dram_tensor(name, list(shape), dt, kind="Internal").ap()

    def sb(self, name, shape, dt, es=None):
        return (es or self.es).enter_context(self.nc.sbuf_tensor(name, list(shape), dt))

    def build(self):
        nc, S = self.nc, self.S
        with self.es:
            self.tr = Tracer(nc, self.es)
            I = {}
            I['xs'] = self.dram_in("xs", [S, D])
            for nm, shp in [("norm_mix_pre", [D]), ("norm_mix_post", [D]), ("norm_ffn_pre", [D]), ("norm_ffn_post", [D]),
                            ("w_in", [D, 7488]), ("mla_q_norm", [512]), ("mla_w_q_up", [512, 3072]),
                            ("mla_kv_norm", [256]), ("mla_w_kv_up", [256, 4096]), ("swa_sinks", [NSH]),
                            ("rel_bias_table", [32, NSH]), ("w_o_mla", [D, D]), ("w_o_swa", [D, D]), ("w_out", [D, D]),
                            ("ffn_w_up", [D, 2 * DFF]), ("ffn_conv_w", [3, DFF]), ("ffn_conv_b", [DFF]),
                            ("ffn_w_down", [DFF, D]),
                            ("cos128", [128, S]), ("sin128", [128, S]), ("kmask", [128, self.NBLK]),
                            ("onehot", [33, 2 * 128 * 128])]:
                I[nm] = self.dram_in(nm, shp)
            self.I = I
            self.out = nc.dram_tensor("out", [self.CH, D], F32, kind="ExternalOutput").ap()
            Wd = {}
            Wd['win'] = self.dram_scr("wb_in", [128, 59, JD, 128], BF16)
            Wd['wq'] = self.dram_scr("wb_q", [128, 32, 4, 128], BF16)
            Wd['wkv'] = self.dram_scr("wb_kv", [128, 32, 2, 128], BF16)
            Wd['womla'] = self.dram_scr("wb_omla", [128, 16, JD, 128], BF16)
            Wd['woswa'] = self.dram_scr("wb_oswa", [128, 16, JD, 128], BF16)
            Wd['wout'] = self.dram_scr("wb_out", [128, 16, JD, 128], BF16)
            Wd['wup'] = self.dram_scr("wb_up", [128, 88, JD, 128], BF16)
            Wd['wdown'] = self.dram_scr("wb_down", [128, 16, JF, 128], BF16)
            self.Wd = Wd
            self.KN = self.dram_scr("scr_kn", [NH, 128, S], BF16)
            self.KR = self.dram_scr("scr_kr", [64, S], BF16)
            self.VN = self.dram_scr("scr_vn", [NH, 128, self.NBLK, 129], BF16)
            self.KS = self.dram_scr("scr_ks", [256, S], BF16)
            self.VS = self.dram_scr("scr_vs", [128, self.NBLK, 4, 65], BF16)
            self.X1S = self.dram_scr("scr_x1", [self.CH, D], F32)
            self.BF = self.dram_scr("scr_bias", [NSH, 2 * 128 * 128], F32)
            self.ED = self.dram_scr("scr_E", [128, NSH, 256], BF16)
            self.RS = self.dram_scr("scr_rs", [1, 512], F32)
            self.PS = self.es.enter_context(nc.psum_tensor("psum_all", [128, 8 * 512], F32))
            self.PSB = self.PS[:].bitcast(BF16)
            self.consts()
            if self.stop_after != 'consts':
                self.phase_k()
                self.tr.barrier()
                if self.stop_after != 'k':
                    self.phase_q()
            self.tr.finish()
        return nc

    def bank(self, i, w=512):
        return self.PS[:, i * 512: i * 512 + w]

    def bankb(self, i, w=1024):
        return self.PSB[:, i * 1024: i * 1024 + w]

    def consts(self):
        nc, tr, I = self.nc, self.tr, self.I
        sb = self.sb
        self.identf = sb("identf", [128, 128], F32)
        self.ident = sb("ident", [128, 128], BF16)
        self.onesf = sb("onesf", [128, 128], F32)
        self.onesb = sb("onesb", [128, 128], BF16)
        self.epsc = sb("epsc", [128, 1], F32)
        self.onec = sb("onec", [128, 1], F32)
        self.negc = sb("negc", [128, 1], F32)
        self.kmask = sb("kmask_sb", [128, self.NBLK], F32)
        self.expsink = sb("expsink", [128, NSH], F32)
        self.cw = sb("convw", [128, 3, JF], F32)
        self.cb = sb("convb", [128, JF], F32)
        self.ahalo = sb("ahalo", [128, JF, 2], F32)
        self.gains = {}
        for nm, J in [("norm_mix_pre", JD), ("norm_mix_post", JD), ("norm_ffn_pre", JD), ("norm_ffn_post", JD),
                      ("mla_q_norm", 4), ("mla_kv_norm", 2)]:
            g = sb("g_" + nm, [128, J], F32)
            self.gains[nm] = g
            tr.op('sp', lambda e, g=g, nm=nm: e.dma_start(out=g[:], in_=I[nm].rearrange("(j p) -> p j", p=128), allow_slow_non_contiguous=True),
                  writes=['g_' + nm], dma=True)
        self.ngains = {}
        for nm, J in [("norm_mix_pre", JD), ("mla_q_norm", 4)]:
            g = sb("ng_" + nm, [128, J], F32)
            self.ngains[nm] = g
            tr.op('dve', lambda e, g=g, nm=nm: e.tensor_scalar(out=g[:], in0=self.gains[nm][:], scalar1=-1.0, scalar2=None,
                                                              op0=ALU.mult), reads=['g_' + nm], writes=['ng_' + nm])
        tr.op('pool', lambda e: e.memset(self.identf[:], 1.0), writes=['identf'])
        tr.op('pool', lambda e: e.affine_select(out=self.identf[:], in_=self.identf[:], pattern=[[1, 128]],
                                                compare_op=ALU.is_equal, fill=0.0, base=0, channel_multiplier=-1),
              reads=['identf'], writes=['identf'])
        tr.op('dve', lambda e: e.tensor_copy(out=self.ident[:], in_=self.identf[:]), reads=['identf'], writes=['ident'])
        tr.op('dve', lambda e: e.memset(self.onesf[:], 1.0), writes=['onesf'])
        tr.op('dve', lambda e: e.memset(self.onesb[:], 1.0), writes=['onesb'])
        tr.op('dve', lambda e: e.memset(self.epsc[:], EPS), writes=['epsc'])
        tr.op('dve', lambda e: e.memset(self.onec[:], 1.0), writes=['onec'])
        tr.op('dve', lambda e: e.memset(self.negc[:], -1.0), writes=['negc'])
        tr.op('dve', lambda e: e.memset(self.ahalo[:], 0.0), writes=['ahalo'])
        tr.op('sp', lambda e: e.dma_start(out=self.kmask[:], in_=I['kmask']), writes=['kmask'], dma=True)
        tr.op('sp', lambda e: e.dma_start(out=self.cw[:], in_=I['ffn_conv_w'].rearrange("w (j p) -> p w j", p=128), allow_slow_non_contiguous=True),
              writes=['cw'], dma=True)
        tr.op('sp', lambda e: e.dma_start(out=self.cb[:], in_=I['ffn_conv_b'].rearrange("(j p) -> p j", p=128), allow_slow_non_contiguous=True),
              writes=['cb'], dma=True)
        tr.op('sp', lambda e: e.dma_start(out=self.expsink[:], in_=I['swa_sinks'].partition_broadcast(128)),
              writes=['expsink'], dma=True)
        tr.op('act', lambda e: e.activation(out=self.expsink[:], in_=self.expsink[:], func=AF.Exp),
              reads=['expsink'], writes=['expsink'])
        with ExitStack() as les:
            tab = self.sb("tab_ext", [33, NSH], F32, les)
            oh = self.sb("oh_sb", [33, 4096], F32, les)
            bst = self.sb("bias_st", [NSH, 4096], F32, les)
            ebig = self.sb("ebig", [128, NSH, 256], F32, les)
            esb = self.sb("esb", [128, NSH, 256], BF16, les)
            tr.op('dve', lambda e: e.memset(tab[:], NEG), writes=['tab'])
            tr.op('sp', lambda e: e.dma_start(out=tab[0:32, :], in_=I['rel_bias_table']), writes=['tab'], dma=True)
            for cch in range(8):
                tr.op('sp', lambda e, cch=cch: e.dma_start(out=oh[:], in_=I['onehot'][:, cch * 4096:(cch + 1) * 4096]),
                      writes=['oh'], dma=True)
                for q in range(8):
                    bk = q % 4
                    tr.op('pe', lambda e, q=q, bk=bk: e.matmul(self.bank(bk)[0:32, :], lhsT=tab[:], rhs=oh[:, q * 512:(q + 1) * 512],
                                                               start=True, stop=True),
                          reads=['tab', 'oh'], writes=['P%d' % bk])
                    tr.op('act', lambda e, q=q, bk=bk: e.activation(out=bst[:, q * 512:(q + 1) * 512], in_=self.bank(bk)[0:32, :],
                                                                    func=AF.Copy),
                          reads=['P%d' % bk], writes=['bst'])
                tr.op('sp', lambda e, cch=cch: e.dma_start(out=self.BF[:, cch * 4096:(cch + 1) * 4096], in_=bst[:]),
                      reads=['bst'], writes=['BF'], dma=True)
            self.tr.barrier()
            tr.op('sp', lambda e: e.dma_start(out=ebig[:], in_=self.BF.rearrange("h (k x) -> k h x", k=128)),
                  writes=['ebig'], dma=True)
            tr.op('act', lambda e: e.activation(out=esb[:], in_=ebig[:], func=AF.Exp), reads=['ebig'], writes=['esb'])
            tr.op('sp', lambda e: e.dma_start(out=self.ED, in_=esb[:]), reads=['esb'], writes=['ED'], dma=True)
            self.tr.barrier()

    def phase_w_items(self, les):
        tr, I = self.tr, self.I
        stage = [self.sb("wst%d" % i, [128, JD, 256], F32, les) for i in range(2)]
        ost = [self.sb("wos%d" % i, [128, 2, JD, 128], BF16, les) for i in range(2)]
        self._wcnt = 0
        self._ccnt = 0

        def cast(out_ap, in_ap, gain_ap, rkeys, wkeys):
            k = self._ccnt % 3
            self._ccnt += 1
            if k == 0:
                tr.op('act', lambda e: e.activation(out=out_ap, in_=in_ap, func=AF.Copy, scale=gain_ap),
                      reads=rkeys, writes=wkeys)
            elif k == 1:
                tr.op('dve', lambda e: e.tensor_scalar(out=out_ap, in0=in_ap, scalar1=gain_ap, scalar2=None, op0=ALU.mult),
                      reads=rkeys, writes=wkeys)
            else:
                tr.op('pool', lambda e: e.tensor_scalar(out=out_ap, in0=in_ap, scalar1=gain_ap, scalar2=0.0,
                                                        op0=ALU.mult, op1=ALU.add),
                      reads=rkeys, writes=wkeys)

        def gcol(gname, j, sign):
            if gname is None:
                return (self.onec if sign > 0 else self.negc)[:, 0:1]
            return (self.gains[gname] if sign > 0 else self.ngains[gname])[:, j:j + 1]

        def piece_item(src, dst, J, gname, sc, ncol, sign, dn, doff, j0=0):
            def emit():
                b = self._wcnt % 2
                self._wcnt += 1
                st, os_ = stage[b], ost[b]
                tr.op('sp', lambda e: e.dma_start(
                    out=st[:, 0:J, 0:ncol], in_=src[j0 * 128:(j0 + J) * 128, sc:sc + ncol].rearrange("(j p) c -> p j c", p=128)),
                    writes=['wst%d' % b], dma=True)
                for j in range(J):
                    cast(os_[:, 0, j, 0:ncol], st[:, j, 0:ncol], gcol(gname, j0 + j, sign), ['wst%d' % b, 'gains'], ['wos%d' % b])
                tr.op('pool', lambda e: e.dma_start(out=dst[:, dn, j0:j0 + J, doff:doff + ncol], in_=os_[:, 0, 0:J, 0:ncol]),
                      reads=['wos%d' % b], writes=['wscr'], dma=True)
            return emit

        def pair_item(src, dst, J, gname, sc, dn, j0):
            def emit():
                b = self._wcnt % 2
                self._wcnt += 1
                st, os_ = stage[b], ost[b]
                tr.op('sp', lambda e: e.dma_start(
                    out=st[:, 0:J, :], in_=src[j0 * 128:(j0 + J) * 128, sc:sc + 256].rearrange("(j p) c -> p j c", p=128)),
                    writes=['wst%d' % b], dma=True)
                for j in range(J):
                    cast(os_[:, :, j, :], st[:, j, :].rearrange("p (g c) -> p g c", g=2), gcol(gname, j0 + j, 1.0),
                         ['wst%d' % b, 'gains'], ['wos%d' % b])
                tr.op('pool', lambda e: e.dma_start(out=dst[:, dn:dn + 2, j0:j0 + J, :], in_=os_[:, :, 0:J, :]),
                      reads=['wos%d' % b], writes=['wscr'], dma=True)
            return emit

        def regular(src, dst, J, gname, src0, nch, dst0):
            its = []
            assert nch % 2 == 0
            for c in range(0, nch, 2):
                for j0 in range(0, J, JD):
                    its.append(pair_item(src, dst, min(JD, J - j0), gname, src0 + 128 * c, dst0 + c, j0))
            return its

        def pieces(src, dst, J, gname, plist):
            return [piece_item(src, dst, J, gname, sc, ncol, sign, dn, doff) for (sc, ncol, sign, dn, doff) in plist]

        tr.op('dve', lambda e: e.memset(self.epsc[:], EPS),
              reads=['g_norm_mix_pre', 'g_norm_ffn_pre', 'g_mla_q_norm', 'g_mla_kv_norm', 'ng_norm_mix_pre', 'ng_mla_q_norm',
                     'onec', 'negc'], writes=['gains', 'epsc'])
        Wd = self.Wd
        pl, _ = win_pieces()
        spec = [p for p in pl if p[1] != 128 or p[4] != 0]
        pre = []
        pre += pieces(I['w_in'], Wd['win'], JD, "norm_mix_pre", spec)
        pre += regular(I['w_in'], Wd['win'], JD, "norm_mix_pre", 512, 2, 4)
        pre += regular(I['w_in'], Wd['win'], JD, "norm_mix_pre", 2880, 2, 23)
        pre += regular(I['w_in'], Wd['win'], JD, "norm_mix_pre", 3136, 2, 25)
        pl, _ = wkv_pieces()
        pre += pieces(I['mla_w_kv_up'], Wd['wkv'], 2, "mla_kv_norm", pl)
        rest = []
        rest += regular(I['w_in'], Wd['win'], JD, "norm_mix_pre", 0, 4, 0)
        rest += regular(I['w_in'], Wd['win'], JD, "norm_mix_pre", 832, 16, 7)
        rest += regular(I['w_in'], Wd['win'], JD, "norm_mix_pre", 3392, 32, 27)
        pl, _ = wq_pieces()
        rest += pieces(I['mla_w_q_up'], Wd['wq'], 4, "mla_q_norm", pl)
        rest += regular(I['w_o_mla'], Wd['womla'], JD, None, 0, 16, 0)
        rest += regular(I['w_o_swa'], Wd['woswa'], JD, None, 0, 16, 0)
        rest += regular(I['w_out'], Wd['wout'], JD, None, 0, 16, 0)
        rest += regular(I['ffn_w_up'], Wd['wup'], JD, "norm_ffn_pre", 0, 88, 0)
        rest += regular(I['ffn_w_down'], Wd['wdown'], JF, None, 0, 16, 0)
        return pre, rest

    def norm_T(self, bufs, blk, T0, dstT, dkey, src_dma=None, pre=None):
        tr = self.tr
        b = blk % 2
        xin = bufs['xin'][b]
        xk = 'xin%d' % b
        if src_dma is not None:
            tr.op('sp', lambda e: e.dma_start(out=xin[:], in_=src_dma), writes=[xk], dma=True)
        if pre is not None:
            pre(xin, xk)
        ss, rs, rstd, xsb = bufs['ss'], bufs['rs'], bufs['rstd'], bufs['xsb'][0]
        tr.op('act', lambda e: e.activation(out=xsb[:], in_=xin[:], func=AF.Square), reads=[xk], writes=['xsb0'])
        tr.op('dve', lambda e: e.tensor_reduce(out=ss[:], in_=xsb[:], axis=AX.X, op=ALU.add), reads=['xsb0'], writes=['ss'])
        tr.op('act', lambda e: e.activation(out=rs[:], in_=ss[:], func=AF.Sqrt, scale=1.0 / D, bias=self.epsc[:]),
              reads=['ss'], writes=['rs'])
        tr.op('dve', lambda e: e.reciprocal(out=rstd[:], in_=rs[:]), reads=['rs'], writes=['rstd'])
        if blk % 2 == 0:
            tr.op('dve', lambda e: e.tensor_scalar(out=xsb[:], in0=xin[:], scalar1=rstd[:], scalar2=None, op0=ALU.mult),
                  reads=[xk, 'rstd'], writes=['xsb0'])
        else:
            tr.op('act', lambda e: e.activation(out=xsb[:], in_=xin[:], func=AF.Copy, scale=rstd[:]),
                  reads=[xk, 'rstd'], writes=['xsb0'])
        for jg in range(4):
            bk = 6 + (jg % 2)
            for jj in range(4):
                j = jg * 4 + jj
                tr.op('pe', lambda e, j=j, jj=jj, bk=bk: e.transpose(out=self.bankb(bk)[:, jj * 128:(jj + 1) * 128],
                                                                    in_=xsb[:, j * 128:(j + 1) * 128], identity=self.ident[:]),
                      reads=['xsb0', 'ident'], writes=['P%d' % bk])
            src = self.bankb(bk, 512).rearrange("p (a c) -> p a c", a=4)
            dst = dstT[:, jg * 4:(jg + 1) * 4, T0:T0 + 128]
            wk = [dkey + str(jg * 4 + jj) for jj in range(4)]
            if jg % 2 == 0:
                tr.op('act', lambda e, src=src, dst=dst: e.activation(out=dst, in_=src, func=AF.Copy),
                      reads=['P%d' % bk], writes=wk)
            else:
                tr.op('dve', lambda e, src=src, dst=dst: e.tensor_copy(out=dst, in_=src), reads=['P%d' % bk], writes=wk)

    def tokbufs(self, les, pfx):
        return dict(xin=[self.sb(pfx + "xin%d" % i, [128, D], F32, les) for i in range(2)],
                    xsb=[self.sb(pfx + "xsb%d" % i, [128, D], BF16, les) for i in range(1)],
                    ss=self.sb(pfx + "ss", [128, 1], F32, les),
                    rs=self.sb(pfx + "rs", [128, 1], F32, les), rstd=self.sb(pfx + "rstd", [128, 1], F32, les))

    def evac(self, idx, out_ap, in_ap, reads, writes):
        if idx % 2 == 0:
            self.tr.op('act', lambda e: e.activation(out=out_ap, in_=in_ap, func=AF.Copy), reads=reads, writes=writes)
        else:
            self.tr.op('dve', lambda e: e.tensor_copy(out=out_ap, in_=in_ap), reads=reads, writes=writes)

    def featnorm(self, pbanks, nj, dim, T, bufs, dst, dkey):
        tr = self.tr
        sqc, rsb, rstdb = bufs['sqc'], bufs['rsb'], bufs['rstdb']
        for j in range(nj):
            tr.op('act', lambda e, j=j: e.activation(out=sqc[j % 2][:, 0:T], in_=self.bank(pbanks[j], T), func=AF.Square),
                  reads=['P%d' % pbanks[j]], writes=['sqc%d' % (j % 2)])
            tr.op('pe', lambda e, j=j: e.matmul(self.bank(5, T), lhsT=self.onesb[:], rhs=sqc[j % 2][:, 0:T],
                                                start=(j == 0), stop=(j == nj - 1)),
                  reads=['sqc%d' % (j % 2), 'onesb'], writes=['P5'])
        tr.op('act', lambda e: e.activation(out=rsb[:, 0:T], in_=self.bank(5, T), func=AF.Sqrt, scale=1.0 / dim, bias=self.epsc[:]),
              reads=['P5'], writes=['rsb'])
        tr.op('dve', lambda e: e.reciprocal(out=rstdb[:, 0:T], in_=rsb[:, 0:T]), reads=['rsb'], writes=['rstdb'])
        for j in range(nj):
            tr.op('dve', lambda e, j=j: e.tensor_tensor(out=dst[:, j, 0:T], in0=self.bank(pbanks[j], T), in1=rstdb[:, 0:T],
                                                        op=ALU.mult),
                  reads=['P%d' % pbanks[j], 'rstdb'], writes=[dkey + str(j)])

    def phase_k(self):
        tr, I, Wd = self.tr, self.I, self.Wd
        with ExitStack() as les:
            sb = lambda n, s, d: self.sb(n, s, d, les)
            bufs = self.tokbufs(les, "k_")
            hT2 = [sb("k_hT%d" % i, [128, JD, 512], BF16) for i in range(2)]
            wk = sb("k_win", [128, 7, JD, 128], BF16)
            wkv = sb("k_wkv", [128, 32, 2, 128], BF16)
            fb = dict(sqc=[sb("k_sqc%d" % i, [128, 512], BF16) for i in range(2)], rsb=sb("k_rsb", [128, 512], F32),
                      rstdb=sb("k_rstdb", [128, 512], F32))
            ckvn = sb("k_ckvn", [128, 2, 512], BF16)
            cs = sb("k_cos", [64, 512], F32)
            sn = sb("k_sin", [64, 512], F32)
            t1 = sb("k_t1", [64, 512], F32)
            t2 = sb("k_t2", [64, 512], F32)
            kro = sb("k_kro", [64, 512], BF16)
            kst = sb("k_kst", [128, 2, 512], BF16)
            vss = sb("k_vss", [128, 4, 4, 65], BF16)
            knst = sb("k_knst", [128, NH, 512], BF16)
            vnst = sb("k_vnst", [128, NH, 4, 129], BF16)
            tr.op('dve', lambda e: e.memset(vss[:], 1.0), writes=['vss'])
            tr.op('pool', lambda e: e.memset(vnst[:], 1.0), writes=['vnst'])
            wpre, wrest = self.phase_w_items(les)
            for it in wpre:
                it()
            self.tr.barrier()
            wpos = 0
            wper = (len(wrest) + self.NKT - 1) // self.NKT
            tr.op('sp', lambda e: e.dma_start(out=wk[:, 0:3], in_=Wd['win'][:, 4:7]), writes=['wk'], dma=True)
            tr.op('sp', lambda e: e.dma_start(out=wk[:, 3:7], in_=Wd['win'][:, 23:27]), writes=['wk'], dma=True)
            tr.op('sp', lambda e: e.dma_start(out=wkv[:], in_=Wd['wkv']), writes=['wkv'], dma=True)
            self._kev = 0

            def proj_gen(kt):
                ev = self._kev
                s0 = kt * 512
                hT = hT2[kt % 2]
                hpf = 'hT%s' % ('a' if kt % 2 == 0 else 'b')
                hk = [hpf + str(j) for j in range(JD)]
                tr.op('sp', lambda e, s0=s0: e.dma_start(out=cs[:], in_=I['cos128'][0:64, s0:s0 + 512]), writes=['cs'], dma=True)
                tr.op('sp', lambda e, s0=s0: e.dma_start(out=sn[:], in_=I['sin128'][0:64, s0:s0 + 512]), writes=['sn'], dma=True)
                for c in range(2):
                    for j in range(JD):
                        tr.op('pe', lambda e, c=c, j=j: e.matmul(self.bank(c), lhsT=wk[:, c, j, :], rhs=hT[:, j, :],
                                                                 start=(j == 0), stop=(j == JD - 1)),
                              reads=['wk', hk[j]], writes=['P%d' % c])
                self.featnorm([0, 1], 2, 256.0, 512, fb, ckvn, 'ckvn')
                yield
                for c in range(2):
                    for j in range(JD):
                        tr.op('pe', lambda e, c=c, j=j: e.matmul(self.bank(2 + c)[0:64, :], lhsT=wk[:, 2, j, c * 64:(c + 1) * 64],
                                                                 rhs=hT[:, j, :], start=(j == 0), stop=(j == JD - 1)),
                              reads=['wk', hk[j]], writes=['P%d' % (2 + c)])
                tr.op('dve', lambda e: e.tensor_tensor(out=t1[:], in0=self.bank(2)[0:64, :], in1=cs[:], op=ALU.mult),
                      reads=['P2', 'cs'], writes=['t1'])
                tr.op('dve', lambda e: e.tensor_tensor(out=t2[:], in0=self.bank(3)[0:64, :], in1=sn[:], op=ALU.mult),
                      reads=['P3', 'sn'], writes=['t2'])
                tr.op('pool', lambda e: e.tensor_tensor(out=kro[:], in0=t1[:], in1=t2[:], op=ALU.add), reads=['t1', 't2'], writes=['kro'])
                tr.op('pool', lambda e, s0=s0: e.dma_start(out=self.KR[:, s0:s0 + 512], in_=kro[:]), reads=['kro'],
                      writes=['KR%d' % kt], dma=True)
                for c in range(2):
                    for j in range(JD):
                        tr.op('pe', lambda e, c=c, j=j: e.matmul(self.bank(2 + c), lhsT=wk[:, 3 + c, j, :], rhs=hT[:, j, :],
                                                                 start=(j == 0), stop=(j == JD - 1)),
                              reads=['wk', hk[j]], writes=['P%d' % (2 + c)])
                    self.evac(ev, kst[:, c, :], self.bank(2 + c), ['P%d' % (2 + c)], ['kst'])
                    ev += 1
                tr.op('pool', lambda e, s0=s0: e.dma_start(out=self.KS[:, s0:s0 + 512].rearrange("(c p) s -> p c s", p=128), in_=kst[:]),
                      reads=['kst'], writes=['KS%d' % kt], dma=True)
                for blk in range(4):
                    bk = 4 if blk % 2 == 0 else 2
                    for j in range(JD):
                        tr.op('pe', lambda e, blk=blk, j=j, bk=bk: e.matmul(
                            self.bank(bk, 256).rearrange("p (a c) -> p a c", a=2), lhsT=hT[:, j, blk * 128:(blk + 1) * 128],
                            rhs=wk[:, 5:7, j, :], start=(j == 0), stop=(j == JD - 1)),
                            reads=['wk', hk[j]], writes=['P%d' % bk])
                    self.evac(ev, vss[:, blk, :, 0:64], self.bank(bk, 256).rearrange("p (g c) -> p g c", g=4), ['P%d' % bk], ['vss'])
                    ev += 1
                tr.op('pool', lambda e, kt=kt: e.dma_start(out=self.VS[:, 4 * kt:4 * kt + 4], in_=vss[:]), reads=['vss'],
                      writes=['VS%d' % kt], dma=True)
                yield
                cj = ['ckvn0', 'ckvn1']
                for h in range(NH):
                    bk = 2 + (h % 4)
                    for j in range(2):
                        tr.op('pe', lambda e, h=h, j=j, bk=bk: e.matmul(self.bank(bk), lhsT=wkv[:, h, j, :], rhs=ckvn[:, j, :],
                                                                        start=(j == 0), stop=(j == 1)),
                              reads=['wkv', cj[j]], writes=['P%d' % bk])
                    self.evac(ev, knst[:, h, :], self.bank(bk), ['P%d' % bk], ['knst'])
                    ev += 1
                tr.op('pool', lambda e, s0=s0: e.dma_start(out=self.KN[:, :, s0:s0 + 512].rearrange("h d s -> d h s"), in_=knst[:]),
                      reads=['knst'], writes=['KN%d' % kt], dma=True)
                yield
                i = 0
                for blk in range(4):
                    for hg in range(4):
                        bk = 2 + (i % 4)
                        i += 1
                        for j in range(2):
                            tr.op('pe', lambda e, blk=blk, hg=hg, j=j, bk=bk: e.matmul(
                                self.bank(bk).rearrange("p (a c) -> p a c", a=4), lhsT=ckvn[:, j, blk * 128:(blk + 1) * 128],
                                rhs=wkv[:, 16 + 4 * hg:20 + 4 * hg, j, :], start=(j == 0), stop=(j == 1)),
                                reads=['wkv', cj[j]], writes=['P%d' % bk])
                        self.evac(ev, vnst[:, 4 * hg:4 * hg + 4, blk, 0:128], self.bank(bk).rearrange("p (a c) -> p a c", a=4),
                                  ['P%d' % bk], ['vnst'])
                        ev += 1
                tr.op('pool', lambda e, kt=kt: e.dma_start(out=self.VN[:, :, 4 * kt:4 * kt + 4, :].rearrange("h k b c -> k h b c"),
                                                          in_=vnst[:]),
                      reads=['vnst'], writes=['VN%d' % kt], dma=True)
                self._kev = ev

            for kt in range(self.NKT + 1):
                g = proj_gen(kt - 1) if kt >= 1 else None
                for blk in range(4):
                    if kt < self.NKT:
                        self.norm_T(bufs, blk, blk * 128, hT2[kt % 2], 'hT%s' % ('a' if kt % 2 == 0 else 'b'),
                                    src_dma=I['xs'][kt * 512 + blk * 128: kt * 512 + (blk + 1) * 128, :])
                    if g is not None:
                        next(g, None)
                for it in wrest[wpos:wpos + wper]:
                    it()
                wpos += wper
            for it in wrest[wpos:]:
                it()
            self.tr.barrier()

    def _ffn_chunk(self, tr, cur, sub, wn, T, halo, hT, hk, aext, sg, gT):
        isb = wn >= JF
        jc = wn % JF
        bk = (0 + 2 * (jc % 2)) + (1 if isb else 0)
        wv, wkey = cur[0][:, sub], cur[1]
        for j in range(JD):
            tr.op('pe', lambda e, j=j, wv=wv, bk=bk: e.matmul(self.bank(bk, T), lhsT=wv[:, j, :], rhs=hT[:, j, 0:T],
                                                              start=(j == 0), stop=(j == JD - 1)),
                  reads=[wkey, hk[j]], writes=['P%d' % bk])
        if not isb:
            ax = aext[jc % 2]
            ak = 'aext%d' % (jc % 2)
            if not halo:
                tr.op('pool', lambda e, ax=ax, jc=jc: e.tensor_copy(out=ax[:, 0:2], in_=self.ahalo[:, jc, :]),
                      reads=['ahalo%d' % jc], writes=[ak])
                tr.op('act', lambda e, ax=ax, bk=bk: e.activation(out=ax[:, 2:2 + T], in_=self.bank(bk, T), func=AF.Copy),
                      reads=['P%d' % bk], writes=[ak])
            tr.op('act', lambda e, jc=jc, bk=bk: e.activation(out=self.ahalo[:, jc, :], in_=self.bank(bk)[:, T - 2:T], func=AF.Copy),
                  reads=['P%d' % bk], writes=['ahalo%d' % jc])
        else:
            ax = aext[jc % 2]
            ak = 'aext%d' % (jc % 2)
            tt = sg[jc % 2]
            tk = 'sg%d' % (jc % 2)
            tr.op('dve', lambda e, ax=ax, tt=tt, jc=jc: e.tensor_scalar(out=tt[:, 0:T], in0=ax[:, 2:2 + T], scalar1=self.cw[:, 2, jc:jc + 1],
                                                                        scalar2=self.cb[:, jc:jc + 1], op0=ALU.mult, op1=ALU.add),
                  reads=[ak, 'cw', 'cb'], writes=[tk])
            tr.op('dve', lambda e, ax=ax, tt=tt, jc=jc: e.scalar_tensor_tensor(out=tt[:, 0:T], in0=ax[:, 1:1 + T], scalar=self.cw[:, 1, jc:jc + 1],
                                                                               in1=tt[:, 0:T], op0=ALU.mult, op1=ALU.add),
                  reads=[ak, tk, 'cw'], writes=[tk])
            tr.op('dve', lambda e, ax=ax, tt=tt, jc=jc: e.scalar_tensor_tensor(out=tt[:, 0:T], in0=ax[:, 0:T], scalar=self.cw[:, 0, jc:jc + 1],
                                                                               in1=tt[:, 0:T], op0=ALU.mult, op1=ALU.add),
                  reads=[ak, tk, 'cw'], writes=[tk])
            tr.op('act', lambda e, tt=tt: e.activation(out=tt[:, 0:T], in_=tt[:, 0:T], func=AF.Gelu_apprx_tanh),
                  reads=[tk], writes=[tk])
            tr.op('dve', lambda e, tt=tt, jc=jc, bk=bk: e.tensor_tensor(out=gT[:, jc, 0:T], in0=self.bank(bk, T), in1=tt[:, 0:T], op=ALU.mult),
                  reads=['P%d' % bk, tk], writes=['gT%d' % jc])


    def phase_q(self):
        tr, I, Wd = self.tr, self.I, self.Wd
        KC = self.KC
        with ExitStack() as les:
            sb = lambda n, s, d: self.sb(n, s, d, les)
            bufs = self.tokbufs(les, "q_")
            hT = sb("q_hT", [128, JD, 512], BF16)
            R = sb("q_R", [128, 32768], BF16)
            qnT = R[:, 0:8192].rearrange("p (a t) -> p a t", a=16)
            qsT = R[:, 8192:16384].rearrange("p (a t) -> p a t", a=16)
            oaT = qsT
            obT = R[:, 16384:24576].rearrange("p (a t) -> p a t", a=16)
            qrT = R[:, 24576:32768].rearrange("p (a t) -> p a t", a=16)
            tr.op('pool', lambda e: e.memset(qrT, 0.0), writes=['qr%d' % i for i in range(16)])
            gT = R[:, 0:JF * 512].rearrange("p (a t) -> p a t", a=JF)
            mT = qnT
            A = sb("q_A", [128, 3 * (2 * KC * 128 + KC * 129)], BF16)
            asz = 2 * KC * 128 + KC * 129
            ygT = A[:, 0:8192].rearrange("p (a t) -> p a t", a=16) if 3 * asz >= 8192 else sb("q_ygT", [128, JD, 512], BF16)
            knc = [A[:, i * asz: i * asz + KC * 128] for i in range(3)]
            krc = [A[:, i * asz + KC * 128: i * asz + 2 * KC * 128] for i in range(3)]
            vac = [A[:, i * asz + 2 * KC * 128: (i + 1) * asz].rearrange("p (b c) -> p b c", c=129) for i in range(3)]
            wr = [sb("q_wr%d" % i, [128, 5632], BF16) for i in range(2)]
            fb = dict(sqc=[sb("q_sqc%d" % i, [128, 512], BF16) for i in range(2)], rsb=sb("q_rsb", [128, 512], F32),
                      rstdb=sb("q_rstdb", [128, 512], F32))
            cqn = sb("q_cqn", [128, 4, 512], BF16)
            cs = sb("q_cos", [128, 512], F32)
            sn = sb("q_sin", [128, 512], F32)
            pT = [sb("q_pT%d" % i, [128, 512], BF16) for i in range(3)]
            otok = sb("q_otok", [128, 4, 128], BF16)
            dd = sb("q_dd", [128, 4], F32)
            rden = sb("q_rden", [128, 4], F32)
            ksT2 = sb("q_ksT2", [128, 2, 4, 640], BF16)
            tr.op('pool', lambda e: e.memset(ksT2[:], 0.0), writes=['ksT2'])
            vsA = sb("q_vsA", [128, 5, 4, 65], BF16)
            pTs = [sb("q_pTs%d" % i, [128, 1024], BF16) for i in range(2)]
            obtok = [sb("q_obtok%d" % i, [128, 4, 128], BF16) for i in range(2)]
            eh = [sb("q_eh%d" % i, [128, 2, 128], BF16) for i in range(2)]
            sg = [sb("q_sg%d" % i, [128, 512], F32) for i in range(2)]
            t1, t2 = sg
            ysq = fb['sqc']
            rtok2 = sb("q_rtok2", [128, 8], F32)
            aext = [sb("q_aext%d" % i, [128, 516], F32) for i in range(2)]
            self._wl = 0
            self._ev = 0

            def wload(wname, n0, g, J):
                b = self._wl % 2
                self._wl += 1
                v = wr[b][:, 0:g * J * 128].rearrange("p (g j c) -> p g j c", g=g, j=J)
                tr.op('sp', lambda e: e.dma_start(out=v, in_=Wd[wname][:, n0:n0 + g]), writes=['wr%d' % b], dma=True)
                return v, 'wr%d' % b

            def proj_groups(wname, chunks, J, gmax):
                groups = []
                i = 0
                while i < len(chunks):
                    g = 1
                    while g < gmax and i + g < len(chunks) and chunks[i + g] == chunks[i] + g:
                        g += 1
                    groups.append((chunks[i], g))
                    i += g
                pend = wload(wname, groups[0][0], groups[0][1], J)
                for gi, (n0, g) in enumerate(groups):
                    cur = pend
                    pend = wload(wname, groups[gi + 1][0], groups[gi + 1][1], J) if gi + 1 < len(groups) else None
                    for k in range(g):
                        yield n0 + k, cur[0][:, k], cur[1]

            def post_norm(T, nb, gname, res_src, res_dst_fn, last):
                rsb, rstdb = fb['rsb'], fb['rstdb']
                tr.op('act', lambda e: e.activation(out=rsb[:, 0:T], in_=self.bank(5, T), func=AF.Sqrt, scale=1.0 / D, bias=self.epsc[:]),
                      reads=['P5'], writes=['rsb'])
                tr.op('dve', lambda e: e.reciprocal(out=rstdb[:, 0:T], in_=rsb[:, 0:T]), reads=['rsb'], writes=['rstdb'])
                if self.stop_after == 'q8b':
                    return
                tr.op('sp', lambda e: e.dma_start(out=self.RS[0:1, 0:T], in_=rstdb[0:1, 0:T]), reads=['rstdb'], writes=['RS'], dma=True)
                tr.op('sp', lambda e: e.dma_start(out=rtok2[:, 0:nb], in_=self.RS[0, 0:T].rearrange("(b p) -> p b", p=128),
                                                  allow_slow_non_contiguous=True), reads=['RS'], writes=['rtok2'], dma=True)
                if self.stop_after == 'q8c':
                    return
                for blk in range(nb):
                    b = blk % 2
                    xin = bufs['xin'][b]
                    xk = 'xin%d' % b
                    tr.op('sp', lambda e, blk=blk, xin=xin: e.dma_start(out=xin[:], in_=res_src(blk)), writes=[xk], dma=True)
                    for mg in range(4):
                        bk = 6 + (mg % 2)
                        for mm in range(4):
                            m = mg * 4 + mm
                            tr.op('pe', lambda e, m=m, mm=mm, bk=bk, blk=blk: e.transpose(
                                out=self.bankb(bk)[:, mm * 128:(mm + 1) * 128], in_=ygT[:, m, blk * 128:(blk + 1) * 128],
                                identity=self.ident[:]), reads=['yg%d' % m, 'ident'], writes=['P%d' % bk])
                        tr.op('dve', lambda e, mg=mg, bk=bk, blk=blk, xin=xin: e.scalar_tensor_tensor(
                            out=xin[:, mg * 512:(mg + 1) * 512], in0=self.bankb(bk, 512), scalar=rtok2[:, blk:blk + 1],
                            in1=xin[:, mg * 512:(mg + 1) * 512], op0=ALU.mult, op1=ALU.add),
                            reads=['P%d' % bk, 'rtok2', xk], writes=[xk])
                    if self.stop_after == 'q8d':
                        continue
                    res_dst_fn(blk, xin, xk)

            def y_chunk(m, ps_bank, T, nb, gname):
                pk = 'P%d' % ps_bank
                tr.op('act', lambda e: e.activation(out=ysq[m % 2][:, 0:T], in_=self.bank(ps_bank, T), func=AF.Square),
                      reads=[pk], writes=['sqc%d' % (m % 2)])

                tr.op('dve', lambda e: e.tensor_scalar(out=ygT[:, m, 0:T], in0=self.bank(ps_bank, T),
                                                       scalar1=self.gains[gname][:, m:m + 1], scalar2=None, op0=ALU.mult),
                      reads=[pk, 'g_' + gname], writes=['yg%d' % m])
                tr.op('pe', lambda e: e.matmul(self.bank(5, T), lhsT=self.onesb[:], rhs=ysq[m % 2][:, 0:T],
                                               start=(m == 0), stop=(m == JD - 1)),
                      reads=['sqc%d' % (m % 2), 'onesb'], writes=['P5'])

            tiles = [(self.OWN0 - 1, 1, True)] + [(self.OWN0 + 4 * i, 4, False) for i in range(self.NT)]
            for (B0, nb, halo) in tiles:
                T = nb * 128
                s0 = B0 * 128
                r0 = s0 - self.OWN0 * 128
                hk = ['hT%d' % j for j in range(JD)]
                for blk in range(nb):
                    self.norm_T(bufs, blk, blk * 128, hT, 'hT', src_dma=I['xs'][s0 + blk * 128: s0 + (blk + 1) * 128, :])
                tr.op('sp', lambda e, s0=s0, T=T: e.dma_start(out=cs[:, 0:T], in_=I['cos128'][:, s0:s0 + T]), writes=['cs'], dma=True)
                tr.op('sp', lambda e, s0=s0, T=T: e.dma_start(out=sn[:, 0:T], in_=I['sin128'][:, s0:s0 + T]), writes=['sn'], dma=True)
                for n, wv, wkey in proj_groups('win', [0, 1, 2, 3], JD, 2):
                    for j in range(JD):
                        tr.op('pe', lambda e, n=n, j=j, wv=wv: e.matmul(self.bank(n, T), lhsT=wv[:, j, :], rhs=hT[:, j, 0:T],
                                                                        start=(j == 0), stop=(j == JD - 1)),
                              reads=[wkey, hk[j]], writes=['P%d' % n])
                self.featnorm([0, 1, 2, 3], 4, 512.0, T, fb, cqn, 'cqn')
                cqk = ['cqn%d' % j for j in range(4)]
                i = 0
                for n, wv, wkey in proj_groups('wq', list(range(16)), 4, 8):
                    bk = i % 4
                    i += 1
                    for j in range(4):
                        tr.op('pe', lambda e, j=j, wv=wv, bk=bk: e.matmul(self.bank(bk, T), lhsT=wv[:, j, :], rhs=cqn[:, j, 0:T],
                                                                          start=(j == 0), stop=(j == 3)),
                              reads=[wkey, cqk[j]], writes=['P%d' % bk])
                    self.evac(self._ev, qnT[:, n, 0:T], self.bank(bk, T), ['P%d' % bk], ['qn%d' % n])
                    self._ev += 1
                order = []
                for hp in range(8):
                    order += [16 + hp, 24 + hp]
                for n, wv, wkey in proj_groups('wq', order, 4, 1):
                    hp = (n - 16) % 8
                    rot = n >= 24
                    bk = 1 if rot else 0
                    for j in range(4):
                        tr.op('pe', lambda e, j=j, wv=wv, bk=bk: e.matmul(self.bank(bk, T), lhsT=wv[:, j, :], rhs=cqn[:, j, 0:T],
                                                                          start=(j == 0), stop=(j == 3)),
                              reads=[wkey, cqk[j]], writes=['P%d' % bk])
                    if not rot:
                        tr.op('dve', lambda e: e.tensor_tensor(out=t1[:, 0:T], in0=self.bank(0, T), in1=cs[:, 0:T], op=ALU.mult),
                              reads=['P0', 'cs'], writes=['sg0'])
                    else:
                        tr.op('dve', lambda e: e.tensor_tensor(out=t2[:, 0:T], in0=self.bank(1, T), in1=sn[:, 0:T], op=ALU.mult),
                              reads=['P1', 'sn'], writes=['sg1'])
                        for hh in range(2):
                            tr.op('pool', lambda e, hp=hp, hh=hh: e.tensor_tensor(
                                out=qrT[hh * 64:(hh + 1) * 64, 2 * hp + hh, 0:T], in0=t1[hh * 64:(hh + 1) * 64, 0:T],
                                in1=t2[hh * 64:(hh + 1) * 64, 0:T], op=ALU.add),
                                reads=['sg0', 'sg1'], writes=['qr%d' % (2 * hp + hh)])
                i = 0
                for n, wv, wkey in proj_groups('win', list(range(7, 23)), JD, 2):
                    bk = i % 4
                    i += 1
                    for j in range(JD):
                        tr.op('pe', lambda e, j=j, wv=wv, bk=bk: e.matmul(self.bank(bk, T), lhsT=wv[:, j, :], rhs=hT[:, j, 0:T],
                                                                          start=(j == 0), stop=(j == JD - 1)),
                              reads=[wkey, hk[j]], writes=['P%d' % bk])
                    self.evac(self._ev, qsT[:, n - 7, 0:T], self.bank(bk, T), ['P%d' % bk], ['qs%d' % (n - 7)])
                    self._ev += 1
                if self.stop_after == 'q4':
                    self.tr.barrier()
                    return
                nkb = nb + 1
                for half in range(2):
                    tr.op('sp', lambda e, half=half: e.dma_start(
                        out=ksT2[half * 64:(half + 1) * 64, half, :, 0:nkb * 128],
                        in_=self.KS[:, (B0 - 1) * 128:(B0 + nb) * 128].rearrange("(g d) s -> d g s", d=64)),
                        reads=['KS%d' % k for k in range((B0 - 1) // 4, (B0 + nb - 1) // 4 + 1)], writes=['ksT2'], dma=True)
                tr.op('sp', lambda e: e.dma_start(out=vsA[:, 0:nkb], in_=self.VS[:, B0 - 1:B0 + nb]),
                      reads=['VS%d' % k for k in range((B0 - 1) // 4, (B0 + nb - 1) // 4 + 1)], writes=['vsA'], dma=True)
                def swa_gen(h):
                    g = h // 8
                    hp, ho = h // 2, (h % 2) * 64
                    sbk = 0 if h % 2 == 0 else 2
                    SP_ = self.PS[:, sbk * 512: sbk * 512 + 2 * nb * 128]
                    skeys = ['P%d' % sbk, 'P%d' % (sbk + 1)]
                    obk = 4 + (h % 2)
                    mmlist = []
                    for kb in range(nkb):
                        if kb == 0:
                            mmlist.append((kb, 0, 0, 1))
                        elif kb == nb:
                            mmlist.append((kb, 2 * nb - 1, nb - 1, 1))
                        else:
                            seg0 = 2 * kb - 1
                            if (seg0 * 128) // 512 != ((seg0 + 2) * 128 - 1) // 512:
                                mmlist.append((kb, seg0, kb - 1, 1))
                                mmlist.append((kb, seg0 + 1, kb, 1))
                            else:
                                mmlist.append((kb, seg0, kb - 1, 2))
                    for (kb, seg, qb, nq) in mmlist:
                        tr.op('pe', lambda e, kb=kb, seg=seg, qb=qb, nq=nq: e.matmul(
                            SP_[:, seg * 128:(seg + nq) * 128], lhsT=ksT2[:, h % 2, g, kb * 128:(kb + 1) * 128],
                            rhs=qsT[:, hp, qb * 128:(qb + nq) * 128], start=True, stop=True),
                            reads=['ksT2', 'qs%d' % hp], writes=skeys)
                    pt = pTs[h % 2]
                    pk = 'pTs%d' % (h % 2)
                    tr.op('act', lambda e, pt=pt: e.activation(out=pt[:, 0:128], in_=SP_[:, 0:128], func=AF.Exp, scale=SWA_SCALE,
                                                               bias=self.kmask[:, B0 - 1:B0]),
                          reads=skeys + ['kmask'], writes=[pk])
                    tr.op('act', lambda e, pt=pt: e.activation(out=pt[:, 128:2 * nb * 128], in_=SP_[:, 128:2 * nb * 128], func=AF.Exp,
                                                               scale=SWA_SCALE), reads=skeys, writes=[pk])
                    pv = pt[:, 0:2 * nb * 128].rearrange("p (b r q) -> p b r q", b=nb, r=2)
                    ehh = eh[h % 2]
                    tr.op('sp', lambda e, ehh=ehh, h=h: e.dma_start(out=ehh[:].rearrange("p r q -> p (r q)"), in_=self.ED[:, h, :]),
                          writes=['eh%d' % (h % 2)], dma=True)
                    ev_ = ehh[:].unsqueeze(1).to_broadcast([128, nb, 2, 128])
                    tr.op('pool' if h % 2 == 0 else 'dve',
                          lambda e, pv=pv, ev_=ev_: e.tensor_tensor(out=pv, in0=pv, in1=ev_, op=ALU.mult),
                          reads=[pk, 'eh%d' % (h % 2)], writes=[pk])
                    yield
                    OB = self.bank(obk, nb * 65).rearrange("p (b c) -> p b c", c=65)
                    first = True
                    for qb in range(nb):
                        for r in range(2):
                            tr.op('pe', lambda e, qb=qb, r=r, first=first, pv=pv: e.matmul(
                                OB[:, qb, :], lhsT=pv[:, qb, r, :], rhs=vsA[:, qb + r, g, :], start=first,
                                stop=(qb == nb - 1 and r == 1), skip_group_check=True),
                                reads=[pk, 'vsA'], writes=['P%d' % obk])
                            first = False
                    tr.op('dve', lambda e, OB=OB, h=h: e.tensor_scalar(out=dd[:, 0:nb], in0=OB[:, :, 64], scalar1=self.expsink[:, h:h + 1],
                                                                       scalar2=None, op0=ALU.add),
                          reads=['P%d' % obk, 'expsink'], writes=['dd'])
                    tr.op('dve', lambda e: e.reciprocal(out=rden[:, 0:nb], in_=dd[:, 0:nb]), reads=['dd'], writes=['rden'])
                    obt = obtok[hp % 2]
                    tr.op('dve', lambda e, OB=OB, obt=obt: e.tensor_tensor(
                        out=obt[:, 0:nb, ho:ho + 64], in0=OB[:, :, 0:64], in1=rden[:, 0:nb].unsqueeze(2).to_broadcast([128, nb, 64]),
                        op=ALU.mult), reads=['P%d' % obk, 'rden'], writes=['obtok%d' % (hp % 2)])
                    if h % 2 == 1:
                        tbk = 6 + (hp % 2)
                        for qb in range(nb):
                            tr.op('pe', lambda e, qb=qb, obt=obt, tbk=tbk: e.transpose(out=self.bankb(tbk)[:, qb * 128:(qb + 1) * 128],
                                                                                       in_=obt[:, qb, :], identity=self.ident[:]),
                                  reads=['obtok%d' % (hp % 2), 'ident'], writes=['P%d' % tbk])
                        tr.op('act', lambda e, hp=hp, tbk=tbk: e.activation(out=obT[:, hp, 0:T], in_=self.bankb(tbk, T), func=AF.Copy),
                              reads=['P%d' % tbk], writes=['ob%d' % hp])
                sgens = [swa_gen(h) for h in range(NSH)]
                next(sgens[0])
                for h in range(NSH):
                    if h + 1 < NSH:
                        next(sgens[h + 1])
                    next(sgens[h], None)
                if self.stop_after == 'q5':
                    self.tr.barrier()
                    return
                nkbm = B0 + nb
                nch = (nkbm + KC - 1) // KC
                self._kv = getattr(self, '_kv', 0)

                def kvload(h, ci):
                    sl = self._kv % 3
                    self._kv += 1
                    k0 = ci * KC
                    kn_ = min(KC, nkbm - k0)
                    kts = list(range(k0 // 4, (k0 + kn_ - 1) // 4 + 1))
                    tr.op('sp', lambda e: e.dma_start(out=knc[sl][:, 0:kn_ * 128], in_=self.KN[h, :, k0 * 128:(k0 + kn_) * 128]),
                          reads=['KN%d' % k for k in kts], writes=['knc%d' % sl], dma=True)
                    for half in range(2):
                        tr.op('sp', lambda e, half=half: e.dma_start(out=krc[sl][half * 64:(half + 1) * 64, 0:kn_ * 128],
                                                                     in_=self.KR[:, k0 * 128:(k0 + kn_) * 128]),
                              reads=['KR%d' % k for k in kts], writes=['krc%d' % sl], dma=True)
                    tr.op('sp', lambda e: e.dma_start(out=vac[sl][:, 0:kn_, :], in_=self.VN[h, :, k0:k0 + kn_, :]),
                          reads=['VN%d' % k for k in kts], writes=['vac%d' % sl], dma=True)
                    return sl, k0, kn_

                loads = [(h, ci) for h in range(NH) for ci in range(nch)]
                loaded = {}

                def ensure(gi):
                    if gi < len(loads) and gi not in loaded:
                        loaded[gi] = kvload(*loads[gi])

                ensure(0)
                self._pt = getattr(self, '_pt', 0)
                for h in range(NH):
                    hp, ho = h // 2, (h % 2) * 64
                    oa, obb = (2, 3) if h % 2 == 0 else (4, 5)
                    obanks = [oa, obb]
                    started = [False, False]
                    steps = []
                    for ci in range(nch):
                        k0 = ci * KC
                        for kbl in range(min(KC, nkbm - k0)):
                            steps.append((h * nch + ci, kbl, k0 + kbl))

                    def s_step(si):
                        gi, kbl, kb = steps[si]
                        ensure(gi)
                        if kbl == 0:
                            ensure(gi + 1)
                        sl = loaded[gi][0]
                        r = max(0, kb - B0)
                        c0 = r * 128
                        sbk = (0, 1, 7)[si % 3]
                        tr.op('pe', lambda e: e.matmul(self.bank(sbk)[:, c0:T], lhsT=knc[sl][:, kbl * 128:(kbl + 1) * 128],
                                                       rhs=qnT[:, h, c0:T], start=True, stop=False),
                              reads=['knc%d' % sl, 'qn%d' % h], writes=['P%d' % sbk])
                        tr.op('pe', lambda e: e.matmul(self.bank(sbk)[:, c0:T], lhsT=krc[sl][:, kbl * 128:(kbl + 1) * 128],
                                                       rhs=qrT[:, h, c0:T], start=False, stop=True),
                              reads=['krc%d' % sl, 'qr%d' % h], writes=['P%d' % sbk])

                    def e_step(si):
                        gi, kbl, kb = steps[si]
                        sl = loaded[gi][0]
                        r = max(0, kb - B0)
                        c0 = r * 128
                        sbk = (0, 1, 7)[si % 3]
                        pi = self._pt % 3
                        self._pt += 1
                        p = pT[pi]
                        tr.op('act', lambda e: e.activation(out=p[:, c0:T], in_=self.bank(sbk)[:, c0:T], func=AF.Exp, scale=MLA_SCALE,
                                                            bias=self.kmask[:, kb:kb + 1]),
                              reads=['P%d' % sbk, 'kmask'], writes=['pT%d' % pi])
                        if kb >= B0:
                            tr.op('pool', lambda e: e.affine_select(out=p[:, c0:c0 + 128], in_=p[:, c0:c0 + 128], pattern=[[1, 128]],
                                                                    compare_op=ALU.is_ge, fill=0.0, base=0, channel_multiplier=-1),
                                  reads=['pT%d' % pi], writes=['pT%d' % pi])
                        return pi

                    def v_step(si, pi):
                        gi, kbl, kb = steps[si]
                        sl = loaded[gi][0]
                        r = max(0, kb - B0)
                        p = pT[pi]
                        for qs in range(r, nb):
                            ob_ = obanks[qs // 2]
                            st = not started[qs // 2]
                            started[qs // 2] = True
                            last = (kb == B0 + qs) and (qs % 2 == 1 or qs == nb - 1)
                            tr.op('pe', lambda e, qs=qs, ob_=ob_, st=st, last=last: e.matmul(
                                self.bank(ob_)[:, (qs % 2) * 129:(qs % 2) * 129 + 129], lhsT=p[:, qs * 128:(qs + 1) * 128],
                                rhs=vac[sl][:, kbl, :], start=st, stop=last, skip_group_check=True),
                                reads=['pT%d' % pi, 'vac%d' % sl], writes=['P%d' % ob_])

                    ns = len(steps)
                    s_step(0)
                    if ns > 1:
                        s_step(1)
                    for si in range(ns):
                        if si + 2 < ns:
                            s_step(si + 2)
                        pi = e_step(si)
                        v_step(si, pi)
                    for bi in range((nb + 1) // 2):
                        nq = min(2, nb - 2 * bi)
                        OV = self.bank(obanks[bi], 2 * 129).rearrange("p (q c) -> p q c", c=129)
                        tr.op('dve', lambda e, OV=OV, bi=bi, nq=nq: e.tensor_scalar(out=dd[:, 2 * bi:2 * bi + nq], in0=OV[:, 0:nq, 128],
                                                                                    scalar1=1e-30, scalar2=None, op0=ALU.add),
                              reads=['P%d' % obanks[bi]], writes=['dd'])
                        tr.op('dve', lambda e, bi=bi, nq=nq: e.reciprocal(out=rden[:, 2 * bi:2 * bi + nq], in_=dd[:, 2 * bi:2 * bi + nq]),
                              reads=['dd'], writes=['rden'])
                        tr.op('dve', lambda e, OV=OV, bi=bi, nq=nq: e.tensor_tensor(
                            out=otok[:, 2 * bi:2 * bi + nq, :], in0=OV[:, 0:nq, 0:128],
                            in1=rden[:, 2 * bi:2 * bi + nq].unsqueeze(2).to_broadcast([128, nq, 128]), op=ALU.mult),
                            reads=['P%d' % obanks[bi], 'rden'], writes=['otok'])
                    tbk = 6
                    for qb in range(nb):
                        tr.op('pe', lambda e, qb=qb, tbk=tbk: e.transpose(out=self.bankb(tbk)[:, qb * 128:(qb + 1) * 128], in_=otok[:, qb, :],
                                                                         identity=self.ident[:]),
                              reads=['otok', 'ident'], writes=['P%d' % tbk])
                    tr.op('act', lambda e, h=h, tbk=tbk: e.activation(out=oaT[:, h, 0:T], in_=self.bankb(tbk, T), func=AF.Copy),
                          reads=['P%d' % tbk] + ['qs%d' % h], writes=['oa%d' % h, 'qs%d' % h])
                if self.debug and not halo and B0 == self.OWN0:
                    for nm, src in [("dbg_oa", oaT), ("dbg_ob", obT), ("dbg_qn", qnT), ("dbg_hT", hT[:])]:
                        dt_ = self.dram_scr(nm, [128, 16, 512], BF16)
                        tr.op('sp', lambda e, dt_=dt_, src=src: e.dma_start(out=dt_, in_=src), reads=['oa%d' % i for i in range(16)] + ['ob%d' % i for i in range(16)], dma=True)
                    dt_ = self.dram_scr("dbg_qr", [128, 16, 512], BF16)
                    tr.op('sp', lambda e, dt_=dt_: e.dma_start(out=dt_, in_=qrT), dma=True)
                self.tr.barrier()
                if self.stop_after == 'q6':
                    self.tr.barrier()
                    return
                wl = []
                for n2 in range(8):
                    wl.append(('womla', 2 * n2))
                    wl.append(('woswa', 2 * n2))
                    wl.append(('win', 27 + 2 * n2))
                    wl.append(('win', 43 + 2 * n2))
                pend = wload(wl[0][0], wl[0][1], 2, JD)
                for wi, (wname, wn) in enumerate(wl):
                    cur = pend
                    pend = wload(wl[wi + 1][0], wl[wi + 1][1], 2, JD) if wi + 1 < len(wl) else None
                    kind = wi % 4
                    for sub in range(2):
                        n = 2 * (wi // 4) + sub
                        bk = (n % 2) * 4 + kind
                        wv, wkey = cur[0][:, sub], cur[1]
                        for j in range(JD):
                            if kind == 0:
                                rhs, rk = oaT[:, j, 0:T], 'oa%d' % j
                            elif kind == 1:
                                rhs, rk = obT[:, j, 0:T], 'ob%d' % j
                            else:
                                rhs, rk = hT[:, j, 0:T], hk[j]
                            tr.op('pe', lambda e, j=j, wv=wv, bk=bk, rhs=rhs: e.matmul(self.bank(bk, T), lhsT=wv[:, j, :], rhs=rhs,
                                                                                       start=(j == 0), stop=(j == JD - 1)),
                                  reads=[wkey, rk], writes=['P%d' % bk])
                    if kind != 3:
                        continue
                    for sub in range(2):
                        n = 2 * (wi // 4) + sub
                        b0 = 0 if n % 2 == 0 else 4
                        tr.op('act', lambda e, b0=b0: e.activation(out=sg[0][:, 0:T], in_=self.bank(b0 + 2, T), func=AF.Sigmoid),
                              reads=['P%d' % (b0 + 2)], writes=['sg0'])
                        tr.op('act', lambda e, b0=b0: e.activation(out=sg[1][:, 0:T], in_=self.bank(b0 + 3, T), func=AF.Sigmoid),
                              reads=['P%d' % (b0 + 3)], writes=['sg1'])
                        tr.op('dve', lambda e, b0=b0: e.tensor_tensor(out=sg[0][:, 0:T], in0=self.bank(b0, T), in1=sg[0][:, 0:T], op=ALU.mult),
                              reads=['P%d' % b0, 'sg0'], writes=['sg0'])
                        tr.op('dve', lambda e, b0=b0: e.tensor_tensor(out=sg[1][:, 0:T], in0=self.bank(b0 + 1, T), in1=sg[1][:, 0:T], op=ALU.mult),
                              reads=['P%d' % (b0 + 1), 'sg1'], writes=['sg1'])
                        tr.op('pool', lambda e, n=n: e.tensor_tensor(out=mT[:, n, 0:T], in0=sg[0][:, 0:T], in1=sg[1][:, 0:T], op=ALU.add),
                              reads=['sg0', 'sg1'], writes=['mT%d' % n])
                if self.debug and not halo and B0 == self.OWN0:
                    dt_ = self.dram_scr("dbg_m", [128, 16, 512], BF16)
                    tr.op('sp', lambda e, dt_=dt_: e.dma_start(out=dt_, in_=mT), reads=['mT%d' % i for i in range(16)], dma=True)
                self.tr.barrier()
                if self.stop_after == 'q7':
                    self.tr.barrier()
                    return
                i = 0
                for m, wv, wkey in proj_groups('wout', list(range(16)), JD, 2):
                    bk = i % 4
                    i += 1
                    for n in range(JD):
                        tr.op('pe', lambda e, n=n, wv=wv, bk=bk: e.matmul(self.bank(bk, T), lhsT=wv[:, n, :], rhs=mT[:, n, 0:T],
                                                                          start=(n == 0), stop=(n == JD - 1)),
                              reads=[wkey, 'mT%d' % n], writes=['P%d' % bk])
                    y_chunk(m, bk, T, nb, 'norm_mix_post')

                if self.stop_after == 'q8a':
                    self.tr.barrier()
                    return

                def x1_done(blk, xin, xk):
                    if not halo:
                        tr.op('sp', lambda e: e.dma_start(out=self.X1S[r0 + blk * 128:r0 + (blk + 1) * 128, :], in_=xin[:]),
                              reads=[xk], writes=['X1S'], dma=True)
                    self.norm_T(bufs, blk, blk * 128, hT, 'hT')

                post_norm(T, nb, 'norm_mix_post', lambda blk: I['xs'][s0 + blk * 128:s0 + (blk + 1) * 128, :], x1_done, False)
                self.tr.barrier()
                if self.stop_after in ('q8b', 'q8c', 'q8d'):
                    return
                if self.stop_after == 'q8':
                    self.tr.barrier()
                    return
                wl = []
                for j2 in range(JF // 2):
                    wl.append(2 * j2)
                    if not halo:
                        wl.append(JF + 2 * j2)
                pend = wload('wup', wl[0], 2, JD)
                for wi, wn0 in enumerate(wl):
                    cur = pend
                    pend = wload('wup', wl[wi + 1], 2, JD) if wi + 1 < len(wl) else None
                    for sub in range(2):
                        self._ffn_chunk(tr, cur, sub, wn0 + sub, T, halo, hT, hk, aext, sg, gT)
                if False:
                    isb = jc = bk = wv = wkey = None
                    pass
                if halo:
                    self.tr.barrier()
                    if self.stop_after == 'halo':
                        return
                    continue
                i = 0
                for m, wv, wkey in proj_groups('wdown', list(range(16)), JF, 1):
                    bk = i % 4
                    i += 1
                    for k in range(JF):
                        tr.op('pe', lambda e, k=k, wv=wv, bk=bk: e.matmul(self.bank(bk, T), lhsT=wv[:, k, :], rhs=gT[:, k, 0:T],
                                                                          start=(k == 0), stop=(k == JF - 1)),
                              reads=[wkey, 'gT%d' % k], writes=['P%d' % bk])
                    y_chunk(m, bk, T, nb, 'norm_ffn_post')

                def out_done(blk, xin, xk):
                    tr.op('sp', lambda e: e.dma_start(out=self.out[r0 + blk * 128:r0 + (blk + 1) * 128, :], in_=xin[:]),
                          reads=[xk], writes=['OUT'], dma=True)

                post_norm(T, nb, 'norm_ffn_post', lambda blk: self.X1S[r0 + blk * 128:r0 + (blk + 1) * 128, :], out_done, True)
                self.tr.barrier()


def _t5_bucket(n):
    n = np.maximum(n, 0)
    nf = np.maximum(n, 1).astype(np.float32)
    large = 16 + (np.log(nf / np.float32(16)) / np.float32(math.log(128 / 16)) * np.float32(16)).astype(np.int32)
    large = np.minimum(large, 31)
    return np.where(n < 16, n, large)


def _onehot():
    k = np.arange(128)[:, None, None]
    r = np.arange(2)[None, :, None]
    q = np.arange(128)[None, None, :]
    dist = q + 128 * (1 - r) - k
    valid = (dist >= 0) & (dist < 128)
    b = np.where(valid, _t5_bucket(dist), 32)
    oh = np.zeros((33, 128, 2, 128), np.float32)
    for i in range(33):
        oh[i] = (b == i)
    return oh.reshape(33, -1)


_CACHE = {}
NCORES_DEBUG = None


def run(inputs, S):
    x = np.asarray(inputs['x'], np.float32)
    B = x.shape[0]
    CH = S // 4
    NBLK = S // 128
    if S not in _CACHE:
        _CACHE[S] = Builder(S).build()
    nc = _CACHE[S]
    inv = (10000.0 ** (-np.arange(0, 64, 2, dtype=np.float32) / np.float32(64))).astype(np.float32)
    oh = _onehot()
    shared = {}
    for nm in ["norm_mix_pre", "norm_mix_post", "norm_ffn_pre", "norm_ffn_post", "w_in", "mla_q_norm", "mla_w_q_up",
               "mla_kv_norm", "mla_w_kv_up", "swa_sinks", "w_o_mla", "w_o_swa", "w_out", "ffn_w_up", "ffn_conv_w",
               "ffn_conv_b", "ffn_w_down"]:
        a = np.asarray(inputs[nm], np.float32)
        shared[nm] = np.ascontiguousarray(a.reshape(a.shape[1:]))
    shared["rel_bias_table"] = np.ascontiguousarray(np.asarray(inputs["rel_bias_table"], np.float32))
    shared["onehot"] = oh
    in_maps = []
    for core in range(8):
        b, c = core // 4, core % 4
        if b >= B:
            b = B - 1
        pad = (3 - c) * CH
        xs = np.zeros((S, D), np.float32)
        xs[pad:] = x[b, :S - pad]
        pos = (np.arange(S) - pad).astype(np.float32)
        ang = (pos[None, :] * inv[:, None]).astype(np.float32)
        cos = np.cos(ang.astype(np.float64)).astype(np.float32)
        sin = np.sin(ang.astype(np.float64)).astype(np.float32)
        cos128 = np.ascontiguousarray(np.tile(cos, (4, 1)))
        sin128 = np.ascontiguousarray(np.tile(sin, (4, 1)))
        km = np.zeros((128, NBLK), np.float32)
        km[:, :pad // 128] = NEG
        m = dict(shared)
        m.update(xs=xs, cos128=cos128, sin128=sin128, kmask=km)
        in_maps.append(m)
    ncores = NCORES_DEBUG or 8
    res = run_bass_kernel_spmd(nc, in_maps[:ncores], core_ids=list(range(ncores)))
    out = np.zeros((B, S, D), np.float32)
    for core in range(ncores):
        b, c = core // 4, core % 4
        if b < B:
            out[b, c * CH:(c + 1) * CH] = res.results[core]["out"]
    return out


def kernel(**inputs):
    return run(inputs, 16384)
```
